# Optimizing a Trainium2 kernel written in Bass

```python
import math, functools
import jax, jax.numpy as jnp
from jax import lax
import numpy as np

D_MODEL = 1024
BATCH = 8
SEQ = 2048
DEPTH = 2
DEC_BATCH = 32
DEC_SEQ = 1
PAST_LEN = 16384
PAGE_SIZE = 128

CONV_W = 4
D_RNN = 1280
RG_BLOCK = 128
RG_BLOCKS = D_RNN // RG_BLOCK
RG_C = 8.0
GDN_HEADS = 8
GDN_DK = 128
GDN_DV = 128
GDN_KW = GDN_HEADS * GDN_DK
GDN_VW = GDN_HEADS * GDN_DV
GDN_CONV_C = 2 * GDN_KW + GDN_VW
GDN_CHUNK = 64
MLA_HEADS = 8
Q_LORA = 384
KV_LORA = 256
QK_NOPE = 64
QK_ROPE = 32
V_HEAD = 128
MLA_VW = MLA_HEADS * V_HEAD
MLA_SCALE = (QK_NOPE + QK_ROPE) ** -0.5
ROPE_BASE = 10000.0
Q_BLOCK = 128
D_FF = ((8 * D_MODEL // 3 + 255) // 256) * 256
N_BRANCH = 3
IN_SIZES = (D_RNN, D_RNN, GDN_CONV_C, GDN_VW, GDN_HEADS, GDN_HEADS, Q_LORA, KV_LORA + QK_ROPE, N_BRANCH * D_MODEL)
N_IN = sum(IN_SIZES)
EPS = 1e-6

kernel_name = 'hybrid_rglru_gdn_mla_adaln_step'


def rmsnorm(x, g):
    xf = x.astype(jnp.float32)
    y = xf * lax.rsqrt(jnp.mean(xf * xf, axis=-1, keepdims=True) + EPS)
    return (y * g.astype(jnp.float32)).astype(x.dtype)


def l2norm(x):
    xf = x.astype(jnp.float32)
    return xf * lax.rsqrt(jnp.sum(xf * xf, axis=-1, keepdims=True) + EPS)


def causal_conv(u, buf, w):
    T = u.shape[1]
    up = jnp.concatenate([buf.astype(u.dtype), u], axis=1)
    y = up[:, 0:T] * w[0]
    for j in range(1, CONV_W):
        y = y + up[:, j:j + T] * w[j]
    return y, up[:, T:]


def linear_recurrence(a, b, h0):
    def combine(left, right):
        al, bl = left
        ar, br = right
        return ar * al, ar * bl + br
    a_cum, h = lax.associative_scan(combine, (a, b), axis=1)
    return h + a_cum * h0[:, None, :]


def rglru_branch(u_x, u_y, buf, h0, conv_w, conv_b, wa, ba, wx, bx, lam):
    xc, new_buf = causal_conv(u_x, buf, conv_w)
    xc = xc + conv_b
    B, T, _ = xc.shape
    xb = xc.reshape(B, T, RG_BLOCKS, RG_BLOCK)
    r = jax.nn.sigmoid(jnp.einsum('btnj,njk->btnk', xb, wa).reshape(B, T, D_RNN) + ba)
    i = jax.nn.sigmoid(jnp.einsum('btnj,njk->btnk', xb, wx).reshape(B, T, D_RNN) + bx)
    log_a = -RG_C * r.astype(jnp.float32) * jax.nn.softplus(-lam.astype(jnp.float32))
    a = jnp.exp(log_a)
    b = jnp.sqrt(-jnp.expm1(2.0 * log_a)) * (i * xc).astype(jnp.float32)
    h = linear_recurrence(a, b, h0.astype(jnp.float32))
    out = h.astype(u_x.dtype) * jax.nn.gelu(u_y)
    return out, new_buf, h[:, -1]


def gated_delta_chunked(q, k, v, g, beta, S0):
    B, T, H, _ = q.shape
    C = GDN_CHUNK
    n = -(-T // C)
    pad = n * C - T
    f32 = jnp.float32

    def prep4(t):
        t = jnp.pad(t.astype(f32), ((0, 0), (0, pad), (0, 0), (0, 0)))
        return t.reshape(B, n, C, H, t.shape[-1]).transpose(1, 0, 3, 2, 4)

    def prep3(t):
        t = jnp.pad(t.astype(f32), ((0, 0), (0, pad), (0, 0)))
        return t.reshape(B, n, C, H).transpose(1, 0, 3, 2)

    tri_strict = jnp.tril(jnp.ones((C, C), bool), -1)
    tri_incl = jnp.tril(jnp.ones((C, C), bool))
    eye = jnp.eye(C, dtype=f32)

    def step(S, inp):
        qc, kc, vc, gc, bc = inp
        gcum = jnp.cumsum(gc, axis=-1)
        diff = gcum[..., :, None] - gcum[..., None, :]
        dec_strict = jnp.exp(jnp.where(tri_strict, diff, -jnp.inf))
        dec_incl = jnp.exp(jnp.where(tri_incl, diff, -jnp.inf))
        L = bc[..., :, None] * jnp.einsum('bhid,bhjd->bhij', kc, kc) * dec_strict
        rhs = bc[..., None] * (vc - jnp.exp(gcum)[..., None] * jnp.einsum('bhid,bhde->bhie', kc, S))
        U = lax.linalg.triangular_solve(eye + L, rhs, left_side=True, lower=True)
        qk = jnp.einsum('bhid,bhjd->bhij', qc, kc) * dec_incl
        o = jnp.exp(gcum)[..., None] * jnp.einsum('bhid,bhde->bhie', qc, S) + jnp.einsum('bhij,bhje->bhie', qk, U)
        g_last = gcum[..., -1]
        k_dec = kc * jnp.exp(g_last[..., None] - gcum)[..., None]
        S_new = jnp.exp(g_last)[..., None, None] * S + jnp.einsum('bhjd,bhje->bhde', k_dec, U)
        return S_new, o

    S, o = lax.scan(step, S0.astype(f32), (prep4(q), prep4(k), prep4(v), prep3(g), prep3(beta)))
    o = o.transpose(1, 0, 3, 2, 4).reshape(B, n * C, H, -1)[:, :T]
    return o, S


def gdn_branch(u_qkv, u_z, u_a, u_b, buf, S0, conv_w, A_log, dt_bias, norm_g):
    qkv, new_buf = causal_conv(u_qkv, buf, conv_w)
    qkv = jax.nn.silu(qkv)
    B, T, _ = qkv.shape
    q, k, v = jnp.split(qkv, [GDN_KW, 2 * GDN_KW], axis=-1)
    q = l2norm(q.reshape(B, T, GDN_HEADS, GDN_DK)) * (GDN_DK ** -0.5)
    k = l2norm(k.reshape(B, T, GDN_HEADS, GDN_DK))
    v = v.reshape(B, T, GDN_HEADS, GDN_DV)
    beta = jax.nn.sigmoid(u_b.astype(jnp.float32))
    g = -jnp.exp(A_log.astype(jnp.float32)) * jax.nn.softplus(u_a.astype(jnp.float32) + dt_bias.astype(jnp.float32))
    o, S = gated_delta_chunked(q, k, v, g, beta, S0)
    o = rmsnorm(o.astype(u_z.dtype), norm_g) * jax.nn.silu(u_z.reshape(B, T, GDN_HEADS, GDN_DV))
    return o.reshape(B, T, GDN_VW), new_buf, S


def rope_cos_sin(T, pos0):
    inv = ROPE_BASE ** (-jnp.arange(0, QK_ROPE, 2, dtype=jnp.float32) / QK_ROPE)
    ang = (jnp.arange(T, dtype=jnp.float32) + pos0)[:, None] * inv[None, :]
    return jnp.cos(ang), jnp.sin(ang)


def apply_rope(x, cos, sin):
    x1, x2 = jnp.split(x.astype(jnp.float32), 2, axis=-1)
    return jnp.concatenate([x1 * cos - x2 * sin, x2 * cos + x1 * sin], axis=-1).astype(x.dtype)


def mla_prompt_attention(q_nope, q_rope, c_kv, k_rope, w_ukv):
    B, S, H, _ = q_nope.shape
    kv = jnp.einsum('bsc,cf->bsf', c_kv, w_ukv).reshape(B, S, H, QK_NOPE + V_HEAD)
    k_nope, v = kv[..., :QK_NOPE], kv[..., QK_NOPE:]
    nb = S // Q_BLOCK
    qn = q_nope.reshape(B, nb, Q_BLOCK, H, QK_NOPE).transpose(1, 0, 2, 3, 4)
    qr = q_rope.reshape(B, nb, Q_BLOCK, H, QK_ROPE).transpose(1, 0, 2, 3, 4)
    key_pos = jnp.arange(S)

    def block(args):
        qn_b, qr_b, start = args
        s = (jnp.einsum('bqhd,bkhd->bhqk', qn_b, k_nope)
             + jnp.einsum('bqhr,bkr->bhqk', qr_b, k_rope)).astype(jnp.float32) * MLA_SCALE
        qpos = start + jnp.arange(Q_BLOCK)
        s = jnp.where(key_pos[None, :] <= qpos[:, None], s, -jnp.inf)
        p = jax.nn.softmax(s, axis=-1).astype(v.dtype)
        return jnp.einsum('bhqk,bkhe->bqhe', p, v)

    o = lax.map(block, (qn, qr, jnp.arange(nb) * Q_BLOCK))
    return o.transpose(1, 0, 2, 3, 4).reshape(B, S, H * V_HEAD)


def mla_sample_attention(q_nope, q_rope, c_kv, k_rope, w_ukv, ckv_pool, krope_pool, page_table, layer):
    Bd, T, H, _ = q_nope.shape
    w = w_ukv.reshape(KV_LORA, H, QK_NOPE + V_HEAD)
    w_uk, w_uv = w[..., :QK_NOPE], w[..., QK_NOPE:]
    q_lat = jnp.einsum('bthd,chd->bthc', q_nope, w_uk)
    ckv_past = ckv_pool[layer, page_table].reshape(Bd, -1, KV_LORA)
    kr_past = krope_pool[layer, page_table].reshape(Bd, -1, QK_ROPE)
    P = ckv_past.shape[1]
    s_past = jnp.einsum('bthc,bpc->bthp', q_lat, ckv_past) + jnp.einsum('bthr,bpr->bthp', q_rope, kr_past)
    s_new = jnp.einsum('bthc,bsc->bths', q_lat, c_kv) + jnp.einsum('bthr,bsr->bths', q_rope, k_rope)
    causal = jnp.tril(jnp.ones((T, T), bool))
    s_new = jnp.where(causal[None, :, None, :], s_new, -jnp.inf)
    s = jnp.concatenate([s_past, s_new], axis=-1).astype(jnp.float32) * MLA_SCALE
    p = jax.nn.softmax(s, axis=-1).astype(c_kv.dtype)
    o_lat = jnp.einsum('bthp,bpc->bthc', p[..., :P], ckv_past) + jnp.einsum('bths,bsc->bthc', p[..., P:], c_kv)
    return jnp.einsum('bthc,che->bthe', o_lat, w_uv).reshape(Bd, T, H * V_HEAD)


def hybrid_layer(x, c, pos0, rg_buf, rg_h0, gdn_buf, gdn_S0, attend,
                 w_ada, b_ada, g_norm1, g_norm2, w_in,
                 rg_conv_w, rg_conv_b, rg_wa, rg_ba, rg_wx, rg_bx, rg_lambda,
                 gdn_conv_w, gdn_A_log, gdn_dt_bias, gdn_norm_g,
                 mla_q_norm_g, w_uq, mla_kv_norm_g, w_ukv,
                 w_rg_proj, w_gdn_proj, w_mla_proj, w_o, w_ffn_in, w_ffn_out):
    B, T, _ = x.shape
    mod = jnp.einsum('bd,df->bf', jax.nn.silu(c), w_ada) + b_ada
    sh1, sc1, gt1, sh2, sc2, gt2 = jnp.split(mod[:, None, :], 6, axis=-1)

    h = rmsnorm(x, g_norm1) * (1.0 + sc1) + sh1
    u = jnp.einsum('btd,df->btf', h, w_in)
    offs = np.cumsum(IN_SIZES)[:-1].tolist()
    u_rx, u_ry, u_qkv, u_z, u_a, u_b, u_mq, u_mkv, u_gate = jnp.split(u, offs, axis=-1)

    o_rg, rg_buf_new, rg_h = rglru_branch(u_rx, u_ry, rg_buf, rg_h0, rg_conv_w, rg_conv_b,
                                          rg_wa, rg_ba, rg_wx, rg_bx, rg_lambda)
    o_gdn, gdn_buf_new, gdn_S = gdn_branch(u_qkv, u_z, u_a, u_b, gdn_buf, gdn_S0,
                                           gdn_conv_w, gdn_A_log, gdn_dt_bias, gdn_norm_g)
    cos, sin = rope_cos_sin(T, pos0)
    cq = rmsnorm(u_mq, mla_q_norm_g)
    q = jnp.einsum('btc,cf->btf', cq, w_uq).reshape(B, T, MLA_HEADS, QK_NOPE + QK_ROPE)
    q_nope = q[..., :QK_NOPE]
    q_rope = apply_rope(q[..., QK_NOPE:], cos[:, None, :], sin[:, None, :])
    c_kv = rmsnorm(u_mkv[..., :KV_LORA], mla_kv_norm_g)
    k_rope = apply_rope(u_mkv[..., KV_LORA:], cos, sin)
    o_mla = attend(q_nope, q_rope, c_kv, k_rope, w_ukv)

    ga, gb, gc = jnp.split(u_gate, N_BRANCH, axis=-1)
    m = (jax.nn.sigmoid(ga) * jnp.einsum('btf,fd->btd', o_rg, w_rg_proj)
         + jax.nn.sigmoid(gb) * jnp.einsum('btf,fd->btd', o_gdn, w_gdn_proj)
         + jax.nn.sigmoid(gc) * jnp.einsum('btf,fd->btd', o_mla, w_mla_proj))
    x = x + gt1 * jnp.einsum('btd,de->bte', m, w_o)

    h2 = rmsnorm(x, g_norm2) * (1.0 + sc2) + sh2
    gate, up = jnp.split(jnp.einsum('btd,df->btf', h2, w_ffn_in), 2, axis=-1)
    x = x + gt2 * jnp.einsum('btf,fd->btd', jax.nn.silu(gate) * up, w_ffn_out)
    return x, (c_kv, k_rope, rg_buf_new, rg_h, gdn_buf_new, gdn_S)


def setup_inputs(seed: int = 0) -> dict:
    key = jax.random.key(seed)
    keys = iter(jax.random.split(key, 64))
    f32 = jnp.float32
    L = DEPTH

    def nrm(shape, scale):
        return jax.random.normal(next(keys), shape, f32) * scale

    def gain(shape):
        return 1.0 + nrm(shape, 0.02)

    n_pages = PAST_LEN // PAGE_SIZE
    n_pool = (DEC_BATCH * n_pages * 5) // 4
    perm = jax.random.permutation(next(keys), n_pool)
    page_table = perm[:DEC_BATCH * n_pages].reshape(DEC_BATCH, n_pages).astype(jnp.int32)

    u_lam = jax.random.uniform(next(keys), (L, D_RNN), f32, 0.9, 0.999)
    s_lam = u_lam ** (1.0 / RG_C)
    rg_lambda = jnp.log(s_lam) - jnp.log1p(-s_lam)
    A = jax.random.uniform(next(keys), (L, GDN_HEADS), f32, 1.0, 16.0)
    dt = jnp.exp(jax.random.uniform(next(keys), (L, GDN_HEADS), f32, math.log(1e-3), math.log(1e-1)))
    dt_bias = dt + jnp.log(-jnp.expm1(-dt))

    return {
        'x_prompt': nrm((BATCH, SEQ, D_MODEL), 1.0),
        'x_sample': nrm((DEC_BATCH, DEC_SEQ, D_MODEL), 1.0),
        'cache_ckv': nrm((L, n_pool, PAGE_SIZE, KV_LORA), 1.0),
        'cache_krope': nrm((L, n_pool, PAGE_SIZE, QK_ROPE), 1.0),
        'state_rg_conv': nrm((L, DEC_BATCH, CONV_W - 1, D_RNN), 1.0),
        'state_rg_h': nrm((L, DEC_BATCH, D_RNN), 1.0),
        'state_gdn_conv': nrm((L, DEC_BATCH, CONV_W - 1, GDN_CONV_C), 1.0),
        'state_gdn_S': nrm((L, DEC_BATCH, GDN_HEADS, GDN_DK, GDN_DV), 0.5),
        'page_table': page_table,
        'c_prompt': nrm((BATCH, D_MODEL), 1.0),
        'c_sample': nrm((DEC_BATCH, D_MODEL), 1.0),
        'w_ada': nrm((L, D_MODEL, 6 * D_MODEL), 0.5 * D_MODEL ** -0.5),
        'b_ada': nrm((L, 6 * D_MODEL), 0.02),
        'g_norm1': gain((L, D_MODEL)),
        'g_norm2': gain((L, D_MODEL)),
        'w_in': nrm((L, D_MODEL, N_IN), D_MODEL ** -0.5),
        'rg_conv_w': nrm((L, CONV_W, D_RNN), CONV_W ** -0.5),
        'rg_conv_b': nrm((L, D_RNN), 0.02),
        'rg_wa': nrm((L, RG_BLOCKS, RG_BLOCK, RG_BLOCK), RG_BLOCK ** -0.5),
        'rg_ba': nrm((L, D_RNN), 0.02),
        'rg_wx': nrm((L, RG_BLOCKS, RG_BLOCK, RG_BLOCK), RG_BLOCK ** -0.5),
        'rg_bx': nrm((L, D_RNN), 0.02),
        'rg_lambda': rg_lambda,
        'gdn_conv_w': nrm((L, CONV_W, GDN_CONV_C), CONV_W ** -0.5),
        'gdn_A_log': jnp.log(A),
        'gdn_dt_bias': dt_bias,
        'gdn_norm_g': gain((L, GDN_DV)),
        'mla_q_norm_g': gain((L, Q_LORA)),
        'w_uq': nrm((L, Q_LORA, MLA_HEADS * (QK_NOPE + QK_ROPE)), Q_LORA ** -0.5),
        'mla_kv_norm_g': gain((L, KV_LORA)),
        'w_ukv': nrm((L, KV_LORA, MLA_HEADS * (QK_NOPE + V_HEAD)), KV_LORA ** -0.5),
        'w_rg_proj': nrm((L, D_RNN, D_MODEL), D_RNN ** -0.5),
        'w_gdn_proj': nrm((L, GDN_VW, D_MODEL), GDN_VW ** -0.5),
        'w_mla_proj': nrm((L, MLA_VW, D_MODEL), MLA_VW ** -0.5),
        'w_o': nrm((L, D_MODEL, D_MODEL), D_MODEL ** -0.5),
        'w_ffn_in': nrm((L, D_MODEL, 2 * D_FF), D_MODEL ** -0.5),
        'w_ffn_out': nrm((L, D_FF, D_MODEL), D_FF ** -0.5),
        'g_final': gain((D_MODEL,)),
    }


def reference(x_prompt, x_sample, cache_ckv, cache_krope, state_rg_conv, state_rg_h, state_gdn_conv, state_gdn_S,
              page_table, c_prompt, c_sample,
              w_ada, b_ada, g_norm1, g_norm2, w_in,
              rg_conv_w, rg_conv_b, rg_wa, rg_ba, rg_wx, rg_bx, rg_lambda,
              gdn_conv_w, gdn_A_log, gdn_dt_bias, gdn_norm_g,
              mla_q_norm_g, w_uq, mla_kv_norm_g, w_ukv,
              w_rg_proj, w_gdn_proj, w_mla_proj, w_o, w_ffn_in, w_ffn_out, g_final):
    xp, xs = x_prompt, x_sample
    Bp = xp.shape[0]
    zb_rg = jnp.zeros((Bp, CONV_W - 1, D_RNN), xp.dtype)
    zh_rg = jnp.zeros((Bp, D_RNN), xp.dtype)
    zb_gdn = jnp.zeros((Bp, CONV_W - 1, GDN_CONV_C), xp.dtype)
    zS_gdn = jnp.zeros((Bp, GDN_HEADS, GDN_DK, GDN_DV), jnp.float32)
    outs_p = []
    outs_s = []
    for l in range(DEPTH):
        lp = (w_ada[l], b_ada[l], g_norm1[l], g_norm2[l], w_in[l],
              rg_conv_w[l], rg_conv_b[l], rg_wa[l], rg_ba[l], rg_wx[l], rg_bx[l], rg_lambda[l],
              gdn_conv_w[l], gdn_A_log[l], gdn_dt_bias[l], gdn_norm_g[l],
              mla_q_norm_g[l], w_uq[l], mla_kv_norm_g[l], w_ukv[l],
              w_rg_proj[l], w_gdn_proj[l], w_mla_proj[l], w_o[l], w_ffn_in[l], w_ffn_out[l])
        xp, st_p = hybrid_layer(xp, c_prompt, 0.0, zb_rg, zh_rg, zb_gdn, zS_gdn, mla_prompt_attention, *lp)
        sample_attend = functools.partial(mla_sample_attention, ckv_pool=cache_ckv, krope_pool=cache_krope,
                                          page_table=page_table, layer=l)
        xs, st_s = hybrid_layer(xs, c_sample, float(PAST_LEN), state_rg_conv[l], state_rg_h[l],
                                state_gdn_conv[l], state_gdn_S[l], sample_attend, *lp)
        outs_p.append(st_p)
        outs_s.append(st_s)
    y_prompt = rmsnorm(xp, g_final)
    y_sample = rmsnorm(xs, g_final)
    ckv_p = jnp.stack([o[0] for o in outs_p])
    krope_p = jnp.stack([o[1] for o in outs_p])
    rg_conv_p = jnp.stack([o[2] for o in outs_p])
    rg_h_p = jnp.stack([o[3] for o in outs_p])
    gdn_conv_p = jnp.stack([o[4] for o in outs_p])
    gdn_S_p = jnp.stack([o[5] for o in outs_p])
    ckv_s = jnp.stack([o[0] for o in outs_s])
    krope_s = jnp.stack([o[1] for o in outs_s])
    rg_conv_s = jnp.stack([o[2] for o in outs_s])
    rg_h_s = jnp.stack([o[3] for o in outs_s])
    gdn_conv_s = jnp.stack([o[4] for o in outs_s])
    gdn_S_s = jnp.stack([o[5] for o in outs_s])
    return (y_prompt, y_sample, ckv_p, krope_p, rg_conv_p, rg_h_p, gdn_conv_p, gdn_S_p,
            ckv_s, krope_s, rg_conv_s, rg_h_s, gdn_conv_s, gdn_S_s)
```

```python
import os
import numpy as np
from contextlib import ExitStack
import concourse.bass as bass
import concourse.mybir as mybir
from concourse.bass_utils import run_bass_kernel_spmd

F32 = mybir.dt.float32
BF16 = mybir.dt.bfloat16
I32 = mybir.dt.int32
AF = mybir.ActivationFunctionType
ALU = mybir.AluOpType
AX = mybir.AxisListType

NCORES = 8
T = 2048
D = 1024
NS = 4
L = 2
D_RNN = 1280
NPAGE = 128
PAGE = 128
KVL = 256
ROPE = 32
NOPE = 64
VH = 128
H = 8
D_FF = 2816
EPS = 1e-6
PAST = 16384
MLA_SCALE = (NOPE + ROPE) ** -0.5
OFF_RX, OFF_RY, OFF_QKV, OFF_Z, OFF_A, OFF_B, OFF_MQ, OFF_MKV, OFF_GATE = 0, 1280, 2560, 5632, 6656, 6664, 6672, 7056, 7344
CI_RX, CI_RY, CI_Q, CI_K, CI_V, CI_Z, CI_AB, CI_MQ, CI_CKV, CI_KR, CI_GATE, NCI = 0, 10, 20, 28, 36, 44, 52, 53, 56, 58, 59, 83
V_BADA, V_G1, V_G2, V_RGCW, V_RGCB, V_RGBA, V_RGBX, V_RGLAM, V_GDCW, V_QG, V_KVG, V_GDG, V_GF = 0, 48, 56, 64, 104, 114, 124, 134, 144, 240, 243, 245, 246


def chunk_cols():
    cols = []
    for n in range(10):
        cols.append((OFF_RX + n * 128, 128))
    for n in range(10):
        cols.append((OFF_RY + n * 128, 128))
    for part in range(3):
        for h in range(8):
            cols.append((OFF_QKV + part * 1024 + h * 128, 128))
    for h in range(8):
        cols.append((OFF_Z + h * 128, 128))
    cols.append((OFF_A, 16))
    for k in range(3):
        cols.append((OFF_MQ + k * 128, 128))
    for k in range(2):
        cols.append((OFF_MKV + k * 128, 128))
    cols.append((OFF_MKV + 192, 96))
    for j in range(24):
        cols.append((OFF_GATE + j * 128, 128))
    assert len(cols) == NCI
    return cols


class Buf:
    __slots__ = ("name", "w", "r")

    def __init__(self, name=""):
        self.name = name
        self.w = []
        self.r = []


class Tok:
    __slots__ = ("key", "val", "clock")

    def __init__(self, key, val, clock):
        self.key = key
        self.val = val
        self.clock = clock


class KB:
    NDMA = 8

    def __init__(self, nc, es, needed=None):
        self.nc = nc
        self.es = es
        self.needed = needed
        self.waited = set()
        self.hw = {}
        self.hwmap = {}
        self.E = {"pe": nc.tensor, "act": nc.scalar, "dve": nc.vector, "pool": nc.gpsimd, "sp": nc.sync}
        self.sems = {}
        self.cnt = {}
        self.clock = {e: {} for e in self.E}
        self.pend_r = {e: [] for e in self.E}
        self.pend_w = {e: [] for e in self.E}
        for e in self.E:
            self.sems[e] = es.enter_context(nc.semaphore("s_" + e))
            self.cnt[e] = 0
        self.dq = {}
        for q in ("sp", "pool"):
            lst = []
            for i in range(self.NDMA):
                k = "d_%s_%d" % (q, i)
                self.sems[k] = es.enter_context(nc.semaphore(k))
                self.cnt[k] = 0
                lst.append(k)
            self.dq[q] = [lst, 0]
        self.ninst = 0
        self.out_toks = []
        self.banks = []
        self.bank_i = 0
        for i in range(8):
            t = es.enter_context(nc.psum_tensor("pb%d" % i, [128, 512], F32))
            self.banks.append((t, Buf("pb%d" % i)))

    def sb(self, name, shape, dt):
        self.uid = getattr(self, "uid", 0) + 1
        return self.es.enter_context(self.nc.sbuf_tensor("t%d_%s" % (self.uid, name), list(shape), dt))

    def bank(self):
        t, b = self.banks[self.bank_i % 6]
        self.bank_i += 1
        return t, b

    def acc_bank(self, i):
        return self.banks[6 + i]

    def _wait(self, e, tok):
        ck = self.clock[e]
        if ck.get(tok.key, 0) >= tok.val:
            return
        self.waited.add((tok.key, tok.val))
        hwval = tok.val
        if tok.key in self.E and self.needed is not None:
            hwval = self.hwmap[tok.key][tok.val]
        self.E[e].wait_ge(self.sems[tok.key], hwval)
        for k, v in tok.clock.items():
            if ck.get(k, 0) < v:
                ck[k] = v
        if ck.get(tok.key, 0) < tok.val:
            ck[tok.key] = tok.val

    def _deps(self, e, r, w):
        toks = []
        for b in r:
            toks.extend(b.w)
        for b in w:
            toks.extend(b.w)
            toks.extend(b.r)
        for t in toks:
            if e == "pe" and t.key == "pe":
                continue
            self._wait(e, t)

    def _commit(self, e, tok, r, w):
        r = list(r) + self.pend_r[e]
        w = list(w) + self.pend_w[e]
        self.pend_r[e] = []
        self.pend_w[e] = []
        for b in w:
            b.w = [tok]
            b.r = []
        for b in r:
            if b in w:
                continue
            b.r = [t for t in b.r if t.key != tok.key] + [tok]

    def op(self, e, fn, r=(), w=(), inc=True):
        self._deps(e, r, w)
        ins = fn(self.E[e])
        self.ninst += 1
        if inc:
            self.cnt[e] += 1
            if self.needed is None or (e, self.cnt[e]) in self.needed:
                self.hw[e] = self.hw.get(e, 0) + 1
                self.hwmap.setdefault(e, {})[self.cnt[e]] = self.hw[e]
                ins.then_inc(self.sems[e], 1)
            ck = dict(self.clock[e])
            ck[e] = self.cnt[e]
            tok = Tok(e, self.cnt[e], ck)
            self._commit(e, tok, r, w)
        else:
            self.pend_r[e].extend(r)
            self.pend_w[e].extend(w)
        return ins

    def dma(self, q, out, in_, r=(), w=(), is_out=False, **kw):
        lst, i = self.dq[q]
        key = lst[i % len(lst)]
        self.dq[q][1] = i + 1
        if self.cnt[key] > 0:
            self._wait(q, Tok(key, self.cnt[key], {}))
        self._deps(q, r, w)
        ins = self.E[q].dma_start(out=out, in_=in_, **kw)
        self.ninst += 1
        self.cnt[key] += 16
        ins.then_inc(self.sems[key], 16)
        ck = dict(self.clock[q])
        ck[key] = self.cnt[key]
        tok = Tok(key, self.cnt[key], ck)
        for b in w:
            b.w = [tok]
            b.r = []
        for b in r:
            b.r = [t for t in b.r if t.key != key] + [tok]
        if is_out:
            self.out_toks.append(tok)
        return tok

    def dma_fn(self, q, fn, r=(), w=(), extra_w=()):
        lst, i = self.dq[q]
        key = lst[i % len(lst)]
        self.dq[q][1] = i + 1
        if self.cnt[key] > 0:
            self._wait(q, Tok(key, self.cnt[key], {}))
        self._deps(q, r, w)
        ins = fn(self.E[q])
        self.ninst += 1
        self.cnt[key] += 16
        ins.then_inc(self.sems[key], 16)
        ck = dict(self.clock[q])
        ck[key] = self.cnt[key]
        tok = Tok(key, self.cnt[key], ck)
        for b in w:
            b.w = [tok]
            b.r = []
        for b in extra_w:
            b.w = b.w + [tok]
        for b in r:
            b.r = [t for t in b.r if t.key != key] + [tok]
        return tok

    def barrier(self):
        toks = [Tok(k, c, {}) for k, c in self.cnt.items() if c > 0]
        for e in self.E:
            for t in toks:
                if t.key != e:
                    self._wait(e, t)

    def finish(self):
        toks = [Tok(k, c, {}) for k, c in self.cnt.items() if c > 0]
        for t in toks:
            if t.key != "sp":
                self._wait("sp", t)


class Rot:
    def __init__(self, kb, name, shape, dt, n):
        self.t = [kb.sb("%s%d" % (name, i), shape, dt) for i in range(n)]
        self.b = [Buf("%s%d" % (name, i)) for i in range(n)]
        self.i = 0

    def get(self):
        i = self.i % len(self.t)
        self.i += 1
        return self.t[i], self.b[i]


IN_SPECS = [
    ("xp", [T, D], F32), ("xs", [NS, D], F32), ("cache_ckv", [L, 5120, PAGE, KVL], F32), ("cache_krope", [L, 5120, PAGE, ROPE], F32),
    ("st_rg_conv", [L, NS, 3, D_RNN], F32), ("st_rg_h", [L, NS, D_RNN], F32), ("st_gdn_conv", [L, NS, 3, 3072], F32),
    ("st_gdn_S", [L, NS, H, 128, 128], F32), ("ptab", [NS, NPAGE], I32), ("cc", [5, D], F32),
    ("win", [L, NCI, 128, 8, 128], F32), ("wada", [L, 48, 128, 8, 128], F32), ("vecs", [L, 256, 128], F32),
    ("rgw", [L, 20, 128, 128], F32), ("wrgp", [L, 8, 128, 10, 128], F32), ("wgdp", [L, 8, 128, 8, 128], F32),
    ("wmlp", [L, 8, 128, 8, 128], F32), ("wo", [L, D, D], F32), ("wffi", [L, 44, 128, 8, 128], F32), ("wffo", [L, D_FF, D], F32),
    ("wuq", [L, 384, 768], F32), ("wukv", [L, KVL, 1536], F32), ("wukT", [L, H, NOPE, KVL], F32),
    ("gdn_ab", [L, 2, H], F32), ("ident", [128, 128], F32), ("masks", [4, 128, 512], F32), ("ropeC", [96, T], F32), ("ropeS", [96, T], F32),
    ("ropeCs", [96, 1], F32), ("ropeSs", [96, 1], F32), ("rfullT", [96, 96], F32), ("gmask", [4, 128, 128], F32), ("lmask", [14, 128, 128], F32), ("iota", [128, 1], F32),
]
OUT_SPECS = [
    ("y_p", [T, D]), ("y_s", [NS, D]), ("ckv_p", [L, T, KVL]), ("krope_p", [L, T, ROPE]), ("rg_conv_p", [L, 3, D_RNN]), ("rg_h_p", [L, D_RNN]),
    ("gdn_conv_p", [L, 3, 3072]), ("gdn_S_p", [L, H, 128, 128]), ("ckv_s", [L, NS, KVL]), ("krope_s", [L, NS, ROPE]),
    ("rg_conv_s", [L, NS, 3, D_RNN]), ("rg_h_s", [L, NS, D_RNN]), ("gdn_conv_s", [L, NS, 3, 3072]), ("gdn_S_s", [L, NS, H, 128, 128]),
]


def build_program(stage=99, needed=None):
    nc = bass.Bass("TRN2", target_bir_lowering=False)
    I = {}
    for name, shape, dt in IN_SPECS:
        if stage < 50 and name.startswith("cache_"):
            shape = [L, 1] + shape[2:]
        I[name] = nc.dram_tensor(name, shape, dt, kind="ExternalInput").ap()
    O = {}
    for name, shape in OUT_SPECS:
        O[name] = nc.dram_tensor(name, shape, F32, kind="ExternalOutput").ap()
    xa = nc.dram_tensor("xa_scr", [T, D], F32, kind="ExternalOutput").ap()
    xb = nc.dram_tensor("xb_scr", [T, D], F32, kind="ExternalOutput").ap()

    with ExitStack() as es:
        kb = KB(nc, es, needed)
        es.enter_context(nc.allow_non_contiguous_dma(reason="small strided state/vector transfers"))
        es.enter_context(nc.allow_low_precision(reason="bf16 matmul operands, fp32 accumulate"))
        op = kb.op
        dma = kb.dma

        ident_f = kb.sb("ident_f", [128, 128], F32)
        ident_b = kb.sb("ident_b", [128, 128], BF16)
        ones_b = kb.sb("ones_b", [128, 128], BF16)
        ones_f = kb.sb("ones_f", [128, 128], F32)
        Bc = Buf("consts")
        dma("sp", ident_f[:], I["ident"], w=[Bc])
        op("dve", lambda e: e.tensor_copy(out=ident_b[:], in_=ident_f[:]), r=[Bc], w=[Bc])
        op("pool", lambda e: e.memset(ones_b[:], 1.0), w=[Bc])
        op("pool", lambda e: e.memset(ones_f[:], 1.0), w=[Bc])

        vecT = [kb.sb("vecT%d" % l, [128, 256], F32) for l in range(L)]
        modT = [kb.sb("modT%d" % l, [128, 48, 5], F32) for l in range(L)]
        A1 = [kb.sb("A1_%d" % l, [128, 8], F32) for l in range(L)]
        A2 = [kb.sb("A2_%d" % l, [128, 8], F32) for l in range(L)]
        As1 = [kb.sb("As1_%d" % l, [128, 8, NS], F32) for l in range(L)]
        As2 = [kb.sb("As2_%d" % l, [128, 8, NS], F32) for l in range(L)]
        Bmod = [Buf("mod%d" % l) for l in range(L)]
        xsT = kb.sb("xsT", [128, 8, NS], F32)
        Bxs = Buf("xsT")
        hsT = kb.sb("hsT", [128, 8, NS], BF16)
        Bhs = Buf("hsT")

        with ExitStack() as es0:
            kb.es = es0
            craw = kb.sb("craw", [5, D], F32)
            csil = kb.sb("csil", [5, D], F32)
            siluT = kb.sb("siluT", [128, 8, 5], F32)
            vraw = kb.sb("vraw", [128, 2, 128], F32)
            xs_raw = kb.sb("xs_raw", [NS, D], F32)
            wa_rot = Rot(kb, "wadap", [128, 4, 8, 128], F32, 3)
            B0 = Buf("p0")
            dma("sp", craw[:], I["cc"], w=[B0])
            dma("sp", xs_raw[:], I["xs"], w=[B0])
            op("act", lambda e: e.activation(out=csil[:], in_=craw[:], func=AF.Silu), r=[B0], w=[B0])
            pt, pbuf = kb.bank()
            for k in range(8):
                op("pe", lambda e: e.transpose(pt[:, k * 5:(k + 1) * 5], csil[:, k * 128:(k + 1) * 128], ident_f[0:5, 0:5]), r=[B0, Bc], w=[pbuf], inc=(k == 7))
            op("dve", lambda e: e.tensor_copy(out=siluT[:].rearrange("p k s -> p (k s)"), in_=pt[:, 0:40]), r=[pbuf], w=[B0])
            pt, pbuf = kb.bank()
            for k in range(8):
                op("pe", lambda e: e.transpose(pt[:, k * NS:(k + 1) * NS], xs_raw[:, k * 128:(k + 1) * 128], ident_f[0:NS, 0:NS]), r=[B0, Bc], w=[pbuf], inc=(k == 7))
            op("dve", lambda e: e.tensor_copy(out=xsT[:].rearrange("p k s -> p (k s)"), in_=pt[:, 0:8 * NS]), r=[pbuf], w=[Bxs])
            for l in range(L):
                dma("sp", vraw[:], I["vecs"][l].rearrange("(a r) c -> r a c", a=2), w=[B0])
                pt, pbuf = kb.bank()
                for a in range(2):
                    op("pe", lambda e: e.transpose(pt[:, a * 128:(a + 1) * 128], vraw[:, a, :], ident_f[:]), r=[B0, Bc], w=[pbuf], inc=(a == 1))
                op("dve", lambda e: e.tensor_copy(out=vecT[l][:], in_=pt[:, 0:256]), r=[pbuf], w=[Bmod[l]])
                pm, pmb = kb.bank()
                for g in range(12):
                    wt, wb_ = wa_rot.get()
                    dma("sp", wt[:], I["wada"][l, g * 4:(g + 1) * 4].rearrange("j p k m -> p j k m"), w=[wb_])
                    for jj in range(4):
                        j = g * 4 + jj
                        for k in range(8):
                            op("pe", lambda e: e.matmul(pm[:, j * 5:(j + 1) * 5], lhsT=wt[:, jj, k, :], rhs=siluT[:, k, :], start=(k == 0), stop=(k == 7)),
                               r=[wb_, B0], w=[pmb], inc=(k == 7))
                op("dve", lambda e: e.tensor_tensor(out=modT[l][:], in0=pm[:, 0:240].rearrange("p (j s) -> p j s", s=5),
                                                    in1=vecT[l][:, V_BADA:V_BADA + 48].unsqueeze(2).to_broadcast([128, 48, 5]), op=ALU.add), r=[pmb, Bmod[l]], w=[Bmod[l]])
                for (Aout, As_out, goff, scoff) in ((A1[l], As1[l], V_G1, 8), (A2[l], As2[l], V_G2, 32)):
                    op("dve", lambda e: e.scalar_tensor_tensor(out=Aout[:], in0=modT[l][:, scoff:scoff + 8, 0], scalar=1.0, in1=vecT[l][:, goff:goff + 8], op0=ALU.add, op1=ALU.mult),
                       r=[Bmod[l]], w=[Bmod[l]])
                    op("dve", lambda e: e.scalar_tensor_tensor(out=As_out[:], in0=modT[l][:, scoff:scoff + 8, 1:5], scalar=1.0,
                                                               in1=vecT[l][:, goff:goff + 8].unsqueeze(2).to_broadcast([128, 8, NS]), op0=ALU.add, op1=ALU.mult),
                       r=[Bmod[l]], w=[Bmod[l]])
            kb.barrier()
        kb.es = es

        ss_all = kb.sb("ss_all", [128, 4], F32)
        Bss = Buf("ss")

        def norm_block(xt, bxt, A, Bv_off, l_mod, hT, hbuf, tb, junkp, xnp):
            jt, jb = junkp.get()
            op("act", lambda e: e.activation(out=jt[:], in_=xt[:], func=AF.Square, accum_out=ss_all[:, 0:1]), r=[bxt], w=[jb, Bss])
            op("dve", lambda e: e.tensor_scalar(out=ss_all[:, 1:2], in0=ss_all[:, 0:1], scalar1=1.0 / D, scalar2=EPS, op0=ALU.mult, op1=ALU.add), r=[Bss], w=[Bss])
            op("act", lambda e: e.activation(out=ss_all[:, 2:3], in_=ss_all[:, 1:2], func=AF.Sqrt), r=[Bss], w=[Bss])
            op("dve", lambda e: e.reciprocal(out=ss_all[:, 3:4], in_=ss_all[:, 2:3]), r=[Bss], w=[Bss])
            xn, xnb = xnp.get()
            op("act", lambda e: e.activation(out=xn[:], in_=xt[:], func=AF.Copy, scale=ss_all[:, 3:4]), r=[bxt, Bss], w=[xnb])
            pt, pbuf = kb.bank()
            ptb = pt[:].bitcast(BF16)
            for c in range(8):
                op("pe", lambda e: e.transpose(ptb[:, c * 128:(c + 1) * 128], xn[:, c * 128:(c + 1) * 128], ident_b[:]), r=[xnb, Bc], w=[pbuf], inc=(c == 7))
            op("dve", lambda e: e.tensor_tensor(out=jt[:].rearrange("p (c t) -> p c t", c=8), in0=ptb.rearrange("p (c t) -> p c t", c=8),
                                                in1=A[:].unsqueeze(2).to_broadcast([128, 8, 128]), op=ALU.mult), r=[pbuf, Bmod[l_mod]], w=[jb])
            op("pool", lambda e: e.tensor_tensor(out=hT[:, :, tb * 128:(tb + 1) * 128], in0=jt[:].rearrange("p (c t) -> p c t", c=8),
                                                 in1=modT[l_mod][:, Bv_off:Bv_off + 8, 0:1].to_broadcast([128, 8, 128]), op=ALU.add), r=[jb, Bmod[l_mod]], w=[hbuf[tb]])

        def sample_norm(l_mod, As, Bv_off, tmp_pool):
            sq, sqb = tmp_pool.get()
            op("dve", lambda e: e.tensor_tensor(out=sq[:], in0=xsT[:], in1=xsT[:], op=ALU.mult), r=[Bxs], w=[sqb])
            pt, pbuf = kb.bank()
            for k in range(8):
                op("pe", lambda e: e.matmul(pt[:, 0:NS], lhsT=ones_f[:], rhs=sq[:, k, :], start=(k == 0), stop=(k == 7)), r=[sqb, Bc], w=[pbuf], inc=(k == 7))
            rs, rsb = tmp_pool.get()
            op("dve", lambda e: e.tensor_scalar(out=rs[:, 0, :], in0=pt[:, 0:NS], scalar1=1.0 / D, scalar2=EPS, op0=ALU.mult, op1=ALU.add), r=[pbuf], w=[rsb])
            op("act", lambda e: e.activation(out=rs[:, 1, :], in_=rs[:, 0, :], func=AF.Sqrt), r=[rsb], w=[rsb])
            op("dve", lambda e: e.reciprocal(out=rs[:, 2, :], in_=rs[:, 1, :]), r=[rsb], w=[rsb])
            op("dve", lambda e: e.tensor_tensor(out=sq[:], in0=xsT[:], in1=rs[:, 2:3, :].to_broadcast([128, 8, NS]), op=ALU.mult), r=[Bxs, rsb, sqb], w=[sqb])
            op("dve", lambda e: e.tensor_tensor(out=sq[:], in0=sq[:], in1=As[:], op=ALU.mult), r=[sqb, Bmod[l_mod]], w=[sqb])
            op("dve", lambda e: e.tensor_tensor(out=hsT[:], in0=sq[:], in1=modT[l_mod][:, Bv_off:Bv_off + 8, 1:5], op=ALU.add), r=[sqb, Bmod[l_mod]], w=[Bhs])

        def make_gtbc(l, goff, dst, dstb, tmpbc, tmpb):
            pg0, pgb0 = kb.bank()
            pg1, pgb1 = kb.bank()
            for c in range(8):
                op("dve", lambda e: e.tensor_scalar(out=tmpbc[:], in0=ones_f[:], scalar1=modT[l][:, goff + c, 0:1], scalar2=None, op0=ALU.mult), r=[Bmod[l], Bc, tmpb], w=[tmpb])
                pg, pgb = (pg0, pgb0) if c < 4 else (pg1, pgb1)
                op("pe", lambda e: e.matmul(pg[:, (c % 4) * 128:(c % 4 + 1) * 128], lhsT=tmpbc[:], rhs=ident_f[:], start=True, stop=True), r=[tmpb, Bc], w=[pgb])
            op("act", lambda e: e.activation(out=dst[:, 0:512], in_=pg0[:], func=AF.Copy), r=[pgb0], w=[dstb])
            op("act", lambda e: e.activation(out=dst[:, 512:1024], in_=pg1[:], func=AF.Copy), r=[pgb1], w=[dstb])

        wpiece = Rot(kb, "wpiece", [128, 8, 128], BF16, 4)

        def inproj(l, ci, hT, hbuf, width, evac, evac_s):
            wt, wb_ = wpiece.get()
            dma("pool", wt[:], I["win"][l, ci], w=[wb_])
            for tg in range(4):
                pt, pbuf = kb.bank()
                for k in range(8):
                    op("pe", lambda e: e.matmul(pt[0:width, :], lhsT=wt[:, k, 0:width], rhs=hT[:, k, tg * 512:(tg + 1) * 512], start=(k == 0), stop=(k == 7)),
                       r=[wb_] + hbuf[tg * 4:tg * 4 + 4], w=[pbuf], inc=(k == 7))
                evac(tg, pt, pbuf)
            if evac_s is not None:
                pt, pbuf = kb.bank()
                for k in range(8):
                    op("pe", lambda e: e.matmul(pt[0:width, 0:NS], lhsT=wt[:, k, 0:width], rhs=hsT[:, k, :], start=(k == 0), stop=(k == 7)),
                       r=[wb_, Bhs], w=[pbuf], inc=(k == 7))
                evac_s(pt, pbuf)

        hT = kb.sb("hT", [128, 8, T], BF16)
        hbuf = [Buf("hT%d" % i) for i in range(16)]
        oT = kb.sb("oT", [128, 10, T], BF16)
        oTb = [Buf("oT%d" % n) for n in range(10)]
        osT = kb.sb("osT", [128, 10, NS], BF16)
        Bos = Buf("osT")
        mT = kb.sb("mT", [128, 8, T], BF16)
        mTb = [Buf("mT%d" % n) for n in range(8)]
        msT = kb.sb("msT", [128, 8, NS], F32)
        Bms = Buf("msT")
        gm = kb.sb("gmask", [128, 4, 128], F32)
        dma("sp", gm[:], I["gmask"].rearrange("a p c -> p a c"), w=[Bc])
        lm = kb.sb("lmask", [128, 14, 128], BF16)
        dma("pool", lm[:], I["lmask"].rearrange("a p c -> p a c"), w=[Bc])
        projw = Rot(kb, "projw", [128, 10, 128], BF16, 2)

        def branch_proj(l, wname, nk, gate0, first):
            for j in range(8):
                wt, wtb = projw.get()
                dma("pool", wt[:, 0:nk, :], I[wname][l, j], w=[wtb])
                gw, gwb = wpiece.get()
                dma("pool", gw[:], I["win"][l, gate0 + j], w=[gwb])
                for tg in range(4):
                    sl = slice(tg * 512, (tg + 1) * 512)
                    pp, ppb = kb.bank()
                    for n in range(nk):
                        op("pe", lambda e: e.matmul(pp[:], lhsT=wt[:, n, :], rhs=oT[:, n, sl], start=(n == 0), stop=(n == nk - 1)), r=[wtb, oTb[n]], w=[ppb], inc=(n == nk - 1))
                    pg, pgb = kb.bank()
                    for k in range(8):
                        op("pe", lambda e: e.matmul(pg[:], lhsT=gw[:, k, :], rhs=hT[:, k, sl], start=(k == 0), stop=(k == 7)), r=[gwb] + hbuf[tg * 4:tg * 4 + 4], w=[pgb], inc=(k == 7))
                    sg, sgb = sgp.get()
                    op("act", lambda e: e.activation(out=sg[:], in_=pg[:], func=AF.Sigmoid), r=[pgb], w=[sgb])
                    if first:
                        op("dve", lambda e: e.tensor_tensor(out=mT[:, j, sl], in0=pp[:], in1=sg[:], op=ALU.mult), r=[ppb, sgb], w=[mTb[j]])
                    else:
                        op("dve", lambda e: e.tensor_tensor(out=sg[:], in0=pp[:], in1=sg[:], op=ALU.mult), r=[ppb, sgb], w=[sgb])
                        op("pool", lambda e: e.tensor_tensor(out=mT[:, j, sl], in0=mT[:, j, sl], in1=sg[:], op=ALU.add), r=[sgb, mTb[j]], w=[mTb[j]])
                pp, ppb = kb.bank()
                for n in range(nk):
                    op("pe", lambda e: e.matmul(pp[:, 0:NS], lhsT=wt[:, n, :], rhs=osT[:, n, :], start=(n == 0), stop=(n == nk - 1)), r=[wtb, Bos], w=[ppb], inc=False)
                for k in range(8):
                    op("pe", lambda e: e.matmul(pp[:, NS:2 * NS], lhsT=gw[:, k, :], rhs=hsT[:, k, :], start=(k == 0), stop=(k == 7)), r=[gwb, Bhs], w=[ppb], inc=(k == 7))
                sg, sgb = sgp.get()
                op("act", lambda e: e.activation(out=sg[:, 0:NS], in_=pp[:, NS:2 * NS], func=AF.Sigmoid), r=[ppb], w=[sgb])
                if first:
                    op("dve", lambda e: e.tensor_tensor(out=msT[:, j, :], in0=pp[:, 0:NS], in1=sg[:, 0:NS], op=ALU.mult), r=[ppb, sgb], w=[Bms])
                else:
                    op("dve", lambda e: e.tensor_tensor(out=sg[:, 0:NS], in0=pp[:, 0:NS], in1=sg[:, 0:NS], op=ALU.mult), r=[ppb, sgb], w=[sgb])
                    op("dve", lambda e: e.tensor_tensor(out=msT[:, j, :], in0=msT[:, j, :], in1=sg[:, 0:NS], op=ALU.add), r=[sgb, Bms], w=[Bms])

        sgp = Rot(kb, "sgp", [128, 512], F32, 2)

        for l in range(L if stage >= 10 else 1):
            xsrc = I["xp"] if l == 0 else xb
            with ExitStack() as es1:
                kb.es = es1
                xin = Rot(kb, "xin", [128, D], F32, 2)
                junkp = Rot(kb, "junk", [128, D], F32, 2)
                xnp = Rot(kb, "xn", [128, D], BF16, 2)
                tmps = Rot(kb, "tmps", [128, 8, NS], F32, 3)
                if l == 0:
                    for tb in range(16):
                        xt, bxt = xin.get()
                        dma("sp", xt[:], xsrc[tb * 128:(tb + 1) * 128, :], w=[bxt])
                        norm_block(xt, bxt, A1[l], 0, l, hT, hbuf, tb, junkp, xnp)
                if l == 0:
                    sample_norm(l, As1[l], 0, tmps)
                kb.barrier()
            kb.es = es

            with ExitStack() as es2:
                kb.es = es2
                rgw = kb.sb("rgw", [128, 20, 128], BF16)
                Brgw = Buf("rgw")
                dma("pool", rgw[:], I["rgw"][l].rearrange("n j k -> j n k"), w=[Brgw])
                cneg = kb.sb("cneg", [128, 10], F32)
                Bcn = Buf("cneg")
                op("act", lambda e: e.activation(out=cneg[:], in_=vecT[l][:, V_RGLAM:V_RGLAM + 10], func=AF.Exp, scale=-1.0), r=[Bmod[l]], w=[Bcn])
                op("act", lambda e: e.activation(out=cneg[:], in_=cneg[:], func=AF.Ln, bias=1.0), r=[Bcn], w=[Bcn])
                op("dve", lambda e: e.tensor_scalar(out=cneg[:], in0=cneg[:], scalar1=-8.0, scalar2=None, op0=ALU.mult), r=[Bcn], w=[Bcn])
                f32p = Rot(kb, "rgf", [128, T + 3], F32, 4)
                b16p = Rot(kb, "rgb", [128, T], BF16, 3)
                smallp = Rot(kb, "rgs", [128, 8, NS], F32, 6)
                scv = kb.sb("scv", [128, 10, NS, 3], F32)
                sh0 = kb.sb("sh0", [128, 10, NS], F32)
                Bst = Buf("rgstate")
                for n in range(10):
                    dma("sp", scv[:, n], I["st_rg_conv"][l, :, :, n * 128:(n + 1) * 128].rearrange("s j p -> p s j"), w=[Bst])
                for n in range(10):
                    dma("sp", sh0[:, n, :], I["st_rg_h"][l, :, n * 128:(n + 1) * 128].rearrange("s p -> p s"), w=[Bst])
                for n in range(10):
                    ux, uxb = f32p.get()
                    op("pool", lambda e: e.memset(ux[:, 0:3], 0.0), w=[uxb])
                    uxs, uxsb = smallp.get()
                    gy, gyb = b16p.get()
                    gys, gysb = smallp.get()

                    def ev_x(tg, pt, pbuf):
                        op("act", lambda e: e.activation(out=ux[:, 3 + tg * 512:3 + (tg + 1) * 512], in_=pt[:], func=AF.Copy), r=[pbuf], w=[uxb])

                    def ev_xs(pt, pbuf):
                        op("act", lambda e: e.activation(out=uxs[:, 0, :], in_=pt[:, 0:NS], func=AF.Copy), r=[pbuf], w=[uxsb])

                    def ev_y(tg, pt, pbuf):
                        op("act", lambda e: e.activation(out=gy[:, tg * 512:(tg + 1) * 512], in_=pt[:], func=AF.Gelu_apprx_tanh), r=[pbuf], w=[gyb])

                    def ev_ys(pt, pbuf):
                        op("act", lambda e: e.activation(out=gys[:, 0, :], in_=pt[:, 0:NS], func=AF.Gelu_apprx_tanh), r=[pbuf], w=[gysb])

                    inproj(l, CI_RX + n, hT, hbuf, 128, ev_x, ev_xs)
                    inproj(l, CI_RY + n, hT, hbuf, 128, ev_y, ev_ys)
                    dma("sp", O["rg_conv_p"][l, :, n * 128:(n + 1) * 128].rearrange("j p -> p j"), ux[:, T:T + 3], r=[uxb], is_out=True)
                    xc, xcb = f32p.get()
                    cw = lambda j: vecT[l][:, V_RGCW + j * 10 + n:V_RGCW + j * 10 + n + 1]
                    op("dve", lambda e: e.tensor_scalar(out=xc[:, 0:T], in0=ux[:, 3:3 + T], scalar1=cw(3), scalar2=vecT[l][:, V_RGCB + n:V_RGCB + n + 1], op0=ALU.mult, op1=ALU.add),
                       r=[uxb, Bmod[l]], w=[xcb])
                    for j in range(3):
                        op("dve", lambda e: e.scalar_tensor_tensor(out=xc[:, 0:T], in0=ux[:, j:j + T], scalar=cw(j), in1=xc[:, 0:T], op0=ALU.mult, op1=ALU.add),
                           r=[uxb, xcb, Bmod[l]], w=[xcb])
                    xcbf, xcbfb = b16p.get()
                    op("pool", lambda e: e.tensor_copy(out=xcbf[:], in_=xc[:, 0:T]), r=[xcb], w=[xcbfb])
                    xcs, xcsb = smallp.get()
                    op("dve", lambda e: e.tensor_scalar(out=xcs[:, 0, :], in0=uxs[:, 0, :], scalar1=cw(3), scalar2=vecT[l][:, V_RGCB + n:V_RGCB + n + 1], op0=ALU.mult, op1=ALU.add),
                       r=[uxsb, Bmod[l]], w=[xcsb])
                    for j in range(3):
                        op("dve", lambda e: e.scalar_tensor_tensor(out=xcs[:, 0, :], in0=scv[:, n, :, j], scalar=cw(j), in1=xcs[:, 0, :], op0=ALU.mult, op1=ALU.add),
                           r=[Bst, xcsb, Bmod[l]], w=[xcsb])
                    op("dve", lambda e: e.tensor_copy(out=xcs[:, 1, :].bitcast(BF16)[:, 0:NS], in_=xcs[:, 0, :]), r=[xcsb], w=[xcsb])
                    for j in range(2):
                        dma("sp", O["rg_conv_s"][l, :, j, n * 128:(n + 1) * 128].rearrange("s p -> p s"), scv[:, n, :, j + 1], r=[Bst], is_out=True)
                    dma("sp", O["rg_conv_s"][l, :, 2, n * 128:(n + 1) * 128].rearrange("s p -> p s"), uxs[:, 0, :], r=[uxsb], is_out=True)
                    a_t, a_b = f32p.get()
                    i_t, i_b = f32p.get()
                    for tg in range(4):
                        sl = slice(tg * 512, (tg + 1) * 512)
                        pt, pbuf = kb.bank()
                        op("pe", lambda e: e.matmul(pt[:], lhsT=rgw[:, n, :], rhs=xcbf[:, sl], start=True, stop=True), r=[Brgw, xcbfb], w=[pbuf])
                        op("act", lambda e: e.activation(out=a_t[:, sl], in_=pt[:], func=AF.Sigmoid, bias=vecT[l][:, V_RGBA + n:V_RGBA + n + 1]), r=[pbuf, Bmod[l]], w=[a_b])
                        pt2, pbuf2 = kb.bank()
                        op("pe", lambda e: e.matmul(pt2[:], lhsT=rgw[:, 10 + n, :], rhs=xcbf[:, sl], start=True, stop=True), r=[Brgw, xcbfb], w=[pbuf2])
                        op("act", lambda e: e.activation(out=i_t[:, sl], in_=pt2[:], func=AF.Sigmoid, bias=vecT[l][:, V_RGBX + n:V_RGBX + n + 1]), r=[pbuf2, Bmod[l]], w=[i_b])
                    pt, pbuf = kb.bank()
                    op("pe", lambda e: e.matmul(pt[:, 0:NS], lhsT=rgw[:, n, :], rhs=xcs[:, 1, :].bitcast(BF16)[:, 0:NS], start=True, stop=True), r=[Brgw, xcsb], w=[pbuf], inc=False)
                    op("pe", lambda e: e.matmul(pt[:, NS:2 * NS], lhsT=rgw[:, 10 + n, :], rhs=xcs[:, 1, :].bitcast(BF16)[:, 0:NS], start=True, stop=True), r=[Brgw, xcsb], w=[pbuf])
                    gs, gsb = smallp.get()
                    op("act", lambda e: e.activation(out=gs[:, 0, :], in_=pt[:, 0:NS], func=AF.Sigmoid, bias=vecT[l][:, V_RGBA + n:V_RGBA + n + 1]), r=[pbuf, Bmod[l]], w=[gsb])
                    op("act", lambda e: e.activation(out=gs[:, 1, :], in_=pt[:, NS:2 * NS], func=AF.Sigmoid, bias=vecT[l][:, V_RGBX + n:V_RGBX + n + 1]), r=[pbuf, Bmod[l]], w=[gsb])
                    op("act", lambda e: e.activation(out=a_t[:, 0:T], in_=a_t[:, 0:T], func=AF.Exp, scale=cneg[:, n:n + 1]), r=[a_b, Bcn], w=[a_b])
                    op("act", lambda e: e.activation(out=gs[:, 0, :], in_=gs[:, 0, :], func=AF.Exp, scale=cneg[:, n:n + 1]), r=[gsb, Bcn], w=[gsb])
                    op("pool", lambda e: e.tensor_tensor(out=i_t[:, 0:T], in0=i_t[:, 0:T], in1=xc[:, 0:T], op=ALU.mult), r=[i_b, xcb], w=[i_b])
                    op("dve", lambda e: e.tensor_tensor(out=xc[:, 0:T], in0=a_t[:, 0:T], in1=a_t[:, 0:T], op=ALU.mult), r=[a_b, xcb], w=[xcb])
                    op("act", lambda e: e.activation(out=xc[:, 0:T], in_=xc[:, 0:T], func=AF.Sqrt, scale=-1.0, bias=1.0), r=[xcb], w=[xcb])
                    op("pool", lambda e: e.tensor_tensor(out=i_t[:, 0:T], in0=i_t[:, 0:T], in1=xc[:, 0:T], op=ALU.mult), r=[i_b, xcb], w=[i_b])
                    op("dve", lambda e: e.tensor_tensor_scan(out=xc[:, 0:T], data0=a_t[:, 0:T], data1=i_t[:, 0:T], initial=0.0, op0=ALU.mult, op1=ALU.add), r=[a_b, i_b, xcb], w=[xcb])
                    op("dve", lambda e: e.tensor_tensor(out=oT[:, n, :], in0=xc[:, 0:T], in1=gy[:], op=ALU.mult), r=[xcb, gyb], w=[oTb[n]])
                    dma("sp", O["rg_h_p"][l, n * 128:(n + 1) * 128].rearrange("(p o) -> p o", o=1), xc[:, T - 1:T], r=[xcb], is_out=True)
                    op("dve", lambda e: e.tensor_tensor(out=gs[:, 1, :], in0=gs[:, 1, :], in1=xcs[:, 0, :], op=ALU.mult), r=[gsb, xcsb], w=[gsb])
                    op("dve", lambda e: e.tensor_tensor(out=gs[:, 2, :], in0=gs[:, 0, :], in1=gs[:, 0, :], op=ALU.mult), r=[gsb], w=[gsb])
                    op("act", lambda e: e.activation(out=gs[:, 2, :], in_=gs[:, 2, :], func=AF.Sqrt, scale=-1.0, bias=1.0), r=[gsb], w=[gsb])
                    op("dve", lambda e: e.tensor_tensor(out=gs[:, 1, :], in0=gs[:, 1, :], in1=gs[:, 2, :], op=ALU.mult), r=[gsb], w=[gsb])
                    op("dve", lambda e: e.tensor_tensor(out=gs[:, 3, :], in0=gs[:, 0, :], in1=sh0[:, n, :], op=ALU.mult), r=[gsb, Bst], w=[gsb])
                    op("dve", lambda e: e.tensor_tensor(out=gs[:, 3, :], in0=gs[:, 3, :], in1=gs[:, 1, :], op=ALU.add), r=[gsb], w=[gsb])
                    dma("sp", O["rg_h_s"][l, :, n * 128:(n + 1) * 128].rearrange("s p -> p s"), gs[:, 3, :], r=[gsb], is_out=True)
                    op("dve", lambda e: e.tensor_tensor(out=osT[:, n, :], in0=gs[:, 3, :], in1=gys[:, 0, :], op=ALU.mult), r=[gsb, gysb], w=[Bos])
                kb.barrier()
            kb.es = es
            branch_proj(l, "wrgp", 10, CI_GATE, True)
            kb.barrier()
            if stage < 2:
                continue

            with ExitStack() as es3:
                kb.es = es3
                print("sbuf remaining before GDN", nc.sbuf_bytes_remaining)
                TE = T + NS * 128
                NB = 16 + NS
                NC8 = NB * 8
                gab = kb.sb("gab", [8, 2], F32)
                negA = kb.sb("negA", [8, 1], F32)
                names_ = ["g_tok", "b_tok", "gcum", "eg", "negeg", "ed", "eglast", "negb"]
                G = {n_: kb.sb(n_, [128, NC8], F32) for n_ in names_}
                es3a = ExitStack()
                kb.es = es3a
                g_fm = kb.sb("g_fm", [8, TE], F32)
                b_fm = kb.sb("b_fm", [8, TE], F32)
                kb.es = es3
                Bg = Buf("gdn_g")
                dma("sp", gab[:], I["gdn_ab"][l].rearrange("a h -> h a"), w=[Bg])
                op("act", lambda e: e.activation(out=negA[:], in_=gab[:, 0:1], func=AF.Exp), r=[Bg], w=[Bg])
                op("dve", lambda e: e.tensor_scalar(out=negA[:], in0=negA[:], scalar1=-1.0, scalar2=None, op0=ALU.mult), r=[Bg], w=[Bg])
                op("pool", lambda e: e.memset(g_fm[:], 0.0), w=[Bg])
                op("pool", lambda e: e.memset(b_fm[:], 0.0), w=[Bg])
                wt, wtb = wpiece.get()
                dma("pool", wt[:], I["win"][l, CI_AB], w=[wtb])
                scol = lambda t_: t_[:, T:TE].rearrange("p (s t) -> p s t", t=128)[:, :, 0]
                for tg in range(5):
                    sl = slice(tg * 512, (tg + 1) * 512)
                    for (c0, dst, fn, bias) in ((0, g_fm, AF.Exp, gab[:, 1:2]), (8, b_fm, AF.Sigmoid, None)):
                        pa, pab = kb.bank()
                        for k in range(8):
                            if tg < 4:
                                op("pe", lambda e: e.matmul(pa[0:8, :], lhsT=wt[:, k, c0:c0 + 8], rhs=hT[:, k, sl], start=(k == 0), stop=(k == 7)), r=[wtb] + hbuf[tg * 4:tg * 4 + 4], w=[pab], inc=(k == 7))
                            else:
                                op("pe", lambda e: e.matmul(pa[0:8, 0:NS], lhsT=wt[:, k, c0:c0 + 8], rhs=hsT[:, k, :], start=(k == 0), stop=(k == 7)), r=[wtb, Bhs], w=[pab], inc=(k == 7))
                        src = pa[0:8, :] if tg < 4 else pa[0:8, 0:NS]
                        dstv = dst[:, sl] if tg < 4 else scol(dst)
                        if bias is not None:
                            op("act", lambda e: e.activation(out=dstv, in_=src, func=fn, bias=bias), r=[pab, Bg], w=[Bg])
                        else:
                            op("act", lambda e: e.activation(out=dstv, in_=src, func=fn), r=[pab, Bg], w=[Bg])
                op("act", lambda e: e.activation(out=g_fm[:], in_=g_fm[:], func=AF.Ln, bias=1.0), r=[Bg], w=[Bg])
                op("dve", lambda e: e.tensor_scalar(out=g_fm[:], in0=g_fm[:], scalar1=negA[:, 0:1], scalar2=None, op0=ALU.mult), r=[Bg], w=[Bg])
                for (srcfm, dstn) in ((g_fm, "g_tok"), (b_fm, "b_tok")):
                    pt, ptb = kb.bank()
                    for nb in range(NB):
                        op("pe", lambda e: e.transpose(pt[:, nb * 8:(nb + 1) * 8], srcfm[:, nb * 128:(nb + 1) * 128], ident_f[0:8, 0:8]), r=[Bg, Bc], w=[ptb], inc=(nb == NB - 1))
                    op("dve", lambda e: e.tensor_copy(out=G[dstn][:], in_=pt[:, 0:NC8]), r=[ptb], w=[Bg])
                pt, ptb = kb.bank()
                op("pe", lambda e: e.matmul(pt[:, 0:NC8], lhsT=gm[:, 0, :], rhs=G["g_tok"][:], start=True, stop=True), r=[Bg, Bc], w=[ptb])
                op("dve", lambda e: e.tensor_copy(out=G["gcum"][:], in_=pt[:, 0:NC8]), r=[ptb], w=[Bg])
                op("act", lambda e: e.activation(out=G["eg"][:], in_=G["gcum"][:], func=AF.Exp), r=[Bg], w=[Bg])
                op("dve", lambda e: e.tensor_scalar(out=G["negeg"][:], in0=G["eg"][:], scalar1=-1.0, scalar2=None, op0=ALU.mult), r=[Bg], w=[Bg])
                op("dve", lambda e: e.tensor_scalar(out=G["negb"][:], in0=G["b_tok"][:], scalar1=-1.0, scalar2=None, op0=ALU.mult), r=[Bg], w=[Bg])
                pt, ptb = kb.bank()
                op("pe", lambda e: e.matmul(pt[:, 0:NC8], lhsT=gm[:, 1, :], rhs=G["gcum"][:], start=True, stop=True), r=[Bg, Bc], w=[ptb])
                op("act", lambda e: e.activation(out=G["eglast"][:], in_=pt[:, 0:NC8], func=AF.Exp), r=[ptb], w=[Bg])
                op("dve", lambda e: e.tensor_tensor(out=G["ed"][:], in0=pt[:, 0:NC8], in1=G["gcum"][:], op=ALU.subtract), r=[ptb, Bg], w=[Bg])
                op("act", lambda e: e.activation(out=G["ed"][:], in_=G["ed"][:], func=AF.Exp), r=[Bg], w=[Bg])
                kb.barrier()
                es3a.close()
                gcv = kb.sb("gcv", [128, 24, NS, 3], F32)
                Bgcv = Buf("gcv")
                for ch in range(24):
                    dma("sp", gcv[:, ch], I["st_gdn_conv"][l, :, :, ch * 128:(ch + 1) * 128].rearrange("s j p -> p s j"), w=[Bgcv])
                f32p = Rot(kb, "gdf", [128, T + 3], F32, 2)
                csp = Rot(kb, "gdcs", [128, TE], BF16, 4)
                tokp = Rot(kb, "gdtok", [128, NB, 128], BF16, 4)
                smallp = Rot(kb, "gds", [128, 8, NS], F32, 4)
                m32 = Rot(kb, "m32_", [128, 128], F32, 6)
                m16 = Rot(kb, "m16_", [128, 128], BF16, 16)
                mlong = Rot(kb, "mlong_", [128, 128], BF16, 6)
                ssq = kb.sb("ssq", [128, 4, NB], F32)
                Bssq = Buf("ssq")
                sqt = kb.sb("sqt", [128, 8, 128], BF16)
                Bsqt = Buf("sqt")
                S_f = kb.sb("S_f", [128, 128], F32)
                S_b = kb.sb("S_b", [128, 128], BF16)
                BS = Buf("S")
                st4 = kb.sb("st4", [128, 4], F32)
                Bst4 = Buf("st4")

                def to_tok(src_fm, srcb, dst_tok, dstb):
                    for g0 in range(0, NB, 8):
                        ng = min(8, NB - g0)
                        pt, ptb = kb.bank()
                        ptv = pt[:].bitcast(BF16)
                        for i in range(ng):
                            nb = g0 + i
                            op("pe", lambda e: e.transpose(ptv[:, i * 128:(i + 1) * 128], src_fm[:, nb * 128:(nb + 1) * 128], ident_b[:]), r=[srcb, Bc], w=[ptb], inc=(i == ng - 1))
                        op("act", lambda e: e.activation(out=dst_tok[:, g0:g0 + ng, :].rearrange("p a b -> p (a b)"), in_=ptv[:, 0:ng * 128], func=AF.Copy), r=[ptb], w=[dstb])

                def to_fm(src_tok, srcb, dst_fm, dstb):
                    for g0 in range(0, NB, 8):
                        ng = min(8, NB - g0)
                        pt, ptb = kb.bank()
                        ptv = pt[:].bitcast(BF16)
                        for i in range(ng):
                            nb = g0 + i
                            op("pe", lambda e: e.transpose(ptv[:, i * 128:(i + 1) * 128], src_tok[:, nb, :], ident_b[:]), r=[srcb, Bc], w=[ptb], inc=(i == ng - 1))
                        op("dve", lambda e: e.tensor_copy(out=dst_fm[:, g0 * 128:(g0 + ng) * 128], in_=ptv[:, 0:ng * 128]), r=[ptb], w=[dstb])

                for h in range(H):
                    cs = {}
                    for pi, (pname, ci0) in enumerate((("q", CI_Q), ("k", CI_K), ("v", CI_V))):
                        ch = pi * 8 + h
                        ux, uxb = f32p.get()
                        op("pool", lambda e: e.memset(ux[:, 0:3], 0.0), w=[uxb])
                        uxs, uxsb = smallp.get()

                        def ev_x(tg, pt, pbuf):
                            op("act", lambda e: e.activation(out=ux[:, 3 + tg * 512:3 + (tg + 1) * 512], in_=pt[:], func=AF.Copy), r=[pbuf], w=[uxb])

                        def ev_xs(pt, pbuf):
                            op("act", lambda e: e.activation(out=uxs[:, 0, :], in_=pt[:, 0:NS], func=AF.Copy), r=[pbuf], w=[uxsb])
                        inproj(l, ci0 + h, hT, hbuf, 128, ev_x, ev_xs)
                        dma("sp", O["gdn_conv_p"][l, :, ch * 128:(ch + 1) * 128].rearrange("j p -> p j"), ux[:, T:T + 3], r=[uxb], is_out=True)
                        xc, xcb = f32p.get()
                        cw = lambda j: vecT[l][:, V_GDCW + j * 24 + ch:V_GDCW + j * 24 + ch + 1]
                        eng = "dve" if pi != 1 else "pool"
                        op(eng, lambda e: e.tensor_scalar(out=xc[:, 0:T], in0=ux[:, 3:3 + T], scalar1=cw(3), scalar2=None, op0=ALU.mult), r=[uxb, Bmod[l]], w=[xcb])
                        for j in range(3):
                            op("dve", lambda e: e.scalar_tensor_tensor(out=xc[:, 0:T], in0=ux[:, j:j + T], scalar=cw(j), in1=xc[:, 0:T], op0=ALU.mult, op1=ALU.add),
                               r=[uxb, xcb, Bmod[l]], w=[xcb])
                        c_t, c_b = csp.get()
                        op("pool", lambda e: e.memset(c_t[:, T:TE], 0.0), w=[c_b])
                        op("act", lambda e: e.activation(out=c_t[:, 0:T], in_=xc[:, 0:T], func=AF.Silu), r=[xcb], w=[c_b])
                        op("dve", lambda e: e.tensor_scalar(out=uxs[:, 1, :], in0=uxs[:, 0, :], scalar1=cw(3), scalar2=None, op0=ALU.mult), r=[uxsb, Bmod[l]], w=[uxsb])
                        for j in range(3):
                            op("dve", lambda e: e.scalar_tensor_tensor(out=uxs[:, 1, :], in0=gcv[:, ch, :, j], scalar=cw(j), in1=uxs[:, 1, :], op0=ALU.mult, op1=ALU.add),
                               r=[Bgcv, uxsb, Bmod[l]], w=[uxsb])
                        op("act", lambda e: e.activation(out=scol(c_t), in_=uxs[:, 1, :], func=AF.Silu), r=[uxsb], w=[c_b])
                        for j in range(2):
                            dma("sp", O["gdn_conv_s"][l, :, j, ch * 128:(ch + 1) * 128].rearrange("s p -> p s"), gcv[:, ch, :, j + 1], r=[Bgcv], is_out=True)
                        dma("sp", O["gdn_conv_s"][l, :, 2, ch * 128:(ch + 1) * 128].rearrange("s p -> p s"), uxs[:, 0, :], r=[uxsb], is_out=True)
                        cs[pname] = (c_t, c_b)
                    zs, zsb = csp.get()
                    op("pool", lambda e: e.memset(zs[:, T:TE], 0.0), w=[zsb])

                    def ev_z(tg, pt, pbuf):
                        op("act", lambda e: e.activation(out=zs[:, tg * 512:(tg + 1) * 512], in_=pt[:], func=AF.Silu), r=[pbuf], w=[zsb])

                    def ev_zs(pt, pbuf):
                        op("act", lambda e: e.activation(out=scol(zs), in_=pt[:, 0:NS], func=AF.Silu), r=[pbuf], w=[zsb])
                    inproj(l, CI_Z + h, hT, hbuf, 128, ev_z, ev_zs)
                    toks = {}
                    for pi, pname in enumerate(("q", "k", "v")):
                        tt, ttb = tokp.get()
                        to_tok(cs[pname][0], cs[pname][1], tt, ttb)
                        toks[pname] = (tt, ttb)
                        if pname == "v":
                            continue
                        for g0 in range(0, NB, 8):
                            ng = min(8, NB - g0)
                            op("dve", lambda e: e.tensor_tensor(out=sqt[:, 0:ng, :], in0=tt[:, g0:g0 + ng, :], in1=tt[:, g0:g0 + ng, :], op=ALU.mult), r=[ttb, Bsqt], w=[Bsqt])
                            op("dve", lambda e: e.tensor_reduce(out=ssq[:, pi, g0:g0 + ng], in_=sqt[:, 0:ng, :], op=ALU.add, axis=AX.X), r=[Bsqt, Bssq], w=[Bssq])
                        op("dve", lambda e: e.tensor_scalar(out=ssq[:, pi, :], in0=ssq[:, pi, :], scalar1=EPS, scalar2=None, op0=ALU.add), r=[Bssq], w=[Bssq])
                        op("act", lambda e: e.activation(out=ssq[:, pi, :], in_=ssq[:, pi, :], func=AF.Sqrt), r=[Bssq], w=[Bssq])
                        op("dve", lambda e: e.reciprocal(out=ssq[:, 2 + pi, :], in_=ssq[:, pi, :]), r=[Bssq], w=[Bssq])
                        if pname == "q":
                            op("dve", lambda e: e.tensor_scalar(out=ssq[:, 2, :], in0=ssq[:, 2, :], scalar1=128.0 ** -0.5, scalar2=None, op0=ALU.mult), r=[Bssq], w=[Bssq])
                        op("pool", lambda e: e.tensor_tensor(out=tt[:], in0=tt[:], in1=ssq[:, 2 + pi, :].unsqueeze(2).to_broadcast([128, NB, 128]), op=ALU.mult), r=[ttb, Bssq], w=[ttb])
                    qT, qTb = csp.get()
                    kT, kTb = csp.get()
                    to_fm(toks["q"][0], toks["q"][1], qT, qTb)
                    to_fm(toks["k"][0], toks["k"][1], kT, kTb)
                    k_tok, k_tokb = toks["k"]
                    v_tok, v_tokb = toks["v"]
                    kd, kdb = tokp.get()
                    op("pool", lambda e: e.tensor_tensor(out=kd[:], in0=k_tok[:], in1=G["ed"][:].rearrange("p (n h) -> p n h", h=8)[:, :, h:h + 1].to_broadcast([128, NB, 128]), op=ALU.mult),
                       r=[k_tokb, Bg], w=[kdb])
                    for chain in range(1 + NS):
                        blocks = list(range(16)) if chain == 0 else [16 + chain - 1]
                        if chain == 0:
                            op("pool", lambda e: e.memset(S_f[:], 0.0), w=[BS])
                            op("pool", lambda e: e.memset(S_b[:], 0.0), w=[BS])
                        else:
                            dma("sp", S_f[:], I["st_gdn_S"][l, chain - 1, h], w=[BS])
                            op("act", lambda e: e.activation(out=S_b[:], in_=S_f[:], func=AF.Copy), r=[BS], w=[BS])
                        for nb in blocks:
                            bs = slice(nb * 128, (nb + 1) * 128)
                            col = nb * 8 + h
                            cc_ = lambda nm: G[nm][:, col:col + 1]
                            gb, gbb = m32.get()
                            op("pool", lambda e: e.tensor_scalar(out=gb[:], in0=ones_f[:], scalar1=cc_("g_tok"), scalar2=None, op0=ALU.mult), r=[Bg, Bc], w=[gbb])
                            pG, pGb = kb.bank()
                            op("pe", lambda e: e.matmul(pG[:, 0:128], lhsT=gb[:], rhs=gm[:, 0, :], start=True, stop=True), r=[gbb, Bc], w=[pGb])
                            dTs, dTsb = m32.get()
                            dTi, dTib = m32.get()
                            op("dve", lambda e: e.scalar_tensor_tensor(out=dTs[:], in0=pG[:, 0:128], scalar=cc_("gcum"), in1=gm[:, 2, :], op0=ALU.subtract, op1=ALU.subtract), r=[pGb, Bg, Bc], w=[dTsb])
                            op("dve", lambda e: e.scalar_tensor_tensor(out=dTi[:], in0=pG[:, 0:128], scalar=cc_("gcum"), in1=gm[:, 3, :], op0=ALU.subtract, op1=ALU.subtract), r=[pGb, Bg, Bc], w=[dTib])
                            op("act", lambda e: e.activation(out=dTs[:], in_=dTs[:], func=AF.Exp), r=[dTsb], w=[dTsb])
                            op("act", lambda e: e.activation(out=dTi[:], in_=dTi[:], func=AF.Exp), r=[dTib], w=[dTib])
                            pK, pKb = kb.bank()
                            op("pe", lambda e: e.matmul(pK[:, 0:128], lhsT=kT[:, bs], rhs=kT[:, bs], start=True, stop=True), r=[kTb], w=[pKb])
                            P, Pb = mlong.get()
                            op("dve", lambda e: e.scalar_tensor_tensor(out=P[:], in0=pK[:, 0:128], scalar=cc_("negb"), in1=dTs[:], op0=ALU.mult, op1=ALU.mult), r=[pKb, Bg, dTsb], w=[Pb])
                            pQ, pQb = kb.bank()
                            op("pe", lambda e: e.matmul(pQ[:, 0:128], lhsT=kT[:, bs], rhs=qT[:, bs], start=True, stop=True), r=[kTb, qTb], w=[pQb])
                            AqT, AqTb = mlong.get()
                            op("dve", lambda e: e.tensor_tensor(out=AqT[:], in0=pQ[:, 0:128], in1=dTi[:], op=ALU.mult), r=[pQb, dTib], w=[AqTb])
                            pT_, pT_b = kb.bank()
                            pTv = pT_[:].bitcast(BF16)
                            op("pe", lambda e: e.transpose(pTv[:, 0:128], P[:], ident_b[:]), r=[Pb, Bc], w=[pT_b])
                            PT, PTb = mlong.get()
                            op("act", lambda e: e.activation(out=PT[:], in_=pTv[:, 0:128], func=AF.Copy), r=[pT_b], w=[PTb])
                            A0, A0b = m16.get()
                            A0T, A0Tb = m16.get()
                            op("pool", lambda e: e.tensor_tensor(out=A0[:], in0=P[:], in1=lm[:, 0, :], op=ALU.mult), r=[Pb, Bc], w=[A0b])
                            op("pool", lambda e: e.tensor_tensor(out=A0T[:], in0=PT[:], in1=lm[:, 1, :], op=ALU.mult), r=[PTb, Bc], w=[A0Tb])
                            X, Xb = m16.get()
                            XT, XTb = m16.get()
                            op("dve", lambda e: e.tensor_tensor(out=X[:], in0=A0[:], in1=ident_b[:], op=ALU.add), r=[A0b, Bc], w=[Xb])
                            op("dve", lambda e: e.tensor_tensor(out=XT[:], in0=A0T[:], in1=ident_b[:], op=ALU.add), r=[A0Tb, Bc], w=[XTb])
                            for lev in range(1, 7):
                                As, Asb = m16.get()
                                AsT, AsTb = m16.get()
                                op("pool", lambda e: e.tensor_tensor(out=AsT[:], in0=PT[:], in1=lm[:, 2 * lev + 1, :], op=ALU.mult), r=[PTb, Bc], w=[AsTb])
                                pA, pAb = kb.bank()
                                op("pe", lambda e: e.matmul(pA[:, 0:128], lhsT=AsT[:], rhs=X[:], start=True, stop=True), r=[AsTb, Xb], w=[pAb])
                                Y, Yb = m16.get()
                                op("act", lambda e: e.activation(out=Y[:], in_=pA[:, 0:128], func=AF.Copy), r=[pAb], w=[Yb])
                                if lev < 6:
                                    op("pool", lambda e: e.tensor_tensor(out=As[:], in0=P[:], in1=lm[:, 2 * lev, :], op=ALU.mult), r=[Pb, Bc], w=[Asb])
                                    pB, pBb = kb.bank()
                                    op("pe", lambda e: e.matmul(pB[:, 0:128], lhsT=As[:], rhs=XT[:], start=True, stop=True), r=[Asb, XTb], w=[pBb])
                                    Y2, Y2b = m16.get()
                                    op("dve", lambda e: e.tensor_copy(out=Y2[:], in_=pB[:, 0:128]), r=[pBb], w=[Y2b])
                                pX, pXb = kb.bank()
                                op("pe", lambda e: e.matmul(pX[:, 0:128], lhsT=XT[:], rhs=Y[:], start=True, stop=True), r=[XTb, Yb], w=[pXb])
                                Xn, Xnb = m16.get()
                                op("dve", lambda e: e.tensor_tensor(out=Xn[:], in0=pX[:, 0:128], in1=X[:], op=ALU.add), r=[pXb, Xb], w=[Xnb])
                                if lev < 6:
                                    pZ, pZb = kb.bank()
                                    op("pe", lambda e: e.matmul(pZ[:, 0:128], lhsT=X[:], rhs=Y2[:], start=True, stop=True), r=[Xb, Y2b], w=[pZb])
                                    XTn, XTnb = m16.get()
                                    op("dve", lambda e: e.tensor_tensor(out=XTn[:], in0=pZ[:, 0:128], in1=XT[:], op=ALU.add), r=[pZb, XTb], w=[XTnb])
                                    XT, XTb = XTn, XTnb
                                X, Xb = Xn, Xnb
                            pS, pSb = kb.bank()
                            op("pe", lambda e: e.matmul(pS[:, 0:128], lhsT=kT[:, bs], rhs=S_b[:], start=True, stop=True), r=[kTb, BS], w=[pSb])
                            rr, rrb = m16.get()
                            op("dve", lambda e: e.scalar_tensor_tensor(out=rr[:], in0=pS[:, 0:128], scalar=cc_("negeg"), in1=v_tok[:, nb, :], op0=ALU.mult, op1=ALU.add), r=[pSb, Bg, v_tokb], w=[rrb])
                            pW, pWb = kb.bank()
                            op("pe", lambda e: e.matmul(pW[:, 0:128], lhsT=X[:], rhs=rr[:], start=True, stop=True), r=[Xb, rrb], w=[pWb])
                            U, Ub = m16.get()
                            op("act", lambda e: e.activation(out=U[:], in_=pW[:, 0:128], func=AF.Copy, scale=cc_("b_tok")), r=[pWb, Bg], w=[Ub])
                            pO, pOb = kb.bank()
                            op("pe", lambda e: e.matmul(pO[:, 0:128], lhsT=qT[:, bs], rhs=S_b[:], start=True, stop=True), r=[qTb, BS], w=[pOb], inc=False)
                            op("pe", lambda e: e.matmul(pO[:, 128:256], lhsT=AqT[:], rhs=U[:], start=True, stop=True), r=[AqTb, Ub], w=[pOb])
                            o1, o1b = m32.get()
                            op("act", lambda e: e.activation(out=o1[:], in_=pO[:, 128:256], func=AF.Copy), r=[pOb], w=[o1b])
                            op("dve", lambda e: e.scalar_tensor_tensor(out=o1[:], in0=pO[:, 0:128], scalar=cc_("eg"), in1=o1[:], op0=ALU.mult, op1=ALU.add), r=[pOb, Bg, o1b], w=[o1b])
                            pU, pUb = kb.bank()
                            op("pe", lambda e: e.matmul(pU[:, 0:128], lhsT=kd[:, nb, :], rhs=U[:], start=True, stop=True), r=[kdb, Ub], w=[pUb])
                            op("dve", lambda e: e.scalar_tensor_tensor(out=S_f[:], in0=S_f[:], scalar=cc_("eglast"), in1=pU[:, 0:128], op0=ALU.mult, op1=ALU.add), r=[pUb, Bg, BS], w=[BS])
                            op("act", lambda e: e.activation(out=S_b[:], in_=S_f[:], func=AF.Copy), r=[BS], w=[BS])
                            jk, jkb = m32.get()
                            op("act", lambda e: e.activation(out=jk[:], in_=o1[:], func=AF.Square, accum_out=st4[:, 0:1]), r=[o1b], w=[jkb, Bst4])
                            op("dve", lambda e: e.tensor_scalar(out=st4[:, 1:2], in0=st4[:, 0:1], scalar1=1.0 / 128, scalar2=EPS, op0=ALU.mult, op1=ALU.add), r=[Bst4], w=[Bst4])
                            op("act", lambda e: e.activation(out=st4[:, 2:3], in_=st4[:, 1:2], func=AF.Sqrt), r=[Bst4], w=[Bst4])
                            op("dve", lambda e: e.reciprocal(out=st4[:, 3:4], in_=st4[:, 2:3]), r=[Bst4], w=[Bst4])
                            on, onb = m16.get()
                            op("act", lambda e: e.activation(out=on[:], in_=o1[:], func=AF.Copy, scale=st4[:, 3:4]), r=[o1b, Bst4], w=[onb])
                            pN, pNb = kb.bank()
                            pNv = pN[:].bitcast(BF16)
                            op("pe", lambda e: e.transpose(pNv[:, 0:128], on[:], ident_b[:]), r=[onb, Bc], w=[pNb])
                            gcol = vecT[l][:, V_GDG:V_GDG + 1]
                            if chain == 0:
                                op("dve", lambda e: e.scalar_tensor_tensor(out=oT[:, h, bs], in0=pNv[:, 0:128], scalar=gcol, in1=zs[:, bs], op0=ALU.mult, op1=ALU.mult), r=[pNb, Bmod[l], zsb], w=[oTb[h]])
                            else:
                                s_ = chain - 1
                                op("dve", lambda e: e.scalar_tensor_tensor(out=osT[:, h, s_:s_ + 1], in0=pNv[:, 0:1], scalar=gcol, in1=zs[:, nb * 128:nb * 128 + 1], op0=ALU.mult, op1=ALU.mult),
                                   r=[pNb, Bmod[l], zsb], w=[Bos])
                        if chain == 0:
                            dma("sp", O["gdn_S_p"][l, h], S_f[:], r=[BS], is_out=True)
                        else:
                            dma("sp", O["gdn_S_s"][l, chain - 1, h], S_f[:], r=[BS], is_out=True)
                kb.barrier()
            kb.es = es
            branch_proj(l, "wgdp", 8, CI_GATE + 8, False)
            kb.barrier()
            if stage < 3:
                continue

            with ExitStack() as es4:
                kb.es = es4
                print("sbuf remaining before MLA", nc.sbuf_bytes_remaining)
                ropes = kb.sb("ropes", [96, 2], F32)
                rfT_f = kb.sb("rfT_f", [96, 96], F32)
                rfT_b = kb.sb("rfT_b", [96, 96], BF16)
                Bmc = Buf("mla_const")
                dma("sp", ropes[:, 0:1], I["ropeCs"], w=[Bmc])
                dma("sp", ropes[:, 1:2], I["ropeSs"], w=[Bmc])
                dma("sp", rfT_f[:], I["rfullT"], w=[Bmc])
                op("dve", lambda e: e.tensor_copy(out=rfT_b[:], in_=rfT_f[:]), r=[Bmc], w=[Bmc])
                ckvT = kb.sb("ckvT", [128, 2, T], BF16)
                krT = kb.sb("krT", [96, T], BF16)
                Bckv, Bkr, Bcq = Buf("ckvT"), Buf("krT"), Buf("cqT")
                ukvs = kb.sb("ukvs", [128, 2, NS], F32)
                ckvs_f = kb.sb("ckvs_f", [128, 2, NS], F32)
                ckvs_b = kb.sb("ckvs_b", [128, 2, NS], BF16)
                krs = kb.sb("krs", [96, 3, NS], F32)
                krs_b = kb.sb("krs_b", [96, NS], BF16)
                uqs = kb.sb("uqs", [128, 3, NS], F32)
                cqs_b = kb.sb("cqs_b", [128, 3, NS], BF16)
                Bsm = Buf("mla_s")
                def fm_rms(u_views, ubufs, nfeat, gcol0, outs, obufs, W):
                    nch = len(u_views)
                    sq, sqb = sqp.get()
                    for c in range(nch):
                        op("pool", lambda e: e.tensor_tensor(out=sq[:, c, 0:W], in0=u_views[c], in1=u_views[c], op=ALU.mult), r=ubufs, w=[sqb])
                    ps, psb = kb.bank()
                    for c in range(nch):
                        op("pe", lambda e: e.matmul(ps[:, 0:W], lhsT=ones_b[:], rhs=sq[:, c, 0:W], start=(c == 0), stop=(c == nch - 1)), r=[sqb, Bc], w=[psb], inc=(c == nch - 1))
                    rs, rsb = rsp.get()
                    op("act", lambda e: e.activation(out=rs[:, 0:W], in_=ps[:, 0:W], func=AF.Sqrt, scale=1.0 / nfeat, bias=EPS), r=[psb], w=[rsb])
                    op("dve", lambda e: e.reciprocal(out=rs[:, 0:W], in_=rs[:, 0:W]), r=[rsb], w=[rsb])
                    for c in range(nch):
                        for (o_ap, o_b) in zip(outs[c], obufs):
                            op("dve", lambda e: e.scalar_tensor_tensor(out=o_ap, in0=u_views[c], scalar=vecT[l][:, gcol0 + c:gcol0 + c + 1], in1=rs[:, 0:W], op0=ALU.mult, op1=ALU.mult),
                               r=ubufs + [rsb, Bmod[l]], w=[o_b])

                with ExitStack() as es4a:
                    kb.es = es4a
                    ropeC = kb.sb("ropeC", [96, T], F32)
                    ropeS = kb.sb("ropeS", [96, T], F32)
                    dma("sp", ropeC[:], I["ropeC"], w=[Bmc])
                    dma("sp", ropeS[:], I["ropeS"], w=[Bmc])
                    sqp = Rot(kb, "sqp", [128, 3, 512], BF16, 2)
                    rsp = Rot(kb, "rsp", [128, 512], F32, 2)
                    ukv = kb.sb("ukv", [128, 2, T], F32)
                    kr96 = kb.sb("kr96", [96, T], F32)
                    krow = kb.sb("krow", [32, T], BF16)
                    krs_o = kb.sb("krs_o", [32, NS], F32)
                    Bkrow = Buf("krow")
                    Bukv, Bkr96 = Buf("ukv"), Buf("kr96")
                    stg = Rot(kb, "ckvstg", [128, 256], F32, 2)
                    for c in range(2):
                        def ev(tg, pt, pbuf, c=c):
                            op("act", lambda e: e.activation(out=ukv[:, c, tg * 512:(tg + 1) * 512], in_=pt[:], func=AF.Copy), r=[pbuf], w=[Bukv])

                        def evs(pt, pbuf, c=c):
                            op("act", lambda e: e.activation(out=ukvs[:, c, :], in_=pt[:, 0:NS], func=AF.Copy), r=[pbuf], w=[Bsm])
                        inproj(l, CI_CKV + c, hT, hbuf, 128, ev, evs)

                    def ev(tg, pt, pbuf):
                        op("act", lambda e: e.activation(out=kr96[:, tg * 512:(tg + 1) * 512], in_=pt[0:96, :], func=AF.Copy), r=[pbuf], w=[Bkr96])

                    def evs(pt, pbuf):
                        op("act", lambda e: e.activation(out=krs[:, 0, :], in_=pt[0:96, 0:NS], func=AF.Copy), r=[pbuf], w=[Bsm])
                    inproj(l, CI_KR, hT, hbuf, 96, ev, evs)
                    SUB = int(os.environ.get("MLA_SUB", "9"))
                    for tg in range(4 if SUB >= 2 else 0):
                        sl = slice(tg * 512, (tg + 1) * 512)
                        fm_rms([ukv[:, c, sl] for c in range(2)], [Bukv], KVL, V_KVG, [[ckvT[:, c, sl], ukv[:, c, sl]] for c in range(2)], [Bckv, Bukv], 512)
                        if SUB < 3:
                            continue
                        pr, prb = kb.bank()
                        op("act", lambda e: e.activation(out=krT[:, sl], in_=kr96[:, sl], func=AF.Copy), r=[Bkr96], w=[Bkr])
                        op("pe", lambda e: e.matmul(pr[0:96, :], lhsT=rfT_b[:], rhs=krT[:, sl], start=True, stop=True), r=[Bmc, Bkr], w=[prb])
                        t2, t2b = rsp.get()
                        op("dve", lambda e: e.tensor_tensor(out=t2[0:96, :], in0=pr[0:96, :], in1=ropeS[:, sl], op=ALU.mult), r=[prb, Bmc], w=[t2b])
                        op("pool", lambda e: e.tensor_tensor(out=kr96[:, sl], in0=kr96[:, sl], in1=ropeC[:, sl], op=ALU.mult), r=[Bkr96, Bmc], w=[Bkr96])
                        op("pool", lambda e: e.tensor_tensor(out=kr96[:, sl], in0=kr96[:, sl], in1=t2[0:96, :], op=ALU.add), r=[Bkr96, t2b], w=[Bkr96])
                        op("act", lambda e: e.activation(out=krT[:, sl], in_=kr96[:, sl], func=AF.Copy), r=[Bkr96, prb], w=[Bkr])
                        op("dve", lambda e: e.tensor_copy(out=krow[0:32, sl], in_=kr96[64:96, sl]), r=[Bkr96], w=[Bkrow])
                    for tb in range(16 if SUB >= 4 else 0):
                        bs = slice(tb * 128, (tb + 1) * 128)
                        kb.barrier()
                        pt_, ptb = kb.bank()
                        pt = pt_[:].bitcast(BF16)
                        for c in range(2):
                            op("pe", lambda e: e.transpose(pt[:, c * 128:(c + 1) * 128], ckvT[:, c, bs], ident_b[:]), r=[Bckv, Bc], w=[ptb], inc=False)
                        op("pe", lambda e: e.transpose(pt[:, 256:288], krow[0:32, bs], ident_b[0:32, 0:32]), r=[Bkrow, Bc], w=[ptb])
                        st, stb = stg.get()
                        op("act", lambda e: e.activation(out=st[:], in_=pt[:, 0:256], func=AF.Copy), r=[ptb], w=[stb])
                        if os.environ.get("NODMA", "0") in ("0", "2"):
                            dma("sp", O["ckv_p"][l, bs, :], st[:], r=[stb], is_out=True)
                        st2, st2b = stg.get()
                        op("dve", lambda e: e.tensor_copy(out=st2[:, 0:32], in_=pt[:, 256:288]), r=[ptb], w=[st2b])
                        if os.environ.get("NODMA", "0") in ("0", "3"):
                            dma("pool", O["krope_p"][l, bs, :], st2[:, 0:32], r=[st2b], is_out=True)
                    if SUB < 5:
                        kb.barrier()
                        kb.es = es4
                        break
                    fm_rms([ukvs[:, c, :] for c in range(2)], [Bsm], KVL, V_KVG, [[ckvs_f[:, c, :], ckvs_b[:, c, :]] for c in range(2)], [Bsm, Bsm], NS)
                    pr, prb = kb.bank()
                    op("pe", lambda e: e.matmul(pr[0:96, 0:NS], lhsT=rfT_f[:], rhs=krs[:, 0, :], start=True, stop=True), r=[Bmc, Bsm], w=[prb])
                    op("dve", lambda e: e.tensor_scalar(out=krs[:, 1, :], in0=pr[0:96, 0:NS], scalar1=ropes[:, 1:2], scalar2=None, op0=ALU.mult), r=[prb, Bmc], w=[Bsm])
                    op("dve", lambda e: e.scalar_tensor_tensor(out=krs[:, 2, :], in0=krs[:, 0, :], scalar=ropes[:, 0:1], in1=krs[:, 1, :], op0=ALU.mult, op1=ALU.add), r=[Bsm, Bmc], w=[Bsm])
                    op("dve", lambda e: e.tensor_copy(out=krs_b[:], in_=krs[:, 2, :]), r=[Bsm], w=[Bsm])
                    for c in range(2):
                        dma("sp", O["ckv_s"][l, :, c * 128:(c + 1) * 128].rearrange("s p -> p s"), ckvs_f[:, c, :], r=[Bsm], is_out=True)
                    op("dve", lambda e: e.tensor_copy(out=krs_o[:], in_=krs[64:96, 2, :]), r=[Bsm], w=[Bkrow])
                    dma("sp", O["krope_s"][l].rearrange("s p -> p s"), krs_o[:], r=[Bkrow], is_out=True)
                    kb.barrier()
                kb.es = es4
                cqT = kb.sb("cqT", [128, 3, T], BF16)
                with ExitStack() as es4b:
                    kb.es = es4b
                    sqp = Rot(kb, "sqp", [128, 3, 512], BF16, 2)
                    rsp = Rot(kb, "rsp", [128, 512], F32, 2)
                    uq = kb.sb("uq", [128, 3, T], F32)
                    Buq = Buf("uq")
                    for c in range(3):
                        def ev(tg, pt, pbuf, c=c):
                            op("act", lambda e: e.activation(out=uq[:, c, tg * 512:(tg + 1) * 512], in_=pt[:], func=AF.Copy), r=[pbuf], w=[Buq])

                        def evs(pt, pbuf, c=c):
                            op("act", lambda e: e.activation(out=uqs[:, c, :], in_=pt[:, 0:NS], func=AF.Copy), r=[pbuf], w=[Bsm])
                        inproj(l, CI_MQ + c, hT, hbuf, 128, ev, evs)
                    for tg in range(4):
                        sl = slice(tg * 512, (tg + 1) * 512)
                        fm_rms([uq[:, c, sl] for c in range(3)], [Buq], 384, V_QG, [[cqT[:, c, sl]] for c in range(3)], [Bcq], 512)
                    fm_rms([uqs[:, c, :] for c in range(3)], [Bsm], 384, V_QG, [[cqs_b[:, c, :]] for c in range(3)], [Bsm], NS)
                    kb.barrier()
                kb.es = es4
                if stage < 4:
                    kb.barrier()
                    continue
                wuq_b = kb.sb("wuq_b", [128, 3, 768], BF16)
                wukv_b = kb.sb("wukv_b", [128, 2, 1536], BF16)
                es4c = ExitStack()
                kb.es = es4c
                ropeC = kb.sb("ropeCb", [96, T], BF16)
                ropeS = kb.sb("ropeSb", [96, T], BF16)
                masks = kb.sb("masks", [128, 4, 512], BF16)
                dma("pool", ropeC[:], I["ropeC"], w=[Bmc])
                dma("pool", ropeS[:], I["ropeS"], w=[Bmc])
                dma("pool", masks[:], I["masks"].rearrange("o p j -> p o j"), w=[Bmc])
                dma("pool", wuq_b[:], I["wuq"][l].rearrange("(k p) n -> p k n", p=128), w=[Bmc])
                dma("pool", wukv_b[:], I["wukv"][l].rearrange("(k p) n -> p k n", p=128), w=[Bmc])
                hp = Rot(kb, "mlah", [128, T], BF16, 3)
                vp = Rot(kb, "mlav", [128, 16, 128], BF16, 1)
                ptp = Rot(kb, "mlapt", [128, 512], BF16, 3)
                t32 = Rot(kb, "mlat32", [128, 512], F32, 3)
                mx = kb.sb("mx", [128, 12], F32)
                Bmx = Buf("mx")
                pO, pOb = kb.acc_bank(0)
                pD, pDb = kb.acc_bank(1)
                for h in range(H):
                    q_raw, q_rawb = hp.get()
                    Qh, Qhb = hp.get()
                    Kh, Khb = hp.get()
                    Vh, Vhb = vp.get()
                    for tg in range(4):
                        sl = slice(tg * 512, (tg + 1) * 512)
                        pq, pqb = kb.bank()
                        for c in range(3):
                            op("pe", lambda e: e.matmul(pq[0:96, :], lhsT=wuq_b[:, c, h * 96:(h + 1) * 96], rhs=cqT[:, c, sl], start=(c == 0), stop=(c == 2)), r=[Bmc, Bcq], w=[pqb], inc=(c == 2))
                        op("act", lambda e: e.activation(out=q_raw[0:96, sl], in_=pq[0:96, :], func=AF.Copy), r=[pqb], w=[q_rawb])
                        pr, prb = kb.bank()
                        op("pe", lambda e: e.matmul(pr[0:96, :], lhsT=rfT_b[:], rhs=q_raw[0:96, sl], start=True, stop=True), r=[Bmc, q_rawb], w=[prb])
                        ta, tab = t32.get()
                        tb_, tbb = t32.get()
                        op("pool", lambda e: e.tensor_tensor(out=ta[0:96, :], in0=q_raw[0:96, sl], in1=ropeC[:, sl], op=ALU.mult), r=[q_rawb, Bmc], w=[tab])
                        op("dve", lambda e: e.tensor_tensor(out=tb_[0:96, :], in0=pr[0:96, :], in1=ropeS[:, sl], op=ALU.mult), r=[prb, Bmc], w=[tbb])
                        op("pool", lambda e: e.tensor_tensor(out=Qh[0:96, sl], in0=ta[0:96, :], in1=tb_[0:96, :], op=ALU.add), r=[tab, tbb], w=[Qhb])
                        pk, pkb = kb.bank()
                        for c in range(2):
                            op("pe", lambda e: e.matmul(pk[0:64, :], lhsT=wukv_b[:, c, h * 192:h * 192 + 64], rhs=ckvT[:, c, sl], start=(c == 0), stop=(c == 1)), r=[Bmc, Bckv], w=[pkb], inc=(c == 1))
                        op("act", lambda e: e.activation(out=Kh[0:64, sl], in_=pk[0:64, :], func=AF.Copy), r=[pkb], w=[Khb])
                        pv, pvb = kb.bank()
                        for i in range(4):
                            tb = tg * 4 + i
                            for c in range(2):
                                op("pe", lambda e: e.matmul(pv[:, i * 128:(i + 1) * 128], lhsT=ckvT[:, c, tb * 128:(tb + 1) * 128], rhs=wukv_b[:, c, h * 192 + 64:h * 192 + 192], start=(c == 0), stop=(c == 1)),
                                   r=[Bmc, Bckv], w=[pvb], inc=(i == 3 and c == 1))
                        op("dve", lambda e: e.tensor_copy(out=Vh[:, tg * 4:tg * 4 + 4, :].rearrange("p a b -> p (a b)"), in_=pv[:]), r=[pvb], w=[Vhb])
                    op("pool", lambda e: e.tensor_copy(out=Kh[64:96, :], in_=krT[64:96, :]), r=[Bkr], w=[Khb])
                    for qi, (src, srcb) in enumerate(((Qh, Qhb), (Kh, Khb))):
                        for tg in range(4):
                            sl = slice(tg * 512, (tg + 1) * 512)
                            sq, sqb = ptp.get()
                            op("pool", lambda e: e.tensor_tensor(out=sq[0:96, :], in0=src[0:96, sl], in1=src[0:96, sl], op=ALU.mult), r=[srcb], w=[sqb])
                            pn, pnb = kb.bank()
                            op("pe", lambda e: e.matmul(pn[:], lhsT=ones_b[0:96, :], rhs=sq[0:96, :], start=True, stop=True), r=[sqb, Bc], w=[pnb])
                            op("dve", lambda e: e.tensor_reduce(out=mx[:, qi * 4 + tg:qi * 4 + tg + 1], in_=pn[:], op=ALU.max, axis=AX.X), r=[pnb, Bmx], w=[Bmx])
                        op("dve", lambda e: e.tensor_reduce(out=mx[:, 8 + qi:9 + qi], in_=mx[:, qi * 4:qi * 4 + 4], op=ALU.max, axis=AX.X), r=[Bmx], w=[Bmx])
                    op("dve", lambda e: e.tensor_tensor(out=mx[:, 10:11], in0=mx[:, 8:9], in1=mx[:, 9:10], op=ALU.mult), r=[Bmx], w=[Bmx])
                    op("act", lambda e: e.activation(out=mx[:, 10:11], in_=mx[:, 10:11], func=AF.Sqrt), r=[Bmx], w=[Bmx])
                    op("dve", lambda e: e.tensor_scalar(out=mx[:, 11:12], in0=mx[:, 10:11], scalar1=-MLA_SCALE, scalar2=None, op0=ALU.mult), r=[Bmx], w=[Bmx])
                    for g in range(4):
                        qs = slice(g * 512, (g + 1) * 512)
                        nkb = 4 * (g + 1)
                        for kbi in range(nkb):
                            ps, psb = kb.bank()
                            op("pe", lambda e: e.matmul(ps[:], lhsT=Kh[0:96, kbi * 128:(kbi + 1) * 128], rhs=Qh[0:96, qs], start=True, stop=True), r=[Khb, Qhb], w=[psb])
                            pt, ptb = ptp.get()
                            op("act", lambda e: e.activation(out=pt[:], in_=ps[:], func=AF.Exp, scale=MLA_SCALE, bias=mx[:, 11:12]), r=[psb, Bmx], w=[ptb])
                            if kbi >= 4 * g:
                                o_ = kbi - 4 * g
                                op("pool", lambda e: e.tensor_tensor(out=pt[:], in0=pt[:], in1=masks[:, o_, :], op=ALU.mult), r=[ptb, Bmc], w=[ptb])
                            op("pe", lambda e: e.matmul(pO[:], lhsT=Vh[:, kbi, :], rhs=pt[:], start=(kbi == 0), stop=(kbi == nkb - 1)), r=[Vhb, ptb], w=[pOb], inc=False)
                            op("pe", lambda e: e.matmul(pD[:], lhsT=ones_b[:], rhs=pt[:], start=(kbi == 0), stop=(kbi == nkb - 1)), r=[Bc, ptb], w=[pDb], inc=True)
                        rd, rdb = t32.get()
                        op("act", lambda e: e.activation(out=rd[:], in_=pD[:], func=AF.Ln), r=[pDb], w=[rdb])
                        op("act", lambda e: e.activation(out=rd[:], in_=rd[:], func=AF.Exp, scale=-1.0), r=[rdb], w=[rdb])
                        op("dve", lambda e: e.tensor_tensor(out=oT[:, h, qs], in0=pO[:], in1=rd[:], op=ALU.mult), r=[pOb, pDb, rdb], w=[oTb[h]])
                kb.barrier()
                es4c.close()
                kb.es = es4
                if stage >= 50:
                    NPG = NPAGE
                    ptb_i = kb.sb("ptb_i", [128, NS * NPG], I32)
                    idx_all = kb.sb("idx_all", [128, NS * NPG], I32)
                    iota_c = kb.sb("iota_c", [128, 1], F32)
                    wukT_b = kb.sb("wukT_b", [64, 8, KVL], BF16)
                    Bsa = Buf("sa_const")
                    dma("sp", ptb_i[:], I["ptab"].rearrange("s g -> (s g)").partition_broadcast(128), w=[Bsa])
                    dma("sp", iota_c[:], I["iota"], w=[Bsa])
                    dma("pool", wukT_b[:], I["wukT"][l].rearrange("h d c -> d h c"), w=[Bsa])
                    op("dve", lambda e: e.tensor_scalar(out=iota_c[:], in0=iota_c[:], scalar1=float(l * 5120 * 128), scalar2=None, op0=ALU.add), r=[Bsa], w=[Bsa])
                    op("dve", lambda e: e.tensor_scalar(out=idx_all[:], in0=ptb_i[:], scalar1=128.0, scalar2=iota_c[:, 0:1], op0=ALU.mult, op1=ALU.add), r=[Bsa], w=[Bsa])
                    qs_f = kb.sb("qs_f", [96, 3, H * NS], F32)
                    qs_b = kb.sb("qs_b", [96, H * NS], BF16)
                    qn_b = kb.sb("qn_b", [64, H * NS], BF16)
                    qrope0 = kb.sb("qrope0", [32, NS, H], BF16)
                    qlatT = kb.sb("qlatT", [128, 2, NS, H], BF16)
                    Bq = Buf("qs")
                    pq, pqb = kb.bank()
                    for h in range(H):
                        for c in range(3):
                            op("pe", lambda e: e.matmul(pq[0:96, h * NS:(h + 1) * NS], lhsT=wuq_b[:, c, h * 96:(h + 1) * 96], rhs=cqs_b[:, c, :], start=(c == 0), stop=(c == 2)), r=[Bmc, Bsm], w=[pqb], inc=(h == H - 1 and c == 2))
                    op("act", lambda e: e.activation(out=qs_f[:, 0, :], in_=pq[0:96, 0:H * NS], func=AF.Copy), r=[pqb], w=[Bq])
                    op("dve", lambda e: e.tensor_copy(out=qs_b[:], in_=qs_f[:, 0, :]), r=[Bq], w=[Bq])
                    pr, prb = kb.bank()
                    op("pe", lambda e: e.matmul(pr[0:96, 0:H * NS], lhsT=rfT_b[:], rhs=qs_b[:], start=True, stop=True), r=[Bmc, Bq], w=[prb])
                    op("dve", lambda e: e.tensor_scalar(out=qs_f[:, 1, :], in0=pr[0:96, 0:H * NS], scalar1=ropes[:, 1:2], scalar2=None, op0=ALU.mult), r=[prb, Bmc], w=[Bq])
                    op("dve", lambda e: e.scalar_tensor_tensor(out=qs_f[:, 2, :], in0=qs_f[:, 0, :], scalar=ropes[:, 0:1], in1=qs_f[:, 1, :], op0=ALU.mult, op1=ALU.add), r=[Bq, Bmc], w=[Bq])
                    op("dve", lambda e: e.tensor_copy(out=qn_b[:], in_=qs_f[0:64, 2, :]), r=[Bq], w=[Bq])
                    op("dve", lambda e: e.tensor_copy(out=qrope0[:].rearrange("p s h -> p h s"), in_=qs_f[64:96, 2, :].rearrange("p (h s) -> p h s", s=NS)), r=[Bq], w=[Bq])
                    qs_b_r = qrope0
                    krs_b0 = kb.sb("krs_b0", [32, NS], BF16)
                    op("dve", lambda e: e.tensor_copy(out=krs_b0[:], in_=krs[64:96, 2, :]), r=[Bsm], w=[Bq])
                    pl, plb = kb.bank()
                    for cc in range(2):
                        for h in range(H):
                            op("pe", lambda e: e.matmul(pl[:, (cc * H + h) * NS:(cc * H + h + 1) * NS], lhsT=wukT_b[:, h, cc * 128:(cc + 1) * 128], rhs=qn_b[:, h * NS:(h + 1) * NS], start=True, stop=True),
                               r=[Bsa, Bq], w=[plb], inc=(cc == 1 and h == H - 1))
                    op("dve", lambda e: e.tensor_copy(out=qlatT[:].rearrange("p c s h -> p c h s"), in_=pl[:, 0:2 * H * NS].rearrange("p (c h s) -> p c h s", c=2, h=H)), r=[plb], w=[Bq])
                    ckp = Rot(kb, "ckp", [128, 292], BF16, 4)
                    cktp = Rot(kb, "cktp", [128, 384], BF16, 3)
                    ckv_flat = I["cache_ckv"].rearrange("l n t c -> (l n t) c")
                    kr_flat = I["cache_krope"].rearrange("l n t c -> (l n t) c")
                    for i_ in range(4):
                        op("pool", lambda e: e.memset(ckp.t[i_][:, 256:257], 1.0), w=[ckp.b[i_]])
                    sT = kb.sb("sT", [128, 2, 512], F32)
                    PT_s = kb.sb("PT_s", [128, 2, 512], BF16)
                    BsT = Buf("sT")
                    sm = kb.sb("sm", [128, 40], F32)
                    Bsmx = Buf("smx")
                    rowp = kb.sb("rowp", [128, 272], BF16)
                    op("pool", lambda e: e.memset(rowp[:], 0.0), w=[Bsmx])
                    olat = kb.sb("olat", [8, 264], F32)
                    olT = kb.sb("olT", [128, 2, H], BF16)
                    accS0, accS0b = kb.acc_bank(0)
                    accS1, accS1b = kb.acc_bank(1)
                    SA = int(os.environ.get("SA_SUB", "9"))
                    for s_ in range(NS if SA >= 3 else 1):
                        for pg in range(NPG if SA >= 2 else 2):
                            ck, ckb = ckp.get()
                            icol = idx_all[:, s_ * NPG + pg:s_ * NPG + pg + 1]
                            kb.dma_fn("pool", lambda e: e.indirect_dma_start(out=ck[:, 0:256], out_offset=None, in_=ckv_flat, in_offset=bass.IndirectOffsetOnAxis(ap=icol, axis=0)), r=[Bsa], w=[ckb])
                            kb.dma_fn("pool", lambda e: e.indirect_dma_start(out=ck[:, 258:290], out_offset=None, in_=kr_flat, in_offset=bass.IndirectOffsetOnAxis(ap=icol, axis=0)), r=[Bsa], extra_w=[ckb])
                            pt, ptb = kb.bank()
                            ptv = pt[:].bitcast(BF16)
                            op("pe", lambda e: e.transpose(ptv[:, 0:128], ck[:, 0:128], ident_b[:]), r=[ckb, Bc], w=[ptb], inc=False)
                            op("pe", lambda e: e.transpose(ptv[:, 128:256], ck[:, 128:256], ident_b[:]), r=[ckb, Bc], w=[ptb], inc=False)
                            op("pe", lambda e: e.transpose(ptv[0:32, 256:384], ck[:, 258:290], ident_b[:]), r=[ckb, Bc], w=[ptb])
                            ckt, cktb = cktp.get()
                            op("act", lambda e: e.activation(out=ckt[:, 0:256], in_=ptv[:, 0:256], func=AF.Copy), r=[ptb], w=[cktb])
                            op("dve", lambda e: e.tensor_copy(out=ckt[0:32, 256:384], in_=ptv[0:32, 256:384]), r=[ptb], w=[cktb])
                            acc, accb = (accS0, accS0b) if pg < 64 else (accS1, accS1b)
                            cs_ = (pg % 64) * 8
                            for cc in range(2):
                                op("pe", lambda e: e.matmul(acc[:, cs_:cs_ + 8], lhsT=ckt[:, cc * 128:(cc + 1) * 128], rhs=qlatT[:, cc, s_, :], start=(cc == 0), stop=False), r=[cktb, Bq], w=[accb], inc=False)
                            op("pe", lambda e: e.matmul(acc[:, cs_:cs_ + 8], lhsT=ckt[0:32, 256:384], rhs=qrope0[:, s_, :], start=False, stop=True), r=[cktb, Bq], w=[accb], inc=True)
                        if SA < 4:
                            continue
                        op("act", lambda e: e.activation(out=sT[:, 0, :], in_=accS0[:], func=AF.Copy), r=[accS0b], w=[BsT])
                        op("act", lambda e: e.activation(out=sT[:, 1, :], in_=accS1[:], func=AF.Copy), r=[accS1b], w=[BsT])
                        op("dve", lambda e: e.tensor_reduce(out=sm[:, 0:8], in_=sT[:].rearrange("p b (g h) -> p h b g", h=8), op=ALU.max, axis=AX.XY), r=[BsT, Bsmx], w=[Bsmx])
                        p1, p1b = kb.bank()
                        op("pe", lambda e: e.transpose(p1[0:8, 0:128], sm[:, 0:8], ident_f[:]), r=[Bsmx, Bc], w=[p1b])
                        op("dve", lambda e: e.tensor_reduce(out=sm[0:8, 8:9], in_=p1[0:8, 0:128], op=ALU.max, axis=AX.X), r=[p1b, Bsmx], w=[Bsmx])
                        p2, p2b = kb.bank()
                        for cc in range(2):
                            op("pe", lambda e: e.matmul(p2[0:8, 0:1], lhsT=qlatT[:, cc, s_, :], rhs=ckvs_b[:, cc, s_:s_ + 1], start=(cc == 0), stop=False), r=[Bq, Bsm], w=[p2b], inc=False)
                        op("pe", lambda e: e.matmul(p2[0:8, 0:1], lhsT=qs_b_r[:, s_, :], rhs=krs_b0[:, s_:s_ + 1], start=False, stop=True), r=[Bq, Bsm], w=[p2b])
                        op("dve", lambda e: e.tensor_copy(out=sm[0:8, 9:10], in_=p2[0:8, 0:1]), r=[p2b, Bsmx], w=[Bsmx])
                        op("dve", lambda e: e.tensor_tensor(out=sm[0:8, 10:11], in0=sm[0:8, 8:9], in1=sm[0:8, 9:10], op=ALU.max), r=[Bsmx], w=[Bsmx])
                        op("dve", lambda e: e.tensor_tensor(out=sm[0:8, 11:12], in0=sm[0:8, 9:10], in1=sm[0:8, 10:11], op=ALU.subtract), r=[Bsmx], w=[Bsmx])
                        op("act", lambda e: e.activation(out=sm[0:8, 11:12], in_=sm[0:8, 11:12], func=AF.Exp, scale=MLA_SCALE), r=[Bsmx], w=[Bsmx])
                        op("dve", lambda e: e.tensor_scalar(out=sm[0:8, 16:24], in0=ident_f[0:8, 0:8], scalar1=sm[0:8, 10:11], scalar2=None, op0=ALU.mult), r=[Bsmx, Bc], w=[Bsmx])
                        p3, p3b = kb.bank()
                        op("pe", lambda e: e.matmul(p3[:, 0:8], lhsT=ones_f[0:8, :], rhs=sm[0:8, 16:24], start=True, stop=True), r=[Bsmx, Bc], w=[p3b])
                        op("dve", lambda e: e.tensor_copy(out=sm[:, 24:32], in_=p3[:, 0:8]), r=[p3b, Bsmx], w=[Bsmx])
                        op("dve", lambda e: e.tensor_tensor(out=sT[:].rearrange("p b (g h) -> p (b g) h", h=8), in0=sT[:].rearrange("p b (g h) -> p (b g) h", h=8),
                                                            in1=sm[:, 24:32].unsqueeze(1).to_broadcast([128, NPG, 8]), op=ALU.subtract), r=[BsT, Bsmx], w=[BsT])
                        op("act", lambda e: e.activation(out=PT_s[:], in_=sT[:], func=AF.Exp, scale=MLA_SCALE), r=[BsT], w=[BsT])
                        p4, p4b = kb.bank()
                        for cc in range(2):
                            op("pe", lambda e: e.transpose(p4[0:1, cc * 128:(cc + 1) * 128], ckvs_f[:, cc, s_:s_ + 1], ident_f[:]), r=[Bsm, Bc], w=[p4b], inc=False)
                        op("pe", lambda e: e.transpose(p4[0:1, 264:272], sm[0:8, 11:12], ident_f[0:8, 0:8]), r=[Bsmx, Bc], w=[p4b])
                        op("dve", lambda e: e.tensor_copy(out=rowp[0:1, 0:256], in_=p4[0:1, 0:256]), r=[p4b, Bsmx], w=[Bsmx])
                        op("pool", lambda e: e.memset(rowp[0:1, 256:257], 1.0), w=[Bsmx])
                        op("dve", lambda e: e.tensor_copy(out=rowp[0:1, 264:272], in_=p4[0:1, 264:272]), r=[p4b, Bsmx], w=[Bsmx])
                        if SA < 5:
                            continue
                        for pg in range(NPG):
                            ck, ckb = ckp.get()
                            icol = idx_all[:, s_ * NPG + pg:s_ * NPG + pg + 1]
                            kb.dma_fn("pool", lambda e: e.indirect_dma_start(out=ck[:, 0:256], out_offset=None, in_=ckv_flat, in_offset=bass.IndirectOffsetOnAxis(ap=icol, axis=0)), r=[Bsa], w=[ckb])
                            op("pe", lambda e: e.matmul(accS0[0:8, 0:257], lhsT=PT_s[:, pg // 64, (pg % 64) * 8:(pg % 64) * 8 + 8], rhs=ck[:, 0:257], start=(pg == 0), stop=False), r=[ckb, BsT], w=[accS0b], inc=True)
                        op("pe", lambda e: e.matmul(accS0[0:8, 0:257], lhsT=rowp[:, 264:272], rhs=rowp[:, 0:257], start=False, stop=True), r=[Bsmx], w=[accS0b])
                        op("dve", lambda e: e.reciprocal(out=sm[0:8, 12:13], in_=accS0[0:8, 256:257]), r=[accS0b, Bsmx], w=[Bsmx])
                        op("dve", lambda e: e.tensor_scalar(out=olat[:, 0:256], in0=accS0[0:8, 0:256], scalar1=sm[0:8, 12:13], scalar2=None, op0=ALU.mult), r=[accS0b, Bsmx], w=[Bsmx])
                        p5, p5b = kb.bank()
                        for cc in range(2):
                            op("pe", lambda e: e.transpose(p5[:, cc * 8:(cc + 1) * 8], olat[:, cc * 128:(cc + 1) * 128], ident_f[0:8, 0:8]), r=[Bsmx, Bc], w=[p5b], inc=(cc == 1))
                        op("dve", lambda e: e.tensor_copy(out=olT[:].rearrange("p c h -> p (c h)"), in_=p5[:, 0:16]), r=[p5b, Bsmx], w=[Bsmx])
                        p6, p6b = kb.bank()
                        for h in range(H):
                            for cc in range(2):
                                op("pe", lambda e: e.matmul(p6[:, h * 8:(h + 1) * 8], lhsT=wukv_b[:, cc, h * 192 + 64:h * 192 + 192], rhs=olT[:, cc, :], start=(cc == 0), stop=(cc == 1)), r=[Bmc, Bsmx], w=[p6b], inc=(h == H - 1 and cc == 1))
                        op("dve", lambda e: e.tensor_copy(out=osT[:, 0:H, s_], in_=p6[:, 0:72:9]), r=[p6b], w=[Bos])
                kb.barrier()
            kb.es = es
            if stage < 5:
                continue
            branch_proj(l, "wmlp", 8, CI_GATE + 16, False)
            kb.barrier()

            with ExitStack() as es5:
                kb.es = es5
                wo_b = kb.sb("wo_b", [128, 8, D], BF16)
                Bwo = Buf("wo")
                dma("pool", wo_b[:], I["wo"][l].rearrange("(k p) n -> p k n", p=128), w=[Bwo])
                gtb = kb.sb("gt1bc", [128, D], F32)
                Bgt = Buf("gt1bc")
                tmpbc = kb.sb("tmpbc5", [128, 128], F32)
                make_gtbc(l, 16, gtb, Bgt, tmpbc, Buf("tmpbc5"))
                xin = Rot(kb, "xin5", [128, D], F32, 2)
                x1p = Rot(kb, "x1p5", [128, D], F32, 2)
                junkp = Rot(kb, "junk5", [128, D], F32, 1)
                xnp = Rot(kb, "xn5", [128, D], BF16, 1)
                tmps = Rot(kb, "tmps5", [128, 8, NS], F32, 3)
                ms_b = kb.sb("ms_b", [128, 8, NS], BF16)
                for tb in range(16):
                    bs = slice(tb * 128, (tb + 1) * 128)
                    xt, bxt = xin.get()
                    dma("sp", xt[:], xsrc[bs, :], w=[bxt])
                    x1t, x1b = x1p.get()
                    for half in range(2):
                        pw, pwb = kb.bank()
                        for k in range(8):
                            op("pe", lambda e: e.matmul(pw[:], lhsT=mT[:, k, bs], rhs=wo_b[:, k, half * 512:(half + 1) * 512], start=(k == 0), stop=(k == 7)), r=[mTb[k], Bwo], w=[pwb], inc=(k == 7))
                        op("dve", lambda e: e.tensor_tensor(out=x1t[:, half * 512:(half + 1) * 512], in0=pw[:], in1=gtb[:, half * 512:(half + 1) * 512], op=ALU.mult), r=[pwb, Bgt], w=[x1b])
                    op("pool", lambda e: e.tensor_tensor(out=x1t[:], in0=x1t[:], in1=xt[:], op=ALU.add), r=[x1b, bxt], w=[x1b])
                    dma("pool", xa[bs, :], x1t[:], r=[x1b])
                    norm_block(x1t, x1b, A2[l], 24, l, hT, hbuf, tb, junkp, xnp)
                op("dve", lambda e: e.tensor_copy(out=ms_b[:], in_=msT[:]), r=[Bms], w=[Bms])
                pp, ppb = kb.bank()
                for j in range(8):
                    for k in range(8):
                        op("pe", lambda e: e.matmul(pp[:, j * NS:(j + 1) * NS], lhsT=wo_b[:, k, j * 128:(j + 1) * 128], rhs=ms_b[:, k, :], start=(k == 0), stop=(k == 7)), r=[Bwo, Bms], w=[ppb], inc=(k == 7))
                tq, tqb = tmps.get()
                op("dve", lambda e: e.tensor_tensor(out=tq[:], in0=pp[:, 0:8 * NS].rearrange("p (j s) -> p j s", s=NS), in1=modT[l][:, 16:24, 1:5], op=ALU.mult), r=[ppb, Bmod[l]], w=[tqb])
                op("dve", lambda e: e.tensor_tensor(out=xsT[:], in0=xsT[:], in1=tq[:], op=ALU.add), r=[tqb, Bxs], w=[Bxs])
                sample_norm(l, As2[l], 24, tmps)
                kb.barrier()
            kb.es = es

            with ExitStack() as es6:
                kb.es = es6
                print("sbuf remaining before FFN", nc.sbuf_bytes_remaining)
                last = (l == L - 1)
                wffo_b = kb.sb("wffo_b", [128, 22, D], BF16)
                Bwf = Buf("wffo")
                for f0 in range(0, 22, 2):
                    dma("pool", wffo_b[:, f0:f0 + 2, :], I["wffo"][l, f0 * 128:(f0 + 2) * 128, :].rearrange("(f p) n -> p f n", p=128), w=[Bwf])
                gtb = kb.sb("gt2bc", [128, D], F32)
                Bgt = Buf("gt2bc")
                tmpbc = kb.sb("tmpbc6", [128, 128], F32)
                Btmpbc = Buf("tmpbc6")
                make_gtbc(l, 40, gtb, Bgt, tmpbc, Btmpbc)
                xin = Rot(kb, "xin6", [128, D], F32, 2)
                junkp = Rot(kb, "junk6", [128, D], F32, 1)
                xnp = Rot(kb, "xn6", [128, D], BF16, 1)
                sgq = Rot(kb, "sg6", [128, 512], F32, 2)
                tmps = Rot(kb, "tmps6", [128, 8, NS], F32, 3)
                acts = kb.sb("acts", [128, 22, NS], BF16)
                Bacts = Buf("acts")
                st6 = kb.sb("st6", [128, 4], F32)
                Bst6 = Buf("st6")
                if last:
                    gfb = kb.sb("gfbc", [128, D], F32)
                    Bgf = Buf("gfbc")
                    pg0, pgb0 = kb.bank()
                    pg1, pgb1 = kb.bank()
                    for c in range(8):
                        op("dve", lambda e: e.tensor_scalar(out=tmpbc[:], in0=ones_f[:], scalar1=vecT[0][:, V_GF + c:V_GF + c + 1], scalar2=None, op0=ALU.mult), r=[Bmod[0], Bc, Btmpbc], w=[Btmpbc])
                        pg, pgb = (pg0, pgb0) if c < 4 else (pg1, pgb1)
                        op("pe", lambda e: e.matmul(pg[:, (c % 4) * 128:(c % 4 + 1) * 128], lhsT=tmpbc[:], rhs=ident_f[:], start=True, stop=True), r=[Btmpbc, Bc], w=[pgb])
                    op("act", lambda e: e.activation(out=gfb[:, 0:512], in_=pg0[:], func=AF.Copy), r=[pgb0], w=[Bgf])
                    op("act", lambda e: e.activation(out=gfb[:, 512:1024], in_=pg1[:], func=AF.Copy), r=[pgb1], w=[Bgf])
                oTv = oT[:].rearrange("p n (a t) -> p (n a) t", a=2)
                mTv = mT[:].rearrange("p n (a t) -> p (n a) t", a=2)
                actv = [oTv[:, f, :] if f < 20 else mTv[:, f - 20, :] for f in range(22)]
                actb = [Buf("act%d" % f) for f in range(22)]
                for th in range(2):
                    for f in range(22):
                        gw, gwb = wpiece.get()
                        dma("pool", gw[:], I["wffi"][l, 2 * f], w=[gwb])
                        uw, uwb = wpiece.get()
                        dma("pool", uw[:], I["wffi"][l, 2 * f + 1], w=[uwb])
                        for t2 in range(2):
                            tg = th * 2 + t2
                            sl = slice(tg * 512, (tg + 1) * 512)
                            pg, pgb = kb.bank()
                            for k in range(8):
                                op("pe", lambda e: e.matmul(pg[:], lhsT=gw[:, k, :], rhs=hT[:, k, sl], start=(k == 0), stop=(k == 7)), r=[gwb] + hbuf[tg * 4:tg * 4 + 4], w=[pgb], inc=(k == 7))
                            pu, pub = kb.bank()
                            for k in range(8):
                                op("pe", lambda e: e.matmul(pu[:], lhsT=uw[:, k, :], rhs=hT[:, k, sl], start=(k == 0), stop=(k == 7)), r=[uwb] + hbuf[tg * 4:tg * 4 + 4], w=[pub], inc=(k == 7))
                            sg, sgb = sgq.get()
                            op("act", lambda e: e.activation(out=sg[:], in_=pg[:], func=AF.Silu), r=[pgb], w=[sgb])
                            op("dve", lambda e: e.tensor_tensor(out=actv[f][:, t2 * 512:(t2 + 1) * 512], in0=pu[:], in1=sg[:], op=ALU.mult), r=[pub, sgb], w=[actb[f]])
                        if th == 0:
                            pp, ppb = kb.bank()
                            for k in range(8):
                                op("pe", lambda e: e.matmul(pp[:, 0:NS], lhsT=gw[:, k, :], rhs=hsT[:, k, :], start=(k == 0), stop=(k == 7)), r=[gwb, Bhs], w=[ppb], inc=False)
                            for k in range(8):
                                op("pe", lambda e: e.matmul(pp[:, NS:2 * NS], lhsT=uw[:, k, :], rhs=hsT[:, k, :], start=(k == 0), stop=(k == 7)), r=[uwb, Bhs], w=[ppb], inc=(k == 7))
                            sg, sgb = sgq.get()
                            op("act", lambda e: e.activation(out=sg[:, 0:NS], in_=pp[:, 0:NS], func=AF.Silu), r=[ppb], w=[sgb])
                            op("dve", lambda e: e.tensor_tensor(out=acts[:, f, :], in0=pp[:, NS:2 * NS], in1=sg[:, 0:NS], op=ALU.mult), r=[ppb, sgb], w=[Bacts])
                    for t8 in range(8):
                        tb = th * 8 + t8
                        bs = slice(tb * 128, (tb + 1) * 128)
                        xt, bxt = xin.get()
                        dma("sp", xt[:], xa[bs, :], w=[bxt])
                        x2t, x2b = xin.get()
                        for half in range(2):
                            pw, pwb = kb.bank()
                            for f in range(22):
                                op("pe", lambda e: e.matmul(pw[:], lhsT=actv[f][:, t8 * 128:(t8 + 1) * 128], rhs=wffo_b[:, f, half * 512:(half + 1) * 512], start=(f == 0), stop=(f == 21)), r=[actb[f], Bwf], w=[pwb], inc=(f == 21))
                            op("dve", lambda e: e.tensor_tensor(out=x2t[:, half * 512:(half + 1) * 512], in0=pw[:], in1=gtb[:, half * 512:(half + 1) * 512], op=ALU.mult), r=[pwb, Bgt], w=[x2b])
                        op("pool", lambda e: e.tensor_tensor(out=x2t[:], in0=x2t[:], in1=xt[:], op=ALU.add), r=[x2b, bxt], w=[x2b])
                        if not last:
                            dma("pool", xb[bs, :], x2t[:], r=[x2b])
                            norm_block(x2t, x2b, A1[l + 1], 0, l + 1, hT, hbuf, tb, junkp, xnp)
                        else:
                            jt, jb = junkp.get()
                            op("act", lambda e: e.activation(out=jt[:], in_=x2t[:], func=AF.Square, accum_out=st6[:, 0:1]), r=[x2b], w=[jb, Bst6])
                            op("dve", lambda e: e.tensor_scalar(out=st6[:, 1:2], in0=st6[:, 0:1], scalar1=1.0 / D, scalar2=EPS, op0=ALU.mult, op1=ALU.add), r=[Bst6], w=[Bst6])
                            op("act", lambda e: e.activation(out=st6[:, 2:3], in_=st6[:, 1:2], func=AF.Sqrt), r=[Bst6], w=[Bst6])
                            op("dve", lambda e: e.reciprocal(out=st6[:, 3:4], in_=st6[:, 2:3]), r=[Bst6], w=[Bst6])
                            op("dve", lambda e: e.scalar_tensor_tensor(out=jt[:], in0=x2t[:], scalar=st6[:, 3:4], in1=gfb[:], op0=ALU.mult, op1=ALU.mult), r=[x2b, Bst6, Bgf, jb], w=[jb])
                            dma("sp", O["y_p"][bs, :], jt[:], r=[jb], is_out=True)
                pp, ppb = kb.bank()
                for j in range(8):
                    for f in range(22):
                        op("pe", lambda e: e.matmul(pp[:, j * NS:(j + 1) * NS], lhsT=wffo_b[:, f, j * 128:(j + 1) * 128], rhs=acts[:, f, :], start=(f == 0), stop=(f == 21)), r=[Bwf, Bacts], w=[ppb], inc=(f == 21))
                tq, tqb = tmps.get()
                op("dve", lambda e: e.tensor_tensor(out=tq[:], in0=pp[:, 0:8 * NS].rearrange("p (j s) -> p j s", s=NS), in1=modT[l][:, 40:48, 1:5], op=ALU.mult), r=[ppb, Bmod[l]], w=[tqb])
                op("dve", lambda e: e.tensor_tensor(out=xsT[:], in0=xsT[:], in1=tq[:], op=ALU.add), r=[tqb, Bxs], w=[Bxs])
                if last:
                    sq, sqb = tmps.get()
                    op("dve", lambda e: e.tensor_tensor(out=sq[:], in0=xsT[:], in1=xsT[:], op=ALU.mult), r=[Bxs], w=[sqb])
                    pt, pbuf = kb.bank()
                    for k in range(8):
                        op("pe", lambda e: e.matmul(pt[:, 0:NS], lhsT=ones_f[:], rhs=sq[:, k, :], start=(k == 0), stop=(k == 7)), r=[sqb, Bc], w=[pbuf], inc=(k == 7))
                    rs, rsb = tmps.get()
                    op("dve", lambda e: e.tensor_scalar(out=rs[:, 0, :], in0=pt[:, 0:NS], scalar1=1.0 / D, scalar2=EPS, op0=ALU.mult, op1=ALU.add), r=[pbuf], w=[rsb])
                    op("act", lambda e: e.activation(out=rs[:, 1, :], in_=rs[:, 0, :], func=AF.Sqrt), r=[rsb], w=[rsb])
                    op("dve", lambda e: e.reciprocal(out=rs[:, 2, :], in_=rs[:, 1, :]), r=[rsb], w=[rsb])
                    op("dve", lambda e: e.tensor_tensor(out=sq[:], in0=xsT[:], in1=rs[:, 2:3, :].to_broadcast([128, 8, NS]), op=ALU.mult), r=[Bxs, rsb, sqb], w=[sqb])
                    op("dve", lambda e: e.tensor_tensor(out=sq[:], in0=sq[:], in1=vecT[0][:, V_GF:V_GF + 8].unsqueeze(2).to_broadcast([128, 8, NS]), op=ALU.mult), r=[sqb, Bmod[0]], w=[sqb])
                    for k in range(8):
                        dma("sp", O["y_s"][:, k * 128:(k + 1) * 128].rearrange("s p -> p s"), sq[:, k, :], r=[sqb], is_out=True)
                else:
                    sample_norm(l + 1, As1[l + 1], 0, tmps)
                kb.barrier()
            kb.es = es
        kb.finish()
        print("ninst", kb.ninst, "dma counts", {q: v[1] for q, v in kb.dq.items()}, "eng counts", {e: kb.cnt[e] for e in kb.E}, "hw incs", dict(kb.hw))
    return nc, kb.waited


def _prep_shared(inp):
    f = np.float32
    cols = chunk_cols()
    w_in = np.asarray(inp["w_in"], f)
    win = np.zeros((L, NCI, 128, 8, 128), f)
    for ci, (c0, wd) in enumerate(cols):
        blk = w_in[:, :, c0:c0 + wd].reshape(L, 8, 128, wd)
        win[:, ci, :, :, :wd] = blk.transpose(0, 2, 1, 3)
    wada = np.ascontiguousarray(np.asarray(inp["w_ada"], f).reshape(L, 8, 128, 48, 128).transpose(0, 3, 2, 1, 4))
    vecs = np.zeros((L, 256, 128), f)
    for l in range(L):
        rows = [np.asarray(inp["b_ada"][l], f).reshape(48, 128), np.asarray(inp["g_norm1"][l], f).reshape(8, 128), np.asarray(inp["g_norm2"][l], f).reshape(8, 128),
                np.asarray(inp["rg_conv_w"][l], f).reshape(40, 128), np.asarray(inp["rg_conv_b"][l], f).reshape(10, 128), np.asarray(inp["rg_ba"][l], f).reshape(10, 128),
                np.asarray(inp["rg_bx"][l], f).reshape(10, 128), np.asarray(inp["rg_lambda"][l], f).reshape(10, 128), np.asarray(inp["gdn_conv_w"][l], f).reshape(96, 128),
                np.asarray(inp["mla_q_norm_g"][l], f).reshape(3, 128), np.asarray(inp["mla_kv_norm_g"][l], f).reshape(2, 128), np.asarray(inp["gdn_norm_g"][l], f).reshape(1, 128),
                np.asarray(inp["g_final"], f).reshape(8, 128)]
        r = np.concatenate(rows, 0)
        vecs[l, :r.shape[0]] = r
    rgw = np.concatenate([np.asarray(inp["rg_wa"], f), np.asarray(inp["rg_wx"], f)], axis=1)

    def projl(w, nk):
        return np.ascontiguousarray(np.asarray(w, f).reshape(L, nk, 128, 8, 128).transpose(0, 3, 2, 1, 4))
    wffi_src = np.asarray(inp["w_ffn_in"], f).reshape(L, 8, 128, 2, 22, 128)
    wffi = np.ascontiguousarray(wffi_src.transpose(0, 4, 3, 2, 1, 5)).reshape(L, 44, 128, 8, 128)
    wukv = np.asarray(inp["w_ukv"], f)
    wukT = np.ascontiguousarray(wukv.reshape(L, KVL, H, NOPE + VH)[:, :, :, :NOPE].transpose(0, 2, 3, 1))
    gdn_ab = np.stack([np.asarray(inp["gdn_A_log"], f), np.asarray(inp["gdn_dt_bias"], f)], axis=1)
    ident = np.eye(128, dtype=f)
    masks = np.zeros((4, 128, 512), f)
    for o in range(4):
        p = np.arange(128)[:, None]
        j = np.arange(512)[None, :]
        masks[o] = (o * 128 + p <= j)
    gmask = np.zeros((4, 128, 128), f)
    pp_ = np.arange(128)[:, None]
    jj_ = np.arange(128)[None, :]
    gmask[0] = (pp_ <= jj_)
    gmask[1] = (pp_ == 127) * np.ones((1, 128), f)
    gmask[2] = 3e4 * (jj_ <= pp_)
    gmask[3] = 3e4 * (jj_ < pp_)
    lmask = np.zeros((14, 128, 128), f)
    for s_ in range(7):
        b_ = 2 ** s_
        mk = ((pp_ // (2 * b_)) == (jj_ // (2 * b_))) & ((pp_ % (2 * b_)) < b_) & ((jj_ % (2 * b_)) >= b_)
        lmask[2 * s_] = mk
        lmask[2 * s_ + 1] = mk.T
    inv = (10000.0 ** (-np.arange(0, ROPE, 2, dtype=f) / f(ROPE))).astype(f)

    def tables(pos):
        ang = (pos.astype(f)[:, None] * inv[None, :]).astype(f)
        C = np.ones((96, len(pos)), f)
        S = np.zeros((96, len(pos)), f)
        C[64:80] = np.cos(ang).T
        C[80:96] = np.cos(ang).T
        S[64:80] = np.sin(ang).T
        S[80:96] = np.sin(ang).T
        return C, S
    ropeC, ropeS = tables(np.arange(T))
    ropeCs, ropeSs = tables(np.array([PAST]))
    R = np.zeros((96, 96), f)
    for i in range(16):
        R[64 + i, 80 + i] = -1.0
        R[80 + i, 64 + i] = 1.0
    shared = dict(
        cache_ckv=np.asarray(inp["cache_ckv"], f), cache_krope=np.asarray(inp["cache_krope"], f),
        win=win, wada=wada, vecs=vecs, rgw=rgw, wrgp=projl(inp["w_rg_proj"], 10), wgdp=projl(inp["w_gdn_proj"], 8), wmlp=projl(inp["w_mla_proj"], 8),
        wo=np.asarray(inp["w_o"], f), wffi=wffi, wffo=np.asarray(inp["w_ffn_out"], f), wuq=np.asarray(inp["w_uq"], f), wukv=wukv, wukT=wukT,
        gdn_ab=gdn_ab, gmask=gmask, lmask=lmask, iota=np.arange(128, dtype=f).reshape(128, 1), ident=ident, masks=masks, ropeC=ropeC, ropeS=ropeS, ropeCs=ropeCs, ropeSs=ropeSs, rfullT=np.ascontiguousarray(R.T),
    )
    return shared


def kernel(stage=99, ncores=NCORES, **inp):
    f = np.float32
    shared = _prep_shared(inp)
    in_maps = []
    for c in range(NCORES):
        s0 = c * NS
        m = dict(shared)
        m["xp"] = np.ascontiguousarray(np.asarray(inp["x_prompt"][c], f))
        m["xs"] = np.ascontiguousarray(np.asarray(inp["x_sample"][s0:s0 + NS, 0], f))
        m["st_rg_conv"] = np.ascontiguousarray(np.asarray(inp["state_rg_conv"][:, s0:s0 + NS], f))
        m["st_rg_h"] = np.ascontiguousarray(np.asarray(inp["state_rg_h"][:, s0:s0 + NS], f))
        m["st_gdn_conv"] = np.ascontiguousarray(np.asarray(inp["state_gdn_conv"][:, s0:s0 + NS], f))
        m["st_gdn_S"] = np.ascontiguousarray(np.asarray(inp["state_gdn_S"][:, s0:s0 + NS], f))
        m["ptab"] = np.ascontiguousarray(np.asarray(inp["page_table"][s0:s0 + NS], np.int32))
        m["cc"] = np.ascontiguousarray(np.concatenate([np.asarray(inp["c_prompt"][c:c + 1], f), np.asarray(inp["c_sample"][s0:s0 + NS], f)], 0))
        in_maps.append(m)
    if stage < 50:
        for m in in_maps:
            m["cache_ckv"] = np.zeros((L, 1, PAGE, KVL), f)
            m["cache_krope"] = np.zeros((L, 1, PAGE, ROPE), f)
    _, waited = build_program(stage)
    nc, _ = build_program(stage, needed=waited)
    res = run_bass_kernel_spmd(nc, in_maps[:ncores], core_ids=list(range(ncores)))
    R = list(res.results)
    while len(R) < NCORES:
        R.append({k: np.zeros_like(v) for k, v in R[0].items()})

    def cat(name, axis):
        return np.concatenate([np.expand_dims(r[name], axis) if False else r[name] for r in R], axis=axis)
    y_p = np.stack([r["y_p"] for r in R], 0)
    y_s = np.concatenate([r["y_s"] for r in R], 0)[:, None, :]
    ckv_p = np.stack([r["ckv_p"] for r in R], 1)
    krope_p = np.stack([r["krope_p"] for r in R], 1)
    rg_conv_p = np.stack([r["rg_conv_p"] for r in R], 1)
    rg_h_p = np.stack([r["rg_h_p"] for r in R], 1)
    gdn_conv_p = np.stack([r["gdn_conv_p"] for r in R], 1)
    gdn_S_p = np.stack([r["gdn_S_p"] for r in R], 1)
    ckv_s = np.concatenate([r["ckv_s"] for r in R], 1)[:, :, None, :]
    krope_s = np.concatenate([r["krope_s"] for r in R], 1)[:, :, None, :]
    rg_conv_s = np.concatenate([r["rg_conv_s"] for r in R], 1)
    rg_h_s = np.concatenate([r["rg_h_s"] for r in R], 1)
    gdn_conv_s = np.concatenate([r["gdn_conv_s"] for r in R], 1)
    gdn_S_s = np.concatenate([r["gdn_S_s"] for r in R], 1)
    outs = (y_p, y_s, ckv_p, krope_p, rg_conv_p, rg_h_p, gdn_conv_p, gdn_S_p, ckv_s, krope_s, rg_conv_s, rg_h_s, gdn_conv_s, gdn_S_s)
    return tuple(np.ascontiguousarray(o.astype(np.float32)) for o in outs)
```

```python
import os
import numpy as np
from contextlib import ExitStack
import concourse.bass as bass
import concourse.mybir as mybir
from concourse.bass_utils import run_bass_kernel_spmd

F32 = mybir.dt.float32
BF16 = mybir.dt.bfloat16
I32 = mybir.dt.int32
AF = mybir.ActivationFunctionType
ALU = mybir.AluOpType
AX = mybir.AxisListType

NCORES = 8
T = 2048
D = 1024
NS = 4
L = 2
D_RNN = 1280
NPAGE = 128
PAGE = 128
KVL = 256
ROPE = 32
NOPE = 64
VH = 128
H = 8
D_FF = 2816
EPS = 1e-6
PAST = 16384
MLA_SCALE = (NOPE + ROPE) ** -0.5
OFF_RX, OFF_RY, OFF_QKV, OFF_Z, OFF_A, OFF_B, OFF_MQ, OFF_MKV, OFF_GATE = 0, 1280, 2560, 5632, 6656, 6664, 6672, 7056, 7344
CI_RX, CI_RY, CI_Q, CI_K, CI_V, CI_Z, CI_AB, CI_MQ, CI_CKV, CI_KR, CI_GATE, NCI = 0, 10, 20, 28, 36, 44, 52, 53, 56, 58, 59, 83
V_BADA, V_G1, V_G2, V_RGCW, V_RGCB, V_RGBA, V_RGBX, V_RGLAM, V_GDCW, V_QG, V_KVG, V_GDG, V_GF = 0, 48, 56, 64, 104, 114, 124, 134, 144, 240, 243, 245, 246


def chunk_cols():
    cols = []
    for n in range(10):
        cols.append((OFF_RX + n * 128, 128))
    for n in range(10):
        cols.append((OFF_RY + n * 128, 128))
    for part in range(3):
        for h in range(8):
            cols.append((OFF_QKV + part * 1024 + h * 128, 128))
    for h in range(8):
        cols.append((OFF_Z + h * 128, 128))
    cols.append((OFF_A, 16))
    for k in range(3):
        cols.append((OFF_MQ + k * 128, 128))
    for k in range(2):
        cols.append((OFF_MKV + k * 128, 128))
    cols.append((OFF_MKV + 192, 96))
    for j in range(24):
        cols.append((OFF_GATE + j * 128, 128))
    assert len(cols) == NCI
    return cols


class Buf:
    __slots__ = ("name", "w", "r")

    def __init__(self, name=""):
        self.name = name
        self.w = []
        self.r = []


class Tok:
    __slots__ = ("key", "val", "clock")

    def __init__(self, key, val, clock):
        self.key = key
        self.val = val
        self.clock = clock


class KB:
    NDMA = 8

    def __init__(self, nc, es, needed=None):
        self.nc = nc
        self.es = es
        self.needed = needed
        self.waited = set()
        self.hw = {}
        self.hwmap = {}
        self.E = {"pe": nc.tensor, "act": nc.scalar, "dve": nc.vector, "pool": nc.gpsimd, "sp": nc.sync}
        self.sems = {}
        self.cnt = {}
        self.clock = {e: {} for e in self.E}
        self.pend_r = {e: [] for e in self.E}
        self.pend_w = {e: [] for e in self.E}
        for e in self.E:
            self.sems[e] = es.enter_context(nc.semaphore("s_" + e))
            self.cnt[e] = 0
        self.dq = {}
        for q in ("sp", "pool"):
            lst = []
            for i in range(self.NDMA):
                k = "d_%s_%d" % (q, i)
                self.sems[k] = es.enter_context(nc.semaphore(k))
                self.cnt[k] = 0
                lst.append(k)
            self.dq[q] = [lst, 0]
        self.ninst = 0
        self.out_toks = []
        self.banks = []
        self.bank_i = 0
        for i in range(8):
            t = es.enter_context(nc.psum_tensor("pb%d" % i, [128, 512], F32))
            self.banks.append((t, Buf("pb%d" % i)))

    def sb(self, name, shape, dt):
        self.uid = getattr(self, "uid", 0) + 1
        return self.es.enter_context(self.nc.sbuf_tensor("t%d_%s" % (self.uid, name), list(shape), dt))

    def bank(self):
        t, b = self.banks[self.bank_i % 6]
        self.bank_i += 1
        return t, b

    def acc_bank(self, i):
        return self.banks[6 + i]

    def _wait(self, e, tok):
        ck = self.clock[e]
        if ck.get(tok.key, 0) >= tok.val:
            return
        self.waited.add((tok.key, tok.val))
        hwval = tok.val
        if tok.key in self.E and self.needed is not None:
            hwval = self.hwmap[tok.key][tok.val]
        self.E[e].wait_ge(self.sems[tok.key], hwval)
        for k, v in tok.clock.items():
            if ck.get(k, 0) < v:
                ck[k] = v
        if ck.get(tok.key, 0) < tok.val:
            ck[tok.key] = tok.val

    def _deps(self, e, r, w):
        toks = []
        for b in r:
            toks.extend(b.w)
        for b in w:
            toks.extend(b.w)
            toks.extend(b.r)
        for t in toks:
            if e == "pe" and t.key == "pe":
                continue
            self._wait(e, t)

    def _commit(self, e, tok, r, w):
        r = list(r) + self.pend_r[e]
        w = list(w) + self.pend_w[e]
        self.pend_r[e] = []
        self.pend_w[e] = []
        for b in w:
            b.w = [tok]
            b.r = []
        for b in r:
            if b in w:
                continue
            b.r = [t for t in b.r if t.key != tok.key] + [tok]

    def op(self, e, fn, r=(), w=(), inc=True):
        self._deps(e, r, w)
        ins = fn(self.E[e])
        self.ninst += 1
        if inc:
            self.cnt[e] += 1
            if self.needed is None or (e, self.cnt[e]) in self.needed:
                self.hw[e] = self.hw.get(e, 0) + 1
                self.hwmap.setdefault(e, {})[self.cnt[e]] = self.hw[e]
                ins.then_inc(self.sems[e], 1)
            ck = dict(self.clock[e])
            ck[e] = self.cnt[e]
            tok = Tok(e, self.cnt[e], ck)
            self._commit(e, tok, r, w)
        else:
            self.pend_r[e].extend(r)
            self.pend_w[e].extend(w)
        return ins

    def dma(self, q, out, in_, r=(), w=(), is_out=False, **kw):
        lst, i = self.dq[q]
        key = lst[i % len(lst)]
        self.dq[q][1] = i + 1
        if self.cnt[key] > 0:
            self._wait(q, Tok(key, self.cnt[key], {}))
        self._deps(q, r, w)
        ins = self.E[q].dma_start(out=out, in_=in_, **kw)
        self.ninst += 1
        self.cnt[key] += 16
        ins.then_inc(self.sems[key], 16)
        ck = dict(self.clock[q])
        ck[key] = self.cnt[key]
        tok = Tok(key, self.cnt[key], ck)
        for b in w:
            b.w = [tok]
            b.r = []
        for b in r:
            b.r = [t for t in b.r if t.key != key] + [tok]
        if is_out:
            self.out_toks.append(tok)
        return tok

    def dma_fn(self, q, fn, r=(), w=(), extra_w=()):
        lst, i = self.dq[q]
        key = lst[i % len(lst)]
        self.dq[q][1] = i + 1
        if self.cnt[key] > 0:
            self._wait(q, Tok(key, self.cnt[key], {}))
        self._deps(q, r, w)
        ins = fn(self.E[q])
        self.ninst += 1
        self.cnt[key] += 16
        ins.then_inc(self.sems[key], 16)
        ck = dict(self.clock[q])
        ck[key] = self.cnt[key]
        tok = Tok(key, self.cnt[key], ck)
        for b in w:
            b.w = [tok]
            b.r = []
        for b in extra_w:
            b.w = b.w + [tok]
        for b in r:
            b.r = [t for t in b.r if t.key != key] + [tok]
        return tok

    def barrier(self):
        toks = [Tok(k, c, {}) for k, c in self.cnt.items() if c > 0]
        for e in self.E:
            for t in toks:
                if t.key != e:
                    self._wait(e, t)

    def finish(self):
        toks = [Tok(k, c, {}) for k, c in self.cnt.items() if c > 0]
        for t in toks:
            if t.key != "sp":
                self._wait("sp", t)


class Rot:
    def __init__(self, kb, name, shape, dt, n):
        self.t = [kb.sb("%s%d" % (name, i), shape, dt) for i in range(n)]
        self.b = [Buf("%s%d" % (name, i)) for i in range(n)]
        self.i = 0

    def get(self):
        i = self.i % len(self.t)
        self.i += 1
        return self.t[i], self.b[i]


IN_SPECS = [
    ("xp", [T, D], F32), ("xs", [NS, D], F32), ("cache_all", [L, 5120, PAGE, KVL + ROPE], F32),
    ("st_rg_conv", [L, NS, 3, D_RNN], F32), ("st_rg_h", [L, NS, D_RNN], F32), ("st_gdn_conv", [L, NS, 3, 3072], F32),
    ("st_gdn_S", [L, NS, H, 128, 128], F32), ("ptab", [NS, NPAGE], I32), ("cc", [5, D], F32),
    ("win", [L, NCI, 128, 8, 128], F32), ("wada", [L, 48, 128, 8, 128], F32), ("vecs", [L, 256, 128], F32),
    ("rgw", [L, 20, 128, 128], F32), ("wrgp", [L, 8, 128, 10, 128], F32), ("wgdp", [L, 8, 128, 8, 128], F32),
    ("wmlp", [L, 8, 128, 8, 128], F32), ("wo", [L, D, D], F32), ("wffi", [L, 44, 128, 8, 128], F32), ("wffo", [L, D_FF, D], F32),
    ("wuq", [L, 384, 768], F32), ("wukv", [L, KVL, 1536], F32), ("wukT", [L, H, NOPE, KVL], F32),
    ("gdn_ab", [L, 2, H], F32), ("ident", [128, 128], F32), ("masks", [4, 128, 512], F32), ("ropeC", [96, T], F32), ("ropeS", [96, T], F32),
    ("ropeCs", [96, 1], F32), ("ropeSs", [96, 1], F32), ("rfullT", [96, 96], F32), ("gmask", [4, 128, 128], F32), ("lmask", [14, 128, 128], F32), ("iota", [128, 1], F32),
]
OUT_SPECS = [
    ("y_p", [T, D]), ("y_s", [NS, D]), ("ckv_p", [L, T, KVL]), ("krope_p", [L, T, ROPE]), ("rg_conv_p", [L, 3, D_RNN]), ("rg_h_p", [L, D_RNN]),
    ("gdn_conv_p", [L, 3, 3072]), ("gdn_S_p", [L, H, 128, 128]), ("ckv_s", [L, NS, KVL]), ("krope_s", [L, NS, ROPE]),
    ("rg_conv_s", [L, NS, 3, D_RNN]), ("rg_h_s", [L, NS, D_RNN]), ("gdn_conv_s", [L, NS, 3, 3072]), ("gdn_S_s", [L, NS, H, 128, 128]),
]


def build_program(stage=99, needed=None):
    nc = bass.Bass("TRN2", target_bir_lowering=False)
    I = {}
    for name, shape, dt in IN_SPECS:
        if stage < 50 and name.startswith("cache_"):
            shape = [L, 1] + shape[2:]
        I[name] = nc.dram_tensor(name, shape, dt, kind="ExternalInput").ap()
    O = {}
    for name, shape in OUT_SPECS:
        O[name] = nc.dram_tensor(name, shape, F32, kind="ExternalOutput").ap()
    xa = nc.dram_tensor("xa_scr", [T, D], F32, kind="ExternalOutput").ap()
    xb = nc.dram_tensor("xb_scr", [T, D], F32, kind="ExternalOutput").ap()

    with ExitStack() as es:
        kb = KB(nc, es, needed)
        es.enter_context(nc.allow_non_contiguous_dma(reason="small strided state/vector transfers"))
        es.enter_context(nc.allow_low_precision(reason="bf16 matmul operands, fp32 accumulate"))
        op = kb.op
        dma = kb.dma

        ident_f = kb.sb("ident_f", [128, 128], F32)
        ident_b = kb.sb("ident_b", [128, 128], BF16)
        ones_b = kb.sb("ones_b", [128, 128], BF16)
        ones_f = kb.sb("ones_f", [128, 128], F32)
        Bc = Buf("consts")
        dma("sp", ident_f[:], I["ident"], w=[Bc])
        op("dve", lambda e: e.tensor_copy(out=ident_b[:], in_=ident_f[:]), r=[Bc], w=[Bc])
        op("pool", lambda e: e.memset(ones_b[:], 1.0), w=[Bc])
        op("pool", lambda e: e.memset(ones_f[:], 1.0), w=[Bc])

        vecT = [kb.sb("vecT%d" % l, [128, 256], F32) for l in range(L)]
        modT = [kb.sb("modT%d" % l, [128, 48, 5], F32) for l in range(L)]
        A1 = [kb.sb("A1_%d" % l, [128, 8], F32) for l in range(L)]
        A2 = [kb.sb("A2_%d" % l, [128, 8], F32) for l in range(L)]
        As1 = [kb.sb("As1_%d" % l, [128, 8, NS], F32) for l in range(L)]
        As2 = [kb.sb("As2_%d" % l, [128, 8, NS], F32) for l in range(L)]
        Bmod = [Buf("mod%d" % l) for l in range(L)]
        xsT = kb.sb("xsT", [128, 8, NS], F32)
        Bxs = Buf("xsT")
        hsT = kb.sb("hsT", [128, 8, NS], BF16)
        Bhs = Buf("hsT")

        with ExitStack() as es0:
            kb.es = es0
            craw = kb.sb("craw", [5, D], F32)
            csil = kb.sb("csil", [5, D], F32)
            siluT = kb.sb("siluT", [128, 8, 5], F32)
            vraw = kb.sb("vraw", [128, 2, 128], F32)
            xs_raw = kb.sb("xs_raw", [NS, D], F32)
            wa_rot = Rot(kb, "wadap", [128, 4, 8, 128], F32, 3)
            B0 = Buf("p0")
            dma("sp", craw[:], I["cc"], w=[B0])
            dma("sp", xs_raw[:], I["xs"], w=[B0])
            op("act", lambda e: e.activation(out=csil[:], in_=craw[:], func=AF.Silu), r=[B0], w=[B0])
            pt, pbuf = kb.bank()
            for k in range(8):
                op("pe", lambda e: e.transpose(pt[:, k * 5:(k + 1) * 5], csil[:, k * 128:(k + 1) * 128], ident_f[0:5, 0:5]), r=[B0, Bc], w=[pbuf], inc=(k == 7))
            op("dve", lambda e: e.tensor_copy(out=siluT[:].rearrange("p k s -> p (k s)"), in_=pt[:, 0:40]), r=[pbuf], w=[B0])
            pt, pbuf = kb.bank()
            for k in range(8):
                op("pe", lambda e: e.transpose(pt[:, k * NS:(k + 1) * NS], xs_raw[:, k * 128:(k + 1) * 128], ident_f[0:NS, 0:NS]), r=[B0, Bc], w=[pbuf], inc=(k == 7))
            op("dve", lambda e: e.tensor_copy(out=xsT[:].rearrange("p k s -> p (k s)"), in_=pt[:, 0:8 * NS]), r=[pbuf], w=[Bxs])
            for l in range(L):
                dma("sp", vraw[:], I["vecs"][l].rearrange("(a r) c -> r a c", a=2), w=[B0])
                pt, pbuf = kb.bank()
                for a in range(2):
                    op("pe", lambda e: e.transpose(pt[:, a * 128:(a + 1) * 128], vraw[:, a, :], ident_f[:]), r=[B0, Bc], w=[pbuf], inc=(a == 1))
                op("dve", lambda e: e.tensor_copy(out=vecT[l][:], in_=pt[:, 0:256]), r=[pbuf], w=[Bmod[l]])
                pm, pmb = kb.bank()
                for g in range(12):
                    wt, wb_ = wa_rot.get()
                    dma("sp", wt[:], I["wada"][l, g * 4:(g + 1) * 4].rearrange("j p k m -> p j k m"), w=[wb_])
                    for jj in range(4):
                        j = g * 4 + jj
                        for k in range(8):
                            op("pe", lambda e: e.matmul(pm[:, j * 5:(j + 1) * 5], lhsT=wt[:, jj, k, :], rhs=siluT[:, k, :], start=(k == 0), stop=(k == 7)),
                               r=[wb_, B0], w=[pmb], inc=(k == 7))
                op("dve", lambda e: e.tensor_tensor(out=modT[l][:], in0=pm[:, 0:240].rearrange("p (j s) -> p j s", s=5),
                                                    in1=vecT[l][:, V_BADA:V_BADA + 48].unsqueeze(2).to_broadcast([128, 48, 5]), op=ALU.add), r=[pmb, Bmod[l]], w=[Bmod[l]])
                for (Aout, As_out, goff, scoff) in ((A1[l], As1[l], V_G1, 8), (A2[l], As2[l], V_G2, 32)):
                    op("dve", lambda e: e.scalar_tensor_tensor(out=Aout[:], in0=modT[l][:, scoff:scoff + 8, 0], scalar=1.0, in1=vecT[l][:, goff:goff + 8], op0=ALU.add, op1=ALU.mult),
                       r=[Bmod[l]], w=[Bmod[l]])
                    op("dve", lambda e: e.scalar_tensor_tensor(out=As_out[:], in0=modT[l][:, scoff:scoff + 8, 1:5], scalar=1.0,
                                                               in1=vecT[l][:, goff:goff + 8].unsqueeze(2).to_broadcast([128, 8, NS]), op0=ALU.add, op1=ALU.mult),
                       r=[Bmod[l]], w=[Bmod[l]])
            kb.barrier()
        kb.es = es

        ss_all = kb.sb("ss_all", [128, 4], F32)
        Bss = Buf("ss")

        def norm_block(xt, bxt, A, Bv_off, l_mod, hT, hbuf, tb, junkp, xnp):
            jt, jb = junkp.get()
            op("act", lambda e: e.activation(out=jt[:], in_=xt[:], func=AF.Square, accum_out=ss_all[:, 0:1]), r=[bxt], w=[jb, Bss])
            op("dve", lambda e: e.tensor_scalar(out=ss_all[:, 1:2], in0=ss_all[:, 0:1], scalar1=1.0 / D, scalar2=EPS, op0=ALU.mult, op1=ALU.add), r=[Bss], w=[Bss])
            op("act", lambda e: e.activation(out=ss_all[:, 2:3], in_=ss_all[:, 1:2], func=AF.Sqrt), r=[Bss], w=[Bss])
            op("dve", lambda e: e.reciprocal(out=ss_all[:, 3:4], in_=ss_all[:, 2:3]), r=[Bss], w=[Bss])
            xn, xnb = xnp.get()
            op("act", lambda e: e.activation(out=xn[:], in_=xt[:], func=AF.Copy, scale=ss_all[:, 3:4]), r=[bxt, Bss], w=[xnb])
            pt, pbuf = kb.bank()
            ptb = pt[:].bitcast(BF16)
            for c in range(8):
                op("pe", lambda e: e.transpose(ptb[:, c * 128:(c + 1) * 128], xn[:, c * 128:(c + 1) * 128], ident_b[:]), r=[xnb, Bc], w=[pbuf], inc=(c == 7))
            op("dve", lambda e: e.tensor_tensor(out=jt[:].rearrange("p (c t) -> p c t", c=8), in0=ptb.rearrange("p (c t) -> p c t", c=8),
                                                in1=A[:].unsqueeze(2).to_broadcast([128, 8, 128]), op=ALU.mult), r=[pbuf, Bmod[l_mod]], w=[jb])
            op("pool", lambda e: e.tensor_tensor(out=hT[:, :, tb * 128:(tb + 1) * 128], in0=jt[:].rearrange("p (c t) -> p c t", c=8),
                                                 in1=modT[l_mod][:, Bv_off:Bv_off + 8, 0:1].to_broadcast([128, 8, 128]), op=ALU.add), r=[jb, Bmod[l_mod]], w=[hbuf[tb]])

        def sample_norm(l_mod, As, Bv_off, tmp_pool):
            sq, sqb = tmp_pool.get()
            op("dve", lambda e: e.tensor_tensor(out=sq[:], in0=xsT[:], in1=xsT[:], op=ALU.mult), r=[Bxs], w=[sqb])
            pt, pbuf = kb.bank()
            for k in range(8):
                op("pe", lambda e: e.matmul(pt[:, 0:NS], lhsT=ones_f[:], rhs=sq[:, k, :], start=(k == 0), stop=(k == 7)), r=[sqb, Bc], w=[pbuf], inc=(k == 7))
            rs, rsb = tmp_pool.get()
            op("dve", lambda e: e.tensor_scalar(out=rs[:, 0, :], in0=pt[:, 0:NS], scalar1=1.0 / D, scalar2=EPS, op0=ALU.mult, op1=ALU.add), r=[pbuf], w=[rsb])
            op("act", lambda e: e.activation(out=rs[:, 1, :], in_=rs[:, 0, :], func=AF.Sqrt), r=[rsb], w=[rsb])
            op("dve", lambda e: e.reciprocal(out=rs[:, 2, :], in_=rs[:, 1, :]), r=[rsb], w=[rsb])
            op("dve", lambda e: e.tensor_tensor(out=sq[:], in0=xsT[:], in1=rs[:, 2:3, :].to_broadcast([128, 8, NS]), op=ALU.mult), r=[Bxs, rsb, sqb], w=[sqb])
            op("dve", lambda e: e.tensor_tensor(out=sq[:], in0=sq[:], in1=As[:], op=ALU.mult), r=[sqb, Bmod[l_mod]], w=[sqb])
            op("dve", lambda e: e.tensor_tensor(out=hsT[:], in0=sq[:], in1=modT[l_mod][:, Bv_off:Bv_off + 8, 1:5], op=ALU.add), r=[sqb, Bmod[l_mod]], w=[Bhs])

        def make_gtbc(l, goff, dst, dstb, tmpbc, tmpb):
            pg0, pgb0 = kb.bank()
            pg1, pgb1 = kb.bank()
            for c in range(8):
                op("dve", lambda e: e.tensor_scalar(out=tmpbc[:], in0=ones_f[:], scalar1=modT[l][:, goff + c, 0:1], scalar2=None, op0=ALU.mult), r=[Bmod[l], Bc, tmpb], w=[tmpb])
                pg, pgb = (pg0, pgb0) if c < 4 else (pg1, pgb1)
                op("pe", lambda e: e.matmul(pg[:, (c % 4) * 128:(c % 4 + 1) * 128], lhsT=tmpbc[:], rhs=ident_f[:], start=True, stop=True), r=[tmpb, Bc], w=[pgb])
            op("act", lambda e: e.activation(out=dst[:, 0:512], in_=pg0[:], func=AF.Copy), r=[pgb0], w=[dstb])
            op("act", lambda e: e.activation(out=dst[:, 512:1024], in_=pg1[:], func=AF.Copy), r=[pgb1], w=[dstb])

        wpiece = Rot(kb, "wpiece", [128, 8, 128], BF16, 4)

        def inproj(l, ci, hT, hbuf, width, evac, evac_s):
            wt, wb_ = wpiece.get()
            dma("pool", wt[:], I["win"][l, ci], w=[wb_])
            for tg in range(4):
                pt, pbuf = kb.bank()
                for k in range(8):
                    op("pe", lambda e: e.matmul(pt[0:width, :], lhsT=wt[:, k, 0:width], rhs=hT[:, k, tg * 512:(tg + 1) * 512], start=(k == 0), stop=(k == 7)),
                       r=[wb_] + hbuf[tg * 4:tg * 4 + 4], w=[pbuf], inc=(k == 7))
                evac(tg, pt, pbuf)
            if evac_s is not None:
                pt, pbuf = kb.bank()
                for k in range(8):
                    op("pe", lambda e: e.matmul(pt[0:width, 0:NS], lhsT=wt[:, k, 0:width], rhs=hsT[:, k, :], start=(k == 0), stop=(k == 7)),
                       r=[wb_, Bhs], w=[pbuf], inc=(k == 7))
                evac_s(pt, pbuf)

        hT = kb.sb("hT", [128, 8, T], BF16)
        hbuf = [Buf("hT%d" % i) for i in range(16)]
        oT = kb.sb("oT", [128, 10, T], BF16)
        oTb = [Buf("oT%d" % n) for n in range(10)]
        osT = kb.sb("osT", [128, 10, NS], BF16)
        Bos = Buf("osT")
        mT = kb.sb("mT", [128, 8, T], BF16)
        mTb = [Buf("mT%d" % n) for n in range(8)]
        msT = kb.sb("msT", [128, 8, NS], F32)
        Bms = Buf("msT")
        gm = kb.sb("gmask", [128, 4, 128], F32)
        dma("sp", gm[:], I["gmask"].rearrange("a p c -> p a c"), w=[Bc])
        lm = kb.sb("lmask", [128, 14, 128], BF16)
        dma("pool", lm[:], I["lmask"].rearrange("a p c -> p a c"), w=[Bc])
        projw = Rot(kb, "projw", [128, 10, 128], BF16, 2)

        def branch_proj(l, wname, nk, gate0, first):
            for j in range(8):
                wt, wtb = projw.get()
                dma("pool", wt[:, 0:nk, :], I[wname][l, j], w=[wtb])
                gw, gwb = wpiece.get()
                dma("pool", gw[:], I["win"][l, gate0 + j], w=[gwb])
                for tg in range(4):
                    sl = slice(tg * 512, (tg + 1) * 512)
                    pp, ppb = kb.bank()
                    for n in range(nk):
                        op("pe", lambda e: e.matmul(pp[:], lhsT=wt[:, n, :], rhs=oT[:, n, sl], start=(n == 0), stop=(n == nk - 1)), r=[wtb, oTb[n]], w=[ppb], inc=(n == nk - 1))
                    pg, pgb = kb.bank()
                    for k in range(8):
                        op("pe", lambda e: e.matmul(pg[:], lhsT=gw[:, k, :], rhs=hT[:, k, sl], start=(k == 0), stop=(k == 7)), r=[gwb] + hbuf[tg * 4:tg * 4 + 4], w=[pgb], inc=(k == 7))
                    sg, sgb = sgp.get()
                    op("act", lambda e: e.activation(out=sg[:], in_=pg[:], func=AF.Sigmoid), r=[pgb], w=[sgb])
                    if first:
                        op("dve", lambda e: e.tensor_tensor(out=mT[:, j, sl], in0=pp[:], in1=sg[:], op=ALU.mult), r=[ppb, sgb], w=[mTb[j]])
                    else:
                        op("dve", lambda e: e.tensor_tensor(out=sg[:], in0=pp[:], in1=sg[:], op=ALU.mult), r=[ppb, sgb], w=[sgb])
                        op("pool", lambda e: e.tensor_tensor(out=mT[:, j, sl], in0=mT[:, j, sl], in1=sg[:], op=ALU.add), r=[sgb, mTb[j]], w=[mTb[j]])
                pp, ppb = kb.bank()
                for n in range(nk):
                    op("pe", lambda e: e.matmul(pp[:, 0:NS], lhsT=wt[:, n, :], rhs=osT[:, n, :], start=(n == 0), stop=(n == nk - 1)), r=[wtb, Bos], w=[ppb], inc=False)
                for k in range(8):
                    op("pe", lambda e: e.matmul(pp[:, NS:2 * NS], lhsT=gw[:, k, :], rhs=hsT[:, k, :], start=(k == 0), stop=(k == 7)), r=[gwb, Bhs], w=[ppb], inc=(k == 7))
                sg, sgb = sgp.get()
                op("act", lambda e: e.activation(out=sg[:, 0:NS], in_=pp[:, NS:2 * NS], func=AF.Sigmoid), r=[ppb], w=[sgb])
                if first:
                    op("dve", lambda e: e.tensor_tensor(out=msT[:, j, :], in0=pp[:, 0:NS], in1=sg[:, 0:NS], op=ALU.mult), r=[ppb, sgb], w=[Bms])
                else:
                    op("dve", lambda e: e.tensor_tensor(out=sg[:, 0:NS], in0=pp[:, 0:NS], in1=sg[:, 0:NS], op=ALU.mult), r=[ppb, sgb], w=[sgb])
                    op("dve", lambda e: e.tensor_tensor(out=msT[:, j, :], in0=msT[:, j, :], in1=sg[:, 0:NS], op=ALU.add), r=[sgb, Bms], w=[Bms])

        sgp = Rot(kb, "sgp", [128, 512], F32, 2)

        for l in range(L if stage >= 10 else 1):
            xsrc = I["xp"] if l == 0 else xb
            with ExitStack() as es1:
                kb.es = es1
                xin = Rot(kb, "xin", [128, D], F32, 2)
                junkp = Rot(kb, "junk", [128, D], F32, 2)
                xnp = Rot(kb, "xn", [128, D], BF16, 2)
                tmps = Rot(kb, "tmps", [128, 8, NS], F32, 3)
                if l == 0:
                    for tb in range(16):
                        xt, bxt = xin.get()
                        dma("sp", xt[:], xsrc[tb * 128:(tb + 1) * 128, :], w=[bxt])
                        norm_block(xt, bxt, A1[l], 0, l, hT, hbuf, tb, junkp, xnp)
                if l == 0:
                    sample_norm(l, As1[l], 0, tmps)
                kb.barrier()
            kb.es = es

            with ExitStack() as es2:
                kb.es = es2
                rgw = kb.sb("rgw", [128, 20, 128], BF16)
                Brgw = Buf("rgw")
                dma("pool", rgw[:], I["rgw"][l].rearrange("n j k -> j n k"), w=[Brgw])
                cneg = kb.sb("cneg", [128, 10], F32)
                Bcn = Buf("cneg")
                op("act", lambda e: e.activation(out=cneg[:], in_=vecT[l][:, V_RGLAM:V_RGLAM + 10], func=AF.Exp, scale=-1.0), r=[Bmod[l]], w=[Bcn])
                op("act", lambda e: e.activation(out=cneg[:], in_=cneg[:], func=AF.Ln, bias=1.0), r=[Bcn], w=[Bcn])
                op("dve", lambda e: e.tensor_scalar(out=cneg[:], in0=cneg[:], scalar1=-8.0, scalar2=None, op0=ALU.mult), r=[Bcn], w=[Bcn])
                f32p = Rot(kb, "rgf", [128, T + 3], F32, 4)
                b16p = Rot(kb, "rgb", [128, T], BF16, 3)
                smallp = Rot(kb, "rgs", [128, 8, NS], F32, 6)
                scv = kb.sb("scv", [128, 10, NS, 3], F32)
                sh0 = kb.sb("sh0", [128, 10, NS], F32)
                Bst = Buf("rgstate")
                for n in range(10):
                    dma("sp", scv[:, n], I["st_rg_conv"][l, :, :, n * 128:(n + 1) * 128].rearrange("s j p -> p s j"), w=[Bst])
                for n in range(10):
                    dma("sp", sh0[:, n, :], I["st_rg_h"][l, :, n * 128:(n + 1) * 128].rearrange("s p -> p s"), w=[Bst])
                for n in range(10):
                    ux, uxb = f32p.get()
                    op("pool", lambda e: e.memset(ux[:, 0:3], 0.0), w=[uxb])
                    uxs, uxsb = smallp.get()
                    gy, gyb = b16p.get()
                    gys, gysb = smallp.get()

                    def ev_x(tg, pt, pbuf):
                        op("act", lambda e: e.activation(out=ux[:, 3 + tg * 512:3 + (tg + 1) * 512], in_=pt[:], func=AF.Copy), r=[pbuf], w=[uxb])

                    def ev_xs(pt, pbuf):
                        op("act", lambda e: e.activation(out=uxs[:, 0, :], in_=pt[:, 0:NS], func=AF.Copy), r=[pbuf], w=[uxsb])

                    def ev_y(tg, pt, pbuf):
                        op("act", lambda e: e.activation(out=gy[:, tg * 512:(tg + 1) * 512], in_=pt[:], func=AF.Gelu_apprx_tanh), r=[pbuf], w=[gyb])

                    def ev_ys(pt, pbuf):
                        op("act", lambda e: e.activation(out=gys[:, 0, :], in_=pt[:, 0:NS], func=AF.Gelu_apprx_tanh), r=[pbuf], w=[gysb])

                    inproj(l, CI_RX + n, hT, hbuf, 128, ev_x, ev_xs)
                    inproj(l, CI_RY + n, hT, hbuf, 128, ev_y, ev_ys)
                    dma("sp", O["rg_conv_p"][l, :, n * 128:(n + 1) * 128].rearrange("j p -> p j"), ux[:, T:T + 3], r=[uxb], is_out=True)
                    xc, xcb = f32p.get()
                    cw = lambda j: vecT[l][:, V_RGCW + j * 10 + n:V_RGCW + j * 10 + n + 1]
                    op("dve", lambda e: e.tensor_scalar(out=xc[:, 0:T], in0=ux[:, 3:3 + T], scalar1=cw(3), scalar2=vecT[l][:, V_RGCB + n:V_RGCB + n + 1], op0=ALU.mult, op1=ALU.add),
                       r=[uxb, Bmod[l]], w=[xcb])
                    for j in range(3):
                        op("dve", lambda e: e.scalar_tensor_tensor(out=xc[:, 0:T], in0=ux[:, j:j + T], scalar=cw(j), in1=xc[:, 0:T], op0=ALU.mult, op1=ALU.add),
                           r=[uxb, xcb, Bmod[l]], w=[xcb])
                    xcbf, xcbfb = b16p.get()
                    op("pool", lambda e: e.tensor_copy(out=xcbf[:], in_=xc[:, 0:T]), r=[xcb], w=[xcbfb])
                    xcs, xcsb = smallp.get()
                    op("dve", lambda e: e.tensor_scalar(out=xcs[:, 0, :], in0=uxs[:, 0, :], scalar1=cw(3), scalar2=vecT[l][:, V_RGCB + n:V_RGCB + n + 1], op0=ALU.mult, op1=ALU.add),
                       r=[uxsb, Bmod[l]], w=[xcsb])
                    for j in range(3):
                        op("dve", lambda e: e.scalar_tensor_tensor(out=xcs[:, 0, :], in0=scv[:, n, :, j], scalar=cw(j), in1=xcs[:, 0, :], op0=ALU.mult, op1=ALU.add),
                           r=[Bst, xcsb, Bmod[l]], w=[xcsb])
                    op("dve", lambda e: e.tensor_copy(out=xcs[:, 1, :].bitcast(BF16)[:, 0:NS], in_=xcs[:, 0, :]), r=[xcsb], w=[xcsb])
                    for j in range(2):
                        dma("sp", O["rg_conv_s"][l, :, j, n * 128:(n + 1) * 128].rearrange("s p -> p s"), scv[:, n, :, j + 1], r=[Bst], is_out=True)
                    dma("sp", O["rg_conv_s"][l, :, 2, n * 128:(n + 1) * 128].rearrange("s p -> p s"), uxs[:, 0, :], r=[uxsb], is_out=True)
                    a_t, a_b = f32p.get()
                    i_t, i_b = f32p.get()
                    for tg in range(4):
                        sl = slice(tg * 512, (tg + 1) * 512)
                        pt, pbuf = kb.bank()
                        op("pe", lambda e: e.matmul(pt[:], lhsT=rgw[:, n, :], rhs=xcbf[:, sl], start=True, stop=True), r=[Brgw, xcbfb], w=[pbuf])
                        op("act", lambda e: e.activation(out=a_t[:, sl], in_=pt[:], func=AF.Sigmoid, bias=vecT[l][:, V_RGBA + n:V_RGBA + n + 1]), r=[pbuf, Bmod[l]], w=[a_b])
                        pt2, pbuf2 = kb.bank()
                        op("pe", lambda e: e.matmul(pt2[:], lhsT=rgw[:, 10 + n, :], rhs=xcbf[:, sl], start=True, stop=True), r=[Brgw, xcbfb], w=[pbuf2])
                        op("act", lambda e: e.activation(out=i_t[:, sl], in_=pt2[:], func=AF.Sigmoid, bias=vecT[l][:, V_RGBX + n:V_RGBX + n + 1]), r=[pbuf2, Bmod[l]], w=[i_b])
                    pt, pbuf = kb.bank()
                    op("pe", lambda e: e.matmul(pt[:, 0:NS], lhsT=rgw[:, n, :], rhs=xcs[:, 1, :].bitcast(BF16)[:, 0:NS], start=True, stop=True), r=[Brgw, xcsb], w=[pbuf], inc=False)
                    op("pe", lambda e: e.matmul(pt[:, NS:2 * NS], lhsT=rgw[:, 10 + n, :], rhs=xcs[:, 1, :].bitcast(BF16)[:, 0:NS], start=True, stop=True), r=[Brgw, xcsb], w=[pbuf])
                    gs, gsb = smallp.get()
                    op("act", lambda e: e.activation(out=gs[:, 0, :], in_=pt[:, 0:NS], func=AF.Sigmoid, bias=vecT[l][:, V_RGBA + n:V_RGBA + n + 1]), r=[pbuf, Bmod[l]], w=[gsb])
                    op("act", lambda e: e.activation(out=gs[:, 1, :], in_=pt[:, NS:2 * NS], func=AF.Sigmoid, bias=vecT[l][:, V_RGBX + n:V_RGBX + n + 1]), r=[pbuf, Bmod[l]], w=[gsb])
                    op("act", lambda e: e.activation(out=a_t[:, 0:T], in_=a_t[:, 0:T], func=AF.Exp, scale=cneg[:, n:n + 1]), r=[a_b, Bcn], w=[a_b])
                    op("act", lambda e: e.activation(out=gs[:, 0, :], in_=gs[:, 0, :], func=AF.Exp, scale=cneg[:, n:n + 1]), r=[gsb, Bcn], w=[gsb])
                    op("pool", lambda e: e.tensor_tensor(out=i_t[:, 0:T], in0=i_t[:, 0:T], in1=xc[:, 0:T], op=ALU.mult), r=[i_b, xcb], w=[i_b])
                    op("dve", lambda e: e.tensor_tensor(out=xc[:, 0:T], in0=a_t[:, 0:T], in1=a_t[:, 0:T], op=ALU.mult), r=[a_b, xcb], w=[xcb])
                    op("act", lambda e: e.activation(out=xc[:, 0:T], in_=xc[:, 0:T], func=AF.Sqrt, scale=-1.0, bias=1.0), r=[xcb], w=[xcb])
                    op("pool", lambda e: e.tensor_tensor(out=i_t[:, 0:T], in0=i_t[:, 0:T], in1=xc[:, 0:T], op=ALU.mult), r=[i_b, xcb], w=[i_b])
                    op("dve", lambda e: e.tensor_tensor_scan(out=xc[:, 0:T], data0=a_t[:, 0:T], data1=i_t[:, 0:T], initial=0.0, op0=ALU.mult, op1=ALU.add), r=[a_b, i_b, xcb], w=[xcb])
                    op("dve", lambda e: e.tensor_tensor(out=oT[:, n, :], in0=xc[:, 0:T], in1=gy[:], op=ALU.mult), r=[xcb, gyb], w=[oTb[n]])
                    dma("sp", O["rg_h_p"][l, n * 128:(n + 1) * 128].rearrange("(p o) -> p o", o=1), xc[:, T - 1:T], r=[xcb], is_out=True)
                    op("dve", lambda e: e.tensor_tensor(out=gs[:, 1, :], in0=gs[:, 1, :], in1=xcs[:, 0, :], op=ALU.mult), r=[gsb, xcsb], w=[gsb])
                    op("dve", lambda e: e.tensor_tensor(out=gs[:, 2, :], in0=gs[:, 0, :], in1=gs[:, 0, :], op=ALU.mult), r=[gsb], w=[gsb])
                    op("act", lambda e: e.activation(out=gs[:, 2, :], in_=gs[:, 2, :], func=AF.Sqrt, scale=-1.0, bias=1.0), r=[gsb], w=[gsb])
                    op("dve", lambda e: e.tensor_tensor(out=gs[:, 1, :], in0=gs[:, 1, :], in1=gs[:, 2, :], op=ALU.mult), r=[gsb], w=[gsb])
                    op("dve", lambda e: e.tensor_tensor(out=gs[:, 3, :], in0=gs[:, 0, :], in1=sh0[:, n, :], op=ALU.mult), r=[gsb, Bst], w=[gsb])
                    op("dve", lambda e: e.tensor_tensor(out=gs[:, 3, :], in0=gs[:, 3, :], in1=gs[:, 1, :], op=ALU.add), r=[gsb], w=[gsb])
                    dma("sp", O["rg_h_s"][l, :, n * 128:(n + 1) * 128].rearrange("s p -> p s"), gs[:, 3, :], r=[gsb], is_out=True)
                    op("dve", lambda e: e.tensor_tensor(out=osT[:, n, :], in0=gs[:, 3, :], in1=gys[:, 0, :], op=ALU.mult), r=[gsb, gysb], w=[Bos])
                kb.barrier()
            kb.es = es
            branch_proj(l, "wrgp", 10, CI_GATE, True)
            kb.barrier()
            if stage < 2:
                continue

            with ExitStack() as es3:
                kb.es = es3
                print("sbuf remaining before GDN", nc.sbuf_bytes_remaining)
                TE = T + NS * 128
                NB = 16 + NS
                NC8 = NB * 8
                gab = kb.sb("gab", [8, 2], F32)
                negA = kb.sb("negA", [8, 1], F32)
                names_ = ["g_tok", "b_tok", "gcum", "eg", "negeg", "ed", "eglast", "negb"]
                G = {n_: kb.sb(n_, [128, NC8], F32) for n_ in names_}
                es3a = ExitStack()
                kb.es = es3a
                g_fm = kb.sb("g_fm", [8, TE], F32)
                b_fm = kb.sb("b_fm", [8, TE], F32)
                kb.es = es3
                Bg = Buf("gdn_g")
                dma("sp", gab[:], I["gdn_ab"][l].rearrange("a h -> h a"), w=[Bg])
                op("act", lambda e: e.activation(out=negA[:], in_=gab[:, 0:1], func=AF.Exp), r=[Bg], w=[Bg])
                op("dve", lambda e: e.tensor_scalar(out=negA[:], in0=negA[:], scalar1=-1.0, scalar2=None, op0=ALU.mult), r=[Bg], w=[Bg])
                op("pool", lambda e: e.memset(g_fm[:], 0.0), w=[Bg])
                op("pool", lambda e: e.memset(b_fm[:], 0.0), w=[Bg])
                wt, wtb = wpiece.get()
                dma("pool", wt[:], I["win"][l, CI_AB], w=[wtb])
                scol = lambda t_: t_[:, T:TE].rearrange("p (s t) -> p s t", t=128)[:, :, 0]
                for tg in range(5):
                    sl = slice(tg * 512, (tg + 1) * 512)
                    for (c0, dst, fn, bias) in ((0, g_fm, AF.Exp, gab[:, 1:2]), (8, b_fm, AF.Sigmoid, None)):
                        pa, pab = kb.bank()
                        for k in range(8):
                            if tg < 4:
                                op("pe", lambda e: e.matmul(pa[0:8, :], lhsT=wt[:, k, c0:c0 + 8], rhs=hT[:, k, sl], start=(k == 0), stop=(k == 7)), r=[wtb] + hbuf[tg * 4:tg * 4 + 4], w=[pab], inc=(k == 7))
                            else:
                                op("pe", lambda e: e.matmul(pa[0:8, 0:NS], lhsT=wt[:, k, c0:c0 + 8], rhs=hsT[:, k, :], start=(k == 0), stop=(k == 7)), r=[wtb, Bhs], w=[pab], inc=(k == 7))
                        src = pa[0:8, :] if tg < 4 else pa[0:8, 0:NS]
                        dstv = dst[:, sl] if tg < 4 else scol(dst)
                        if bias is not None:
                            op("act", lambda e: e.activation(out=dstv, in_=src, func=fn, bias=bias), r=[pab, Bg], w=[Bg])
                        else:
                            op("act", lambda e: e.activation(out=dstv, in_=src, func=fn), r=[pab, Bg], w=[Bg])
                op("act", lambda e: e.activation(out=g_fm[:], in_=g_fm[:], func=AF.Ln, bias=1.0), r=[Bg], w=[Bg])
                op("dve", lambda e: e.tensor_scalar(out=g_fm[:], in0=g_fm[:], scalar1=negA[:, 0:1], scalar2=None, op0=ALU.mult), r=[Bg], w=[Bg])
                for (srcfm, dstn) in ((g_fm, "g_tok"), (b_fm, "b_tok")):
                    pt, ptb = kb.bank()
                    for nb in range(NB):
                        op("pe", lambda e: e.transpose(pt[:, nb * 8:(nb + 1) * 8], srcfm[:, nb * 128:(nb + 1) * 128], ident_f[0:8, 0:8]), r=[Bg, Bc], w=[ptb], inc=(nb == NB - 1))
                    op("dve", lambda e: e.tensor_copy(out=G[dstn][:], in_=pt[:, 0:NC8]), r=[ptb], w=[Bg])
                pt, ptb = kb.bank()
                op("pe", lambda e: e.matmul(pt[:, 0:NC8], lhsT=gm[:, 0, :], rhs=G["g_tok"][:], start=True, stop=True), r=[Bg, Bc], w=[ptb])
                op("dve", lambda e: e.tensor_copy(out=G["gcum"][:], in_=pt[:, 0:NC8]), r=[ptb], w=[Bg])
                op("act", lambda e: e.activation(out=G["eg"][:], in_=G["gcum"][:], func=AF.Exp), r=[Bg], w=[Bg])
                op("dve", lambda e: e.tensor_scalar(out=G["negeg"][:], in0=G["eg"][:], scalar1=-1.0, scalar2=None, op0=ALU.mult), r=[Bg], w=[Bg])
                op("dve", lambda e: e.tensor_scalar(out=G["negb"][:], in0=G["b_tok"][:], scalar1=-1.0, scalar2=None, op0=ALU.mult), r=[Bg], w=[Bg])
                pt, ptb = kb.bank()
                op("pe", lambda e: e.matmul(pt[:, 0:NC8], lhsT=gm[:, 1, :], rhs=G["gcum"][:], start=True, stop=True), r=[Bg, Bc], w=[ptb])
                op("act", lambda e: e.activation(out=G["eglast"][:], in_=pt[:, 0:NC8], func=AF.Exp), r=[ptb], w=[Bg])
                op("dve", lambda e: e.tensor_tensor(out=G["ed"][:], in0=pt[:, 0:NC8], in1=G["gcum"][:], op=ALU.subtract), r=[ptb, Bg], w=[Bg])
                op("act", lambda e: e.activation(out=G["ed"][:], in_=G["ed"][:], func=AF.Exp), r=[Bg], w=[Bg])
                kb.barrier()
                es3a.close()
                gcv = kb.sb("gcv", [128, 24, NS, 3], F32)
                Bgcv = Buf("gcv")
                for ch in range(24):
                    dma("sp", gcv[:, ch], I["st_gdn_conv"][l, :, :, ch * 128:(ch + 1) * 128].rearrange("s j p -> p s j"), w=[Bgcv])
                f32p = Rot(kb, "gdf", [128, T + 3], F32, 2)
                csp = Rot(kb, "gdcs", [128, TE], BF16, 4)
                tokp = Rot(kb, "gdtok", [128, NB, 128], BF16, 4)
                smallp = Rot(kb, "gds", [128, 8, NS], F32, 4)
                m32 = Rot(kb, "m32_", [128, 128], F32, 6)
                mlong = Rot(kb, "mlong_", [128, 128], BF16, 22)

                class _Carved:
                    pass
                m16 = _Carved()
                m16.t = [oT[:, 8 + (i_ // 16), (i_ % 16) * 128:(i_ % 16 + 1) * 128] for i_ in range(32)]
                m16.b = [Buf("m16c%d" % i_) for i_ in range(32)]
                m16.i = 0
                m16.get = lambda: Rot.get(m16)
                ssq = kb.sb("ssq", [128, 4, NB], F32)
                Bssq = Buf("ssq")
                sqt = kb.sb("sqt", [128, 8, 128], BF16)
                Bsqt = Buf("sqt")
                S_f = kb.sb("S_f", [128, 128], F32)
                S_b = kb.sb("S_b", [128, 128], BF16)
                BS = Buf("S")
                st4 = kb.sb("st4", [128, 4], F32)
                Bst4 = Buf("st4")

                def to_tok(src_fm, srcb, dst_tok, dstb):
                    for g0 in range(0, NB, 8):
                        ng = min(8, NB - g0)
                        pt, ptb = kb.bank()
                        ptv = pt[:].bitcast(BF16)
                        for i in range(ng):
                            nb = g0 + i
                            op("pe", lambda e: e.transpose(ptv[:, i * 128:(i + 1) * 128], src_fm[:, nb * 128:(nb + 1) * 128], ident_b[:]), r=[srcb, Bc], w=[ptb], inc=(i == ng - 1))
                        op("act", lambda e: e.activation(out=dst_tok[:, g0:g0 + ng, :].rearrange("p a b -> p (a b)"), in_=ptv[:, 0:ng * 128], func=AF.Copy), r=[ptb], w=[dstb])

                def to_fm(src_tok, srcb, dst_fm, dstb):
                    for g0 in range(0, NB, 8):
                        ng = min(8, NB - g0)
                        pt, ptb = kb.bank()
                        ptv = pt[:].bitcast(BF16)
                        for i in range(ng):
                            nb = g0 + i
                            op("pe", lambda e: e.transpose(ptv[:, i * 128:(i + 1) * 128], src_tok[:, nb, :], ident_b[:]), r=[srcb, Bc], w=[ptb], inc=(i == ng - 1))
                        op("dve", lambda e: e.tensor_copy(out=dst_fm[:, g0 * 128:(g0 + ng) * 128], in_=ptv[:, 0:ng * 128]), r=[ptb], w=[dstb])

                for h in range(H):
                    cs = {}
                    for pi, (pname, ci0) in enumerate((("q", CI_Q), ("k", CI_K), ("v", CI_V))):
                        ch = pi * 8 + h
                        ux, uxb = f32p.get()
                        op("pool", lambda e: e.memset(ux[:, 0:3], 0.0), w=[uxb])
                        uxs, uxsb = smallp.get()

                        def ev_x(tg, pt, pbuf):
                            op("act", lambda e: e.activation(out=ux[:, 3 + tg * 512:3 + (tg + 1) * 512], in_=pt[:], func=AF.Copy), r=[pbuf], w=[uxb])

                        def ev_xs(pt, pbuf):
                            op("act", lambda e: e.activation(out=uxs[:, 0, :], in_=pt[:, 0:NS], func=AF.Copy), r=[pbuf], w=[uxsb])
                        inproj(l, ci0 + h, hT, hbuf, 128, ev_x, ev_xs)
                        dma("sp", O["gdn_conv_p"][l, :, ch * 128:(ch + 1) * 128].rearrange("j p -> p j"), ux[:, T:T + 3], r=[uxb], is_out=True)
                        xc, xcb = f32p.get()
                        cw = lambda j: vecT[l][:, V_GDCW + j * 24 + ch:V_GDCW + j * 24 + ch + 1]
                        eng = "dve" if pi != 1 else "pool"
                        op(eng, lambda e: e.tensor_scalar(out=xc[:, 0:T], in0=ux[:, 3:3 + T], scalar1=cw(3), scalar2=None, op0=ALU.mult), r=[uxb, Bmod[l]], w=[xcb])
                        for j in range(3):
                            op("dve", lambda e: e.scalar_tensor_tensor(out=xc[:, 0:T], in0=ux[:, j:j + T], scalar=cw(j), in1=xc[:, 0:T], op0=ALU.mult, op1=ALU.add),
                               r=[uxb, xcb, Bmod[l]], w=[xcb])
                        c_t, c_b = csp.get()
                        op("pool", lambda e: e.memset(c_t[:, T:TE], 0.0), w=[c_b])
                        op("act", lambda e: e.activation(out=c_t[:, 0:T], in_=xc[:, 0:T], func=AF.Silu), r=[xcb], w=[c_b])
                        op("dve", lambda e: e.tensor_scalar(out=uxs[:, 1, :], in0=uxs[:, 0, :], scalar1=cw(3), scalar2=None, op0=ALU.mult), r=[uxsb, Bmod[l]], w=[uxsb])
                        for j in range(3):
                            op("dve", lambda e: e.scalar_tensor_tensor(out=uxs[:, 1, :], in0=gcv[:, ch, :, j], scalar=cw(j), in1=uxs[:, 1, :], op0=ALU.mult, op1=ALU.add),
                               r=[Bgcv, uxsb, Bmod[l]], w=[uxsb])
                        op("act", lambda e: e.activation(out=scol(c_t), in_=uxs[:, 1, :], func=AF.Silu), r=[uxsb], w=[c_b])
                        for j in range(2):
                            dma("sp", O["gdn_conv_s"][l, :, j, ch * 128:(ch + 1) * 128].rearrange("s p -> p s"), gcv[:, ch, :, j + 1], r=[Bgcv], is_out=True)
                        dma("sp", O["gdn_conv_s"][l, :, 2, ch * 128:(ch + 1) * 128].rearrange("s p -> p s"), uxs[:, 0, :], r=[uxsb], is_out=True)
                        cs[pname] = (c_t, c_b)
                    zs, zsb = csp.get()
                    op("pool", lambda e: e.memset(zs[:, T:TE], 0.0), w=[zsb])

                    def ev_z(tg, pt, pbuf):
                        op("act", lambda e: e.activation(out=zs[:, tg * 512:(tg + 1) * 512], in_=pt[:], func=AF.Silu), r=[pbuf], w=[zsb])

                    def ev_zs(pt, pbuf):
                        op("act", lambda e: e.activation(out=scol(zs), in_=pt[:, 0:NS], func=AF.Silu), r=[pbuf], w=[zsb])
                    inproj(l, CI_Z + h, hT, hbuf, 128, ev_z, ev_zs)
                    toks = {}
                    for pi, pname in enumerate(("q", "k", "v")):
                        tt, ttb = tokp.get()
                        to_tok(cs[pname][0], cs[pname][1], tt, ttb)
                        toks[pname] = (tt, ttb)
                        if pname == "v":
                            continue
                        for g0 in range(0, NB, 8):
                            ng = min(8, NB - g0)
                            op("dve", lambda e: e.tensor_tensor(out=sqt[:, 0:ng, :], in0=tt[:, g0:g0 + ng, :], in1=tt[:, g0:g0 + ng, :], op=ALU.mult), r=[ttb, Bsqt], w=[Bsqt])
                            op("dve", lambda e: e.tensor_reduce(out=ssq[:, pi, g0:g0 + ng], in_=sqt[:, 0:ng, :], op=ALU.add, axis=AX.X), r=[Bsqt, Bssq], w=[Bssq])
                        op("dve", lambda e: e.tensor_scalar(out=ssq[:, pi, :], in0=ssq[:, pi, :], scalar1=EPS, scalar2=None, op0=ALU.add), r=[Bssq], w=[Bssq])
                        op("act", lambda e: e.activation(out=ssq[:, pi, :], in_=ssq[:, pi, :], func=AF.Sqrt), r=[Bssq], w=[Bssq])
                        op("dve", lambda e: e.reciprocal(out=ssq[:, 2 + pi, :], in_=ssq[:, pi, :]), r=[Bssq], w=[Bssq])
                        if pname == "q":
                            op("dve", lambda e: e.tensor_scalar(out=ssq[:, 2, :], in0=ssq[:, 2, :], scalar1=128.0 ** -0.5, scalar2=None, op0=ALU.mult), r=[Bssq], w=[Bssq])
                        op("pool", lambda e: e.tensor_tensor(out=tt[:], in0=tt[:], in1=ssq[:, 2 + pi, :].unsqueeze(2).to_broadcast([128, NB, 128]), op=ALU.mult), r=[ttb, Bssq], w=[ttb])
                    qT, qTb = csp.get()
                    kT, kTb = csp.get()
                    to_fm(toks["q"][0], toks["q"][1], qT, qTb)
                    to_fm(toks["k"][0], toks["k"][1], kT, kTb)
                    k_tok, k_tokb = toks["k"]
                    v_tok, v_tokb = toks["v"]
                    kd, kdb = tokp.get()
                    op("pool", lambda e: e.tensor_tensor(out=kd[:], in0=k_tok[:], in1=G["ed"][:].rearrange("p (n h) -> p n h", h=8)[:, :, h:h + 1].to_broadcast([128, NB, 128]), op=ALU.mult),
                       r=[k_tokb, Bg], w=[kdb])
                    def prelude(nb):
                        bs = slice(nb * 128, (nb + 1) * 128)
                        col = nb * 8 + h
                        cc_ = lambda nm: G[nm][:, col:col + 1]
                        gb, gbb = m32.get()
                        op("pool", lambda e: e.tensor_scalar(out=gb[:], in0=ones_f[:], scalar1=cc_("g_tok"), scalar2=None, op0=ALU.mult), r=[Bg, Bc], w=[gbb])
                        pG, pGb = kb.bank()
                        op("pe", lambda e: e.matmul(pG[:, 0:128], lhsT=gb[:], rhs=gm[:, 0, :], start=True, stop=True), r=[gbb, Bc], w=[pGb])
                        dTs, dTsb = m32.get()
                        dTi, dTib = m32.get()
                        op("dve", lambda e: e.scalar_tensor_tensor(out=dTs[:], in0=pG[:, 0:128], scalar=cc_("gcum"), in1=gm[:, 2, :], op0=ALU.subtract, op1=ALU.subtract), r=[pGb, Bg, Bc], w=[dTsb])
                        op("dve", lambda e: e.scalar_tensor_tensor(out=dTi[:], in0=pG[:, 0:128], scalar=cc_("gcum"), in1=gm[:, 3, :], op0=ALU.subtract, op1=ALU.subtract), r=[pGb, Bg, Bc], w=[dTib])
                        op("act", lambda e: e.activation(out=dTs[:], in_=dTs[:], func=AF.Exp), r=[dTsb], w=[dTsb])
                        op("act", lambda e: e.activation(out=dTi[:], in_=dTi[:], func=AF.Exp), r=[dTib], w=[dTib])
                        pK, pKb = kb.bank()
                        op("pe", lambda e: e.matmul(pK[:, 0:128], lhsT=kT[:, bs], rhs=kT[:, bs], start=True, stop=True), r=[kTb], w=[pKb])
                        P, Pb = mlong.get()
                        op("dve", lambda e: e.scalar_tensor_tensor(out=P[:], in0=pK[:, 0:128], scalar=cc_("negb"), in1=dTs[:], op0=ALU.mult, op1=ALU.mult), r=[pKb, Bg, dTsb], w=[Pb])
                        pQ, pQb = kb.bank()
                        op("pe", lambda e: e.matmul(pQ[:, 0:128], lhsT=kT[:, bs], rhs=qT[:, bs], start=True, stop=True), r=[kTb, qTb], w=[pQb])
                        AqT, AqTb = mlong.get()
                        op("dve", lambda e: e.tensor_tensor(out=AqT[:], in0=pQ[:, 0:128], in1=dTi[:], op=ALU.mult), r=[pQb, dTib], w=[AqTb])
                        pT_, pT_b = kb.bank()
                        pTv = pT_[:].bitcast(BF16)
                        op("pe", lambda e: e.transpose(pTv[:, 0:128], P[:], ident_b[:]), r=[Pb, Bc], w=[pT_b])
                        PT, PTb = mlong.get()
                        op("act", lambda e: e.activation(out=PT[:], in_=pTv[:, 0:128], func=AF.Copy), r=[pT_b], w=[PTb])
                        A0, A0b = m16.get()
                        A0T, A0Tb = m16.get()
                        op("pool", lambda e: e.tensor_tensor(out=A0[:], in0=P[:], in1=lm[:, 0, :], op=ALU.mult), r=[Pb, Bc], w=[A0b])
                        op("pool", lambda e: e.tensor_tensor(out=A0T[:], in0=PT[:], in1=lm[:, 1, :], op=ALU.mult), r=[PTb, Bc], w=[A0Tb])
                        X, Xb = m16.get()
                        XT, XTb = m16.get()
                        op("dve", lambda e: e.tensor_tensor(out=X[:], in0=A0[:], in1=ident_b[:], op=ALU.add), r=[A0b, Bc], w=[Xb])
                        op("dve", lambda e: e.tensor_tensor(out=XT[:], in0=A0T[:], in1=ident_b[:], op=ALU.add), r=[A0Tb, Bc], w=[XTb])
                        return dict(P=P, Pb=Pb, PT=PT, PTb=PTb, AqT=AqT, AqTb=AqTb, X=X, Xb=Xb, XT=XT, XTb=XTb)

                    def level(c, lev):
                        P, Pb, PT, PTb, X, Xb, XT, XTb = c["P"], c["Pb"], c["PT"], c["PTb"], c["X"], c["Xb"], c["XT"], c["XTb"]
                        As, Asb = m16.get()
                        AsT, AsTb = m16.get()
                        op("pool", lambda e: e.tensor_tensor(out=AsT[:], in0=PT[:], in1=lm[:, 2 * lev + 1, :], op=ALU.mult), r=[PTb, Bc], w=[AsTb])
                        pA, pAb = kb.bank()
                        op("pe", lambda e: e.matmul(pA[:, 0:128], lhsT=AsT[:], rhs=X[:], start=True, stop=True), r=[AsTb, Xb], w=[pAb])
                        Y, Yb = m16.get()
                        op("act", lambda e: e.activation(out=Y[:], in_=pA[:, 0:128], func=AF.Copy), r=[pAb], w=[Yb])
                        if lev < 6:
                            op("pool", lambda e: e.tensor_tensor(out=As[:], in0=P[:], in1=lm[:, 2 * lev, :], op=ALU.mult), r=[Pb, Bc], w=[Asb])
                            pB, pBb = kb.bank()
                            op("pe", lambda e: e.matmul(pB[:, 0:128], lhsT=As[:], rhs=XT[:], start=True, stop=True), r=[Asb, XTb], w=[pBb])
                            Y2, Y2b = m16.get()
                            op("dve", lambda e: e.tensor_copy(out=Y2[:], in_=pB[:, 0:128]), r=[pBb], w=[Y2b])
                        pX, pXb = kb.bank()
                        op("pe", lambda e: e.matmul(pX[:, 0:128], lhsT=XT[:], rhs=Y[:], start=True, stop=True), r=[XTb, Yb], w=[pXb])
                        Xn, Xnb = m16.get()
                        op("dve", lambda e: e.tensor_tensor(out=Xn[:], in0=pX[:, 0:128], in1=X[:], op=ALU.add), r=[pXb, Xb], w=[Xnb])
                        if lev < 6:
                            pZ, pZb = kb.bank()
                            op("pe", lambda e: e.matmul(pZ[:, 0:128], lhsT=X[:], rhs=Y2[:], start=True, stop=True), r=[Xb, Y2b], w=[pZb])
                            XTn, XTnb = m16.get()
                            op("dve", lambda e: e.tensor_tensor(out=XTn[:], in0=pZ[:, 0:128], in1=XT[:], op=ALU.add), r=[pZb, XTb], w=[XTnb])
                            c["XT"], c["XTb"] = XTn, XTnb
                        c["X"], c["Xb"] = Xn, Xnb

                    def partB(chain, nb, c):
                        bs = slice(nb * 128, (nb + 1) * 128)
                        col = nb * 8 + h
                        cc_ = lambda nm: G[nm][:, col:col + 1]
                        X, Xb, AqT, AqTb = c["X"], c["Xb"], c["AqT"], c["AqTb"]
                        pS, pSb = kb.bank()
                        op("pe", lambda e: e.matmul(pS[:, 0:128], lhsT=kT[:, bs], rhs=S_b[:], start=True, stop=True), r=[kTb, BS], w=[pSb])
                        rr, rrb = m16.get()
                        op("dve", lambda e: e.scalar_tensor_tensor(out=rr[:], in0=pS[:, 0:128], scalar=cc_("negeg"), in1=v_tok[:, nb, :], op0=ALU.mult, op1=ALU.add), r=[pSb, Bg, v_tokb], w=[rrb])
                        pW, pWb = kb.bank()
                        op("pe", lambda e: e.matmul(pW[:, 0:128], lhsT=X[:], rhs=rr[:], start=True, stop=True), r=[Xb, rrb], w=[pWb])
                        U, Ub = m16.get()
                        op("act", lambda e: e.activation(out=U[:], in_=pW[:, 0:128], func=AF.Copy, scale=cc_("b_tok")), r=[pWb, Bg], w=[Ub])
                        pO, pOb = kb.bank()
                        op("pe", lambda e: e.matmul(pO[:, 0:128], lhsT=qT[:, bs], rhs=S_b[:], start=True, stop=True), r=[qTb, BS], w=[pOb], inc=False)
                        op("pe", lambda e: e.matmul(pO[:, 128:256], lhsT=AqT[:], rhs=U[:], start=True, stop=True), r=[AqTb, Ub], w=[pOb])
                        o1, o1b = m32.get()
                        op("act", lambda e: e.activation(out=o1[:], in_=pO[:, 128:256], func=AF.Copy), r=[pOb], w=[o1b])
                        op("dve", lambda e: e.scalar_tensor_tensor(out=o1[:], in0=pO[:, 0:128], scalar=cc_("eg"), in1=o1[:], op0=ALU.mult, op1=ALU.add), r=[pOb, Bg, o1b], w=[o1b])
                        pU, pUb = kb.bank()
                        op("pe", lambda e: e.matmul(pU[:, 0:128], lhsT=kd[:, nb, :], rhs=U[:], start=True, stop=True), r=[kdb, Ub], w=[pUb])
                        op("dve", lambda e: e.scalar_tensor_tensor(out=S_f[:], in0=S_f[:], scalar=cc_("eglast"), in1=pU[:, 0:128], op0=ALU.mult, op1=ALU.add), r=[pUb, Bg, BS], w=[BS])
                        op("act", lambda e: e.activation(out=S_b[:], in_=S_f[:], func=AF.Copy), r=[BS], w=[BS])
                        jk, jkb = m32.get()
                        op("act", lambda e: e.activation(out=jk[:], in_=o1[:], func=AF.Square, accum_out=st4[:, 0:1]), r=[o1b], w=[jkb, Bst4])
                        op("dve", lambda e: e.tensor_scalar(out=st4[:, 1:2], in0=st4[:, 0:1], scalar1=1.0 / 128, scalar2=EPS, op0=ALU.mult, op1=ALU.add), r=[Bst4], w=[Bst4])
                        op("act", lambda e: e.activation(out=st4[:, 2:3], in_=st4[:, 1:2], func=AF.Sqrt), r=[Bst4], w=[Bst4])
                        op("dve", lambda e: e.reciprocal(out=st4[:, 3:4], in_=st4[:, 2:3]), r=[Bst4], w=[Bst4])
                        on, onb = m16.get()
                        op("act", lambda e: e.activation(out=on[:], in_=o1[:], func=AF.Copy, scale=st4[:, 3:4]), r=[o1b, Bst4], w=[onb])
                        pN, pNb = kb.bank()
                        pNv = pN[:].bitcast(BF16)
                        op("pe", lambda e: e.transpose(pNv[:, 0:128], on[:], ident_b[:]), r=[onb, Bc], w=[pNb])
                        gcol = vecT[l][:, V_GDG:V_GDG + 1]
                        if chain == 0:
                            op("dve", lambda e: e.scalar_tensor_tensor(out=oT[:, h, bs], in0=pNv[:, 0:128], scalar=gcol, in1=zs[:, bs], op0=ALU.mult, op1=ALU.mult), r=[pNb, Bmod[l], zsb], w=[oTb[h]])
                        else:
                            s_ = chain - 1
                            op("dve", lambda e: e.scalar_tensor_tensor(out=osT[:, h, s_:s_ + 1], in0=pNv[:, 0:1], scalar=gcol, in1=zs[:, nb * 128:nb * 128 + 1], op0=ALU.mult, op1=ALU.mult),
                               r=[pNb, Bmod[l], zsb], w=[Bos])

                    blocks_all = [(0, nb) for nb in range(16)] + [(1 + s_i, 16 + s_i) for s_i in range(NS)]
                    GW = 3
                    for g0 in range(0, len(blocks_all), GW):
                        grp = blocks_all[g0:g0 + GW]
                        ctxs = [prelude(nb) for (_, nb) in grp]
                        for lev in range(1, 7):
                            for c in ctxs:
                                level(c, lev)
                        for (chain, nb), c in zip(grp, ctxs):
                            if chain == 0 and nb == 0:
                                op("pool", lambda e: e.memset(S_f[:], 0.0), w=[BS])
                                op("pool", lambda e: e.memset(S_b[:], 0.0), w=[BS])
                            elif chain > 0:
                                dma("sp", S_f[:], I["st_gdn_S"][l, chain - 1, h], w=[BS])
                                op("act", lambda e: e.activation(out=S_b[:], in_=S_f[:], func=AF.Copy), r=[BS], w=[BS])
                            partB(chain, nb, c)
                            if chain == 0 and nb == 15:
                                dma("sp", O["gdn_S_p"][l, h], S_f[:], r=[BS], is_out=True)
                            elif chain > 0:
                                dma("sp", O["gdn_S_s"][l, chain - 1, h], S_f[:], r=[BS], is_out=True)
                kb.barrier()
            kb.es = es
            branch_proj(l, "wgdp", 8, CI_GATE + 8, False)
            kb.barrier()
            if stage < 3:
                continue

            with ExitStack() as es4:
                kb.es = es4
                print("sbuf remaining before MLA", nc.sbuf_bytes_remaining)
                ropes = kb.sb("ropes", [96, 2], F32)
                rfT_f = kb.sb("rfT_f", [96, 96], F32)
                rfT_b = kb.sb("rfT_b", [96, 96], BF16)
                Bmc = Buf("mla_const")
                dma("sp", ropes[:, 0:1], I["ropeCs"], w=[Bmc])
                dma("sp", ropes[:, 1:2], I["ropeSs"], w=[Bmc])
                dma("sp", rfT_f[:], I["rfullT"], w=[Bmc])
                op("dve", lambda e: e.tensor_copy(out=rfT_b[:], in_=rfT_f[:]), r=[Bmc], w=[Bmc])
                ckvT = kb.sb("ckvT", [128, 2, T], BF16)
                krT = kb.sb("krT", [96, T], BF16)
                Bckv, Bkr, Bcq = Buf("ckvT"), Buf("krT"), Buf("cqT")
                ukvs = kb.sb("ukvs", [128, 2, NS], F32)
                ckvs_f = kb.sb("ckvs_f", [128, 2, NS], F32)
                ckvs_b = kb.sb("ckvs_b", [128, 2, NS], BF16)
                krs = kb.sb("krs", [96, 3, NS], F32)
                krs_b = kb.sb("krs_b", [96, NS], BF16)
                uqs = kb.sb("uqs", [128, 3, NS], F32)
                cqs_b = kb.sb("cqs_b", [128, 3, NS], BF16)
                Bsm = Buf("mla_s")
                def fm_rms(u_views, ubufs, nfeat, gcol0, outs, obufs, W):
                    nch = len(u_views)
                    sq, sqb = sqp.get()
                    for c in range(nch):
                        op("pool", lambda e: e.tensor_tensor(out=sq[:, c, 0:W], in0=u_views[c], in1=u_views[c], op=ALU.mult), r=ubufs, w=[sqb])
                    ps, psb = kb.bank()
                    for c in range(nch):
                        op("pe", lambda e: e.matmul(ps[:, 0:W], lhsT=ones_b[:], rhs=sq[:, c, 0:W], start=(c == 0), stop=(c == nch - 1)), r=[sqb, Bc], w=[psb], inc=(c == nch - 1))
                    rs, rsb = rsp.get()
                    op("act", lambda e: e.activation(out=rs[:, 0:W], in_=ps[:, 0:W], func=AF.Sqrt, scale=1.0 / nfeat, bias=EPS), r=[psb], w=[rsb])
                    op("dve", lambda e: e.reciprocal(out=rs[:, 0:W], in_=rs[:, 0:W]), r=[rsb], w=[rsb])
                    for c in range(nch):
                        for (o_ap, o_b) in zip(outs[c], obufs):
                            op("dve", lambda e: e.scalar_tensor_tensor(out=o_ap, in0=u_views[c], scalar=vecT[l][:, gcol0 + c:gcol0 + c + 1], in1=rs[:, 0:W], op0=ALU.mult, op1=ALU.mult),
                               r=ubufs + [rsb, Bmod[l]], w=[o_b])

                with ExitStack() as es4a:
                    kb.es = es4a
                    ropeC = kb.sb("ropeC", [96, T], F32)
                    ropeS = kb.sb("ropeS", [96, T], F32)
                    dma("sp", ropeC[:], I["ropeC"], w=[Bmc])
                    dma("sp", ropeS[:], I["ropeS"], w=[Bmc])
                    sqp = Rot(kb, "sqp", [128, 3, 512], BF16, 2)
                    rsp = Rot(kb, "rsp", [128, 512], F32, 2)
                    ukv = kb.sb("ukv", [128, 2, T], F32)
                    kr96 = kb.sb("kr96", [96, T], F32)
                    krow = kb.sb("krow", [32, T], BF16)
                    krs_o = kb.sb("krs_o", [32, NS], F32)
                    Bkrow = Buf("krow")
                    Bukv, Bkr96 = Buf("ukv"), Buf("kr96")
                    stg = Rot(kb, "ckvstg", [128, 256], F32, 2)
                    for c in range(2):
                        def ev(tg, pt, pbuf, c=c):
                            op("act", lambda e: e.activation(out=ukv[:, c, tg * 512:(tg + 1) * 512], in_=pt[:], func=AF.Copy), r=[pbuf], w=[Bukv])

                        def evs(pt, pbuf, c=c):
                            op("act", lambda e: e.activation(out=ukvs[:, c, :], in_=pt[:, 0:NS], func=AF.Copy), r=[pbuf], w=[Bsm])
                        inproj(l, CI_CKV + c, hT, hbuf, 128, ev, evs)

                    def ev(tg, pt, pbuf):
                        op("act", lambda e: e.activation(out=kr96[:, tg * 512:(tg + 1) * 512], in_=pt[0:96, :], func=AF.Copy), r=[pbuf], w=[Bkr96])

                    def evs(pt, pbuf):
                        op("act", lambda e: e.activation(out=krs[:, 0, :], in_=pt[0:96, 0:NS], func=AF.Copy), r=[pbuf], w=[Bsm])
                    inproj(l, CI_KR, hT, hbuf, 96, ev, evs)
                    SUB = int(os.environ.get("MLA_SUB", "9"))
                    for tg in range(4 if SUB >= 2 else 0):
                        sl = slice(tg * 512, (tg + 1) * 512)
                        fm_rms([ukv[:, c, sl] for c in range(2)], [Bukv], KVL, V_KVG, [[ckvT[:, c, sl], ukv[:, c, sl]] for c in range(2)], [Bckv, Bukv], 512)
                        if SUB < 3:
                            continue
                        pr, prb = kb.bank()
                        op("act", lambda e: e.activation(out=krT[:, sl], in_=kr96[:, sl], func=AF.Copy), r=[Bkr96], w=[Bkr])
                        op("pe", lambda e: e.matmul(pr[0:96, :], lhsT=rfT_b[:], rhs=krT[:, sl], start=True, stop=True), r=[Bmc, Bkr], w=[prb])
                        t2, t2b = rsp.get()
                        op("dve", lambda e: e.tensor_tensor(out=t2[0:96, :], in0=pr[0:96, :], in1=ropeS[:, sl], op=ALU.mult), r=[prb, Bmc], w=[t2b])
                        op("pool", lambda e: e.tensor_tensor(out=kr96[:, sl], in0=kr96[:, sl], in1=ropeC[:, sl], op=ALU.mult), r=[Bkr96, Bmc], w=[Bkr96])
                        op("pool", lambda e: e.tensor_tensor(out=kr96[:, sl], in0=kr96[:, sl], in1=t2[0:96, :], op=ALU.add), r=[Bkr96, t2b], w=[Bkr96])
                        op("act", lambda e: e.activation(out=krT[:, sl], in_=kr96[:, sl], func=AF.Copy), r=[Bkr96, prb], w=[Bkr])
                        op("dve", lambda e: e.tensor_copy(out=krow[0:32, sl], in_=kr96[64:96, sl]), r=[Bkr96], w=[Bkrow])
                    for tb in range(16 if SUB >= 4 else 0):
                        bs = slice(tb * 128, (tb + 1) * 128)
                        kb.barrier()
                        pt_, ptb = kb.bank()
                        pt = pt_[:].bitcast(BF16)
                        for c in range(2):
                            op("pe", lambda e: e.transpose(pt[:, c * 128:(c + 1) * 128], ckvT[:, c, bs], ident_b[:]), r=[Bckv, Bc], w=[ptb], inc=False)
                        op("pe", lambda e: e.transpose(pt[:, 256:288], krow[0:32, bs], ident_b[0:32, 0:32]), r=[Bkrow, Bc], w=[ptb])
                        st, stb = stg.get()
                        op("act", lambda e: e.activation(out=st[:], in_=pt[:, 0:256], func=AF.Copy), r=[ptb], w=[stb])
                        if os.environ.get("NODMA", "0") in ("0", "2"):
                            dma("sp", O["ckv_p"][l, bs, :], st[:], r=[stb], is_out=True)
                        st2, st2b = stg.get()
                        op("dve", lambda e: e.tensor_copy(out=st2[:, 0:32], in_=pt[:, 256:288]), r=[ptb], w=[st2b])
                        if os.environ.get("NODMA", "0") in ("0", "3"):
                            dma("pool", O["krope_p"][l, bs, :], st2[:, 0:32], r=[st2b], is_out=True)
                    if SUB < 5:
                        kb.barrier()
                        kb.es = es4
                        break
                    fm_rms([ukvs[:, c, :] for c in range(2)], [Bsm], KVL, V_KVG, [[ckvs_f[:, c, :], ckvs_b[:, c, :]] for c in range(2)], [Bsm, Bsm], NS)
                    pr, prb = kb.bank()
                    op("pe", lambda e: e.matmul(pr[0:96, 0:NS], lhsT=rfT_f[:], rhs=krs[:, 0, :], start=True, stop=True), r=[Bmc, Bsm], w=[prb])
                    op("dve", lambda e: e.tensor_scalar(out=krs[:, 1, :], in0=pr[0:96, 0:NS], scalar1=ropes[:, 1:2], scalar2=None, op0=ALU.mult), r=[prb, Bmc], w=[Bsm])
                    op("dve", lambda e: e.scalar_tensor_tensor(out=krs[:, 2, :], in0=krs[:, 0, :], scalar=ropes[:, 0:1], in1=krs[:, 1, :], op0=ALU.mult, op1=ALU.add), r=[Bsm, Bmc], w=[Bsm])
                    op("dve", lambda e: e.tensor_copy(out=krs_b[:], in_=krs[:, 2, :]), r=[Bsm], w=[Bsm])
                    for c in range(2):
                        dma("sp", O["ckv_s"][l, :, c * 128:(c + 1) * 128].rearrange("s p -> p s"), ckvs_f[:, c, :], r=[Bsm], is_out=True)
                    op("dve", lambda e: e.tensor_copy(out=krs_o[:], in_=krs[64:96, 2, :]), r=[Bsm], w=[Bkrow])
                    dma("sp", O["krope_s"][l].rearrange("s p -> p s"), krs_o[:], r=[Bkrow], is_out=True)
                    kb.barrier()
                kb.es = es4
                cqT = kb.sb("cqT", [128, 3, T], BF16)
                with ExitStack() as es4b:
                    kb.es = es4b
                    sqp = Rot(kb, "sqp", [128, 3, 512], BF16, 2)
                    rsp = Rot(kb, "rsp", [128, 512], F32, 2)
                    uq = kb.sb("uq", [128, 3, T], F32)
                    Buq = Buf("uq")
                    for c in range(3):
                        def ev(tg, pt, pbuf, c=c):
                            op("act", lambda e: e.activation(out=uq[:, c, tg * 512:(tg + 1) * 512], in_=pt[:], func=AF.Copy), r=[pbuf], w=[Buq])

                        def evs(pt, pbuf, c=c):
                            op("act", lambda e: e.activation(out=uqs[:, c, :], in_=pt[:, 0:NS], func=AF.Copy), r=[pbuf], w=[Bsm])
                        inproj(l, CI_MQ + c, hT, hbuf, 128, ev, evs)
                    for tg in range(4):
                        sl = slice(tg * 512, (tg + 1) * 512)
                        fm_rms([uq[:, c, sl] for c in range(3)], [Buq], 384, V_QG, [[cqT[:, c, sl]] for c in range(3)], [Bcq], 512)
                    fm_rms([uqs[:, c, :] for c in range(3)], [Bsm], 384, V_QG, [[cqs_b[:, c, :]] for c in range(3)], [Bsm], NS)
                    kb.barrier()
                kb.es = es4
                if stage < 4:
                    kb.barrier()
                    continue
                wuq_b = kb.sb("wuq_b", [128, 3, 768], BF16)
                wukv_b = kb.sb("wukv_b", [128, 2, 1536], BF16)
                es4c = ExitStack()
                kb.es = es4c
                ropeC = kb.sb("ropeCb", [96, T], BF16)
                ropeS = kb.sb("ropeSb", [96, T], BF16)
                masks = kb.sb("masks", [128, 4, 512], BF16)
                dma("pool", ropeC[:], I["ropeC"], w=[Bmc])
                dma("pool", ropeS[:], I["ropeS"], w=[Bmc])
                dma("pool", masks[:], I["masks"].rearrange("o p j -> p o j"), w=[Bmc])
                dma("pool", wuq_b[:], I["wuq"][l].rearrange("(k p) n -> p k n", p=128), w=[Bmc])
                dma("pool", wukv_b[:], I["wukv"][l].rearrange("(k p) n -> p k n", p=128), w=[Bmc])
                hp = Rot(kb, "mlah", [128, T], BF16, 3)
                vp = Rot(kb, "mlav", [128, 16, 128], BF16, 1)
                ptp = Rot(kb, "mlapt", [128, 512], BF16, 3)
                t32 = Rot(kb, "mlat32", [128, 512], F32, 3)
                mx = kb.sb("mx", [128, 12], F32)
                Bmx = Buf("mx")
                pO, pOb = kb.acc_bank(0)
                pD, pDb = kb.acc_bank(1)
                for h in range(H):
                    q_raw, q_rawb = hp.get()
                    Qh, Qhb = hp.get()
                    Kh, Khb = hp.get()
                    Vh, Vhb = vp.get()
                    for tg in range(4):
                        sl = slice(tg * 512, (tg + 1) * 512)
                        pq, pqb = kb.bank()
                        for c in range(3):
                            op("pe", lambda e: e.matmul(pq[0:96, :], lhsT=wuq_b[:, c, h * 96:(h + 1) * 96], rhs=cqT[:, c, sl], start=(c == 0), stop=(c == 2)), r=[Bmc, Bcq], w=[pqb], inc=(c == 2))
                        op("act", lambda e: e.activation(out=q_raw[0:96, sl], in_=pq[0:96, :], func=AF.Copy), r=[pqb], w=[q_rawb])
                        pr, prb = kb.bank()
                        op("pe", lambda e: e.matmul(pr[0:96, :], lhsT=rfT_b[:], rhs=q_raw[0:96, sl], start=True, stop=True), r=[Bmc, q_rawb], w=[prb])
                        ta, tab = t32.get()
                        tb_, tbb = t32.get()
                        op("pool", lambda e: e.tensor_tensor(out=ta[0:96, :], in0=q_raw[0:96, sl], in1=ropeC[:, sl], op=ALU.mult), r=[q_rawb, Bmc], w=[tab])
                        op("dve", lambda e: e.tensor_tensor(out=tb_[0:96, :], in0=pr[0:96, :], in1=ropeS[:, sl], op=ALU.mult), r=[prb, Bmc], w=[tbb])
                        op("pool", lambda e: e.tensor_tensor(out=Qh[0:96, sl], in0=ta[0:96, :], in1=tb_[0:96, :], op=ALU.add), r=[tab, tbb], w=[Qhb])
                        pk, pkb = kb.bank()
                        for c in range(2):
                            op("pe", lambda e: e.matmul(pk[0:64, :], lhsT=wukv_b[:, c, h * 192:h * 192 + 64], rhs=ckvT[:, c, sl], start=(c == 0), stop=(c == 1)), r=[Bmc, Bckv], w=[pkb], inc=(c == 1))
                        op("act", lambda e: e.activation(out=Kh[0:64, sl], in_=pk[0:64, :], func=AF.Copy), r=[pkb], w=[Khb])
                        pv, pvb = kb.bank()
                        for i in range(4):
                            tb = tg * 4 + i
                            for c in range(2):
                                op("pe", lambda e: e.matmul(pv[:, i * 128:(i + 1) * 128], lhsT=ckvT[:, c, tb * 128:(tb + 1) * 128], rhs=wukv_b[:, c, h * 192 + 64:h * 192 + 192], start=(c == 0), stop=(c == 1)),
                                   r=[Bmc, Bckv], w=[pvb], inc=(i == 3 and c == 1))
                        op("dve", lambda e: e.tensor_copy(out=Vh[:, tg * 4:tg * 4 + 4, :].rearrange("p a b -> p (a b)"), in_=pv[:]), r=[pvb], w=[Vhb])
                    op("pool", lambda e: e.tensor_copy(out=Kh[64:96, :], in_=krT[64:96, :]), r=[Bkr], w=[Khb])
                    for qi, (src, srcb) in enumerate(((Qh, Qhb), (Kh, Khb))):
                        for tg in range(4):
                            sl = slice(tg * 512, (tg + 1) * 512)
                            sq, sqb = ptp.get()
                            op("pool", lambda e: e.tensor_tensor(out=sq[0:96, :], in0=src[0:96, sl], in1=src[0:96, sl], op=ALU.mult), r=[srcb], w=[sqb])
                            pn, pnb = kb.bank()
                            op("pe", lambda e: e.matmul(pn[:], lhsT=ones_b[0:96, :], rhs=sq[0:96, :], start=True, stop=True), r=[sqb, Bc], w=[pnb])
                            op("dve", lambda e: e.tensor_reduce(out=mx[:, qi * 4 + tg:qi * 4 + tg + 1], in_=pn[:], op=ALU.max, axis=AX.X), r=[pnb, Bmx], w=[Bmx])
                        op("dve", lambda e: e.tensor_reduce(out=mx[:, 8 + qi:9 + qi], in_=mx[:, qi * 4:qi * 4 + 4], op=ALU.max, axis=AX.X), r=[Bmx], w=[Bmx])
                    op("dve", lambda e: e.tensor_tensor(out=mx[:, 10:11], in0=mx[:, 8:9], in1=mx[:, 9:10], op=ALU.mult), r=[Bmx], w=[Bmx])
                    op("act", lambda e: e.activation(out=mx[:, 10:11], in_=mx[:, 10:11], func=AF.Sqrt), r=[Bmx], w=[Bmx])
                    op("dve", lambda e: e.tensor_scalar(out=mx[:, 11:12], in0=mx[:, 10:11], scalar1=-MLA_SCALE, scalar2=None, op0=ALU.mult), r=[Bmx], w=[Bmx])
                    for g in range(4):
                        qs = slice(g * 512, (g + 1) * 512)
                        nkb = 4 * (g + 1)
                        for kbi in range(nkb):
                            ps, psb = kb.bank()
                            op("pe", lambda e: e.matmul(ps[:], lhsT=Kh[0:96, kbi * 128:(kbi + 1) * 128], rhs=Qh[0:96, qs], start=True, stop=True), r=[Khb, Qhb], w=[psb])
                            pt, ptb = ptp.get()
                            op("act", lambda e: e.activation(out=pt[:], in_=ps[:], func=AF.Exp, scale=MLA_SCALE, bias=mx[:, 11:12]), r=[psb, Bmx], w=[ptb])
                            if kbi >= 4 * g:
                                o_ = kbi - 4 * g
                                op("pool", lambda e: e.tensor_tensor(out=pt[:], in0=pt[:], in1=masks[:, o_, :], op=ALU.mult), r=[ptb, Bmc], w=[ptb])
                            op("pe", lambda e: e.matmul(pO[:], lhsT=Vh[:, kbi, :], rhs=pt[:], start=(kbi == 0), stop=(kbi == nkb - 1)), r=[Vhb, ptb], w=[pOb], inc=False)
                            op("pe", lambda e: e.matmul(pD[:], lhsT=ones_b[:], rhs=pt[:], start=(kbi == 0), stop=(kbi == nkb - 1)), r=[Bc, ptb], w=[pDb], inc=True)
                        rd, rdb = t32.get()
                        op("act", lambda e: e.activation(out=rd[:], in_=pD[:], func=AF.Ln), r=[pDb], w=[rdb])
                        op("act", lambda e: e.activation(out=rd[:], in_=rd[:], func=AF.Exp, scale=-1.0), r=[rdb], w=[rdb])
                        op("dve", lambda e: e.tensor_tensor(out=oT[:, h, qs], in0=pO[:], in1=rd[:], op=ALU.mult), r=[pOb, pDb, rdb], w=[oTb[h]])
                kb.barrier()
                es4c.close()
                kb.es = es4
                if stage >= 50:
                    NPG = NPAGE
                    ptb_i = kb.sb("ptb_i", [128, NS * NPG], I32)
                    idx_all = kb.sb("idx_all", [128, NS * NPG], I32)
                    iota_c = kb.sb("iota_c", [128, 1], F32)
                    wukT_b = kb.sb("wukT_b", [64, 8, KVL], BF16)
                    Bsa = Buf("sa_const")
                    dma("sp", ptb_i[:], I["ptab"].rearrange("s g -> (s g)").partition_broadcast(128), w=[Bsa])
                    dma("sp", iota_c[:], I["iota"], w=[Bsa])
                    dma("pool", wukT_b[:], I["wukT"][l].rearrange("h d c -> d h c"), w=[Bsa])
                    op("dve", lambda e: e.tensor_scalar(out=iota_c[:], in0=iota_c[:], scalar1=float(l * 5120 * 128), scalar2=None, op0=ALU.add), r=[Bsa], w=[Bsa])
                    op("dve", lambda e: e.tensor_scalar(out=idx_all[:], in0=ptb_i[:], scalar1=128.0, scalar2=iota_c[:, 0:1], op0=ALU.mult, op1=ALU.add), r=[Bsa], w=[Bsa])
                    qs_f = kb.sb("qs_f", [96, 3, H * NS], F32)
                    qs_b = kb.sb("qs_b", [96, H * NS], BF16)
                    qn_b = kb.sb("qn_b", [64, H * NS], BF16)
                    qrope0 = kb.sb("qrope0", [32, NS, H], BF16)
                    qlatT = kb.sb("qlatT", [128, 2, NS, H], BF16)
                    Bq = Buf("qs")
                    pq, pqb = kb.bank()
                    for h in range(H):
                        for c in range(3):
                            op("pe", lambda e: e.matmul(pq[0:96, h * NS:(h + 1) * NS], lhsT=wuq_b[:, c, h * 96:(h + 1) * 96], rhs=cqs_b[:, c, :], start=(c == 0), stop=(c == 2)), r=[Bmc, Bsm], w=[pqb], inc=(h == H - 1 and c == 2))
                    op("act", lambda e: e.activation(out=qs_f[:, 0, :], in_=pq[0:96, 0:H * NS], func=AF.Copy), r=[pqb], w=[Bq])
                    op("dve", lambda e: e.tensor_copy(out=qs_b[:], in_=qs_f[:, 0, :]), r=[Bq], w=[Bq])
                    pr, prb = kb.bank()
                    op("pe", lambda e: e.matmul(pr[0:96, 0:H * NS], lhsT=rfT_b[:], rhs=qs_b[:], start=True, stop=True), r=[Bmc, Bq], w=[prb])
                    op("dve", lambda e: e.tensor_scalar(out=qs_f[:, 1, :], in0=pr[0:96, 0:H * NS], scalar1=ropes[:, 1:2], scalar2=None, op0=ALU.mult), r=[prb, Bmc], w=[Bq])
                    op("dve", lambda e: e.scalar_tensor_tensor(out=qs_f[:, 2, :], in0=qs_f[:, 0, :], scalar=ropes[:, 0:1], in1=qs_f[:, 1, :], op0=ALU.mult, op1=ALU.add), r=[Bq, Bmc], w=[Bq])
                    op("dve", lambda e: e.tensor_copy(out=qn_b[:], in_=qs_f[0:64, 2, :]), r=[Bq], w=[Bq])
                    op("dve", lambda e: e.tensor_copy(out=qrope0[:].rearrange("p s h -> p h s"), in_=qs_f[64:96, 2, :].rearrange("p (h s) -> p h s", s=NS)), r=[Bq], w=[Bq])
                    qs_b_r = qrope0
                    krs_b0 = kb.sb("krs_b0", [32, NS], BF16)
                    op("dve", lambda e: e.tensor_copy(out=krs_b0[:], in_=krs[64:96, 2, :]), r=[Bsm], w=[Bq])
                    pl, plb = kb.bank()
                    for cc in range(2):
                        for h in range(H):
                            op("pe", lambda e: e.matmul(pl[:, (cc * H + h) * NS:(cc * H + h + 1) * NS], lhsT=wukT_b[:, h, cc * 128:(cc + 1) * 128], rhs=qn_b[:, h * NS:(h + 1) * NS], start=True, stop=True),
                               r=[Bsa, Bq], w=[plb], inc=(cc == 1 and h == H - 1))
                    op("dve", lambda e: e.tensor_copy(out=qlatT[:].rearrange("p c s h -> p c h s"), in_=pl[:, 0:2 * H * NS].rearrange("p (c h s) -> p c h s", c=2, h=H)), r=[plb], w=[Bq])
                    ckp = Rot(kb, "ckp", [128, 292], BF16, 4)
                    cktp = Rot(kb, "cktp", [128, 384], BF16, 3)
                    call_flat = I["cache_all"].rearrange("l n t c -> (l n t) c")
                    for i_ in range(4):
                        op("pool", lambda e: e.memset(ckp.t[i_][:, 288:289], 1.0), w=[ckp.b[i_]])
                    sT = kb.sb("sT", [128, 2, 512], F32)
                    PT_s = kb.sb("PT_s", [128, 2, 512], BF16)
                    BsT = Buf("sT")
                    sm = kb.sb("sm", [128, 40], F32)
                    Bsmx = Buf("smx")
                    rowp = kb.sb("rowp", [128, 304], BF16)
                    op("pool", lambda e: e.memset(rowp[:], 0.0), w=[Bsmx])
                    olat = kb.sb("olat", [8, 264], F32)
                    olT = kb.sb("olT", [128, 2, H], BF16)
                    accS0, accS0b = kb.acc_bank(0)
                    accS1, accS1b = kb.acc_bank(1)
                    SA = int(os.environ.get("SA_SUB", "9"))
                    for s_ in range(NS if SA >= 3 else 1):
                        for pg in range(NPG if SA >= 2 else 2):
                            ck, ckb = ckp.get()
                            icol = idx_all[:, s_ * NPG + pg:s_ * NPG + pg + 1]
                            kb.dma_fn("pool", lambda e: e.indirect_dma_start(out=ck[:, 0:288], out_offset=None, in_=call_flat, in_offset=bass.IndirectOffsetOnAxis(ap=icol, axis=0)), r=[Bsa], w=[ckb])
                            pt, ptb = kb.bank()
                            ptv = pt[:].bitcast(BF16)
                            op("pe", lambda e: e.transpose(ptv[:, 0:128], ck[:, 0:128], ident_b[:]), r=[ckb, Bc], w=[ptb], inc=False)
                            op("pe", lambda e: e.transpose(ptv[:, 128:256], ck[:, 128:256], ident_b[:]), r=[ckb, Bc], w=[ptb], inc=False)
                            op("pe", lambda e: e.transpose(ptv[0:32, 256:384], ck[:, 256:288], ident_b[:]), r=[ckb, Bc], w=[ptb])
                            ckt, cktb = cktp.get()
                            op("act", lambda e: e.activation(out=ckt[:, 0:256], in_=ptv[:, 0:256], func=AF.Copy), r=[ptb], w=[cktb])
                            op("dve", lambda e: e.tensor_copy(out=ckt[0:32, 256:384], in_=ptv[0:32, 256:384]), r=[ptb], w=[cktb])
                            acc, accb = (accS0, accS0b) if pg < 64 else (accS1, accS1b)
                            cs_ = (pg % 64) * 8
                            for cc in range(2):
                                op("pe", lambda e: e.matmul(acc[:, cs_:cs_ + 8], lhsT=ckt[:, cc * 128:(cc + 1) * 128], rhs=qlatT[:, cc, s_, :], start=(cc == 0), stop=False), r=[cktb, Bq], w=[accb], inc=False)
                            op("pe", lambda e: e.matmul(acc[:, cs_:cs_ + 8], lhsT=ckt[0:32, 256:384], rhs=qrope0[:, s_, :], start=False, stop=True), r=[cktb, Bq], w=[accb], inc=True)
                        if SA < 4:
                            continue
                        op("act", lambda e: e.activation(out=sT[:, 0, :], in_=accS0[:], func=AF.Copy), r=[accS0b], w=[BsT])
                        op("act", lambda e: e.activation(out=sT[:, 1, :], in_=accS1[:], func=AF.Copy), r=[accS1b], w=[BsT])
                        op("dve", lambda e: e.tensor_reduce(out=sm[:, 0:8], in_=sT[:].rearrange("p b (g h) -> p h b g", h=8), op=ALU.max, axis=AX.XY), r=[BsT, Bsmx], w=[Bsmx])
                        p1, p1b = kb.bank()
                        op("pe", lambda e: e.transpose(p1[0:8, 0:128], sm[:, 0:8], ident_f[:]), r=[Bsmx, Bc], w=[p1b])
                        op("dve", lambda e: e.tensor_reduce(out=sm[0:8, 8:9], in_=p1[0:8, 0:128], op=ALU.max, axis=AX.X), r=[p1b, Bsmx], w=[Bsmx])
                        p2, p2b = kb.bank()
                        for cc in range(2):
                            op("pe", lambda e: e.matmul(p2[0:8, 0:1], lhsT=qlatT[:, cc, s_, :], rhs=ckvs_b[:, cc, s_:s_ + 1], start=(cc == 0), stop=False), r=[Bq, Bsm], w=[p2b], inc=False)
                        op("pe", lambda e: e.matmul(p2[0:8, 0:1], lhsT=qs_b_r[:, s_, :], rhs=krs_b0[:, s_:s_ + 1], start=False, stop=True), r=[Bq, Bsm], w=[p2b])
                        op("dve", lambda e: e.tensor_copy(out=sm[0:8, 9:10], in_=p2[0:8, 0:1]), r=[p2b, Bsmx], w=[Bsmx])
                        op("dve", lambda e: e.tensor_tensor(out=sm[0:8, 10:11], in0=sm[0:8, 8:9], in1=sm[0:8, 9:10], op=ALU.max), r=[Bsmx], w=[Bsmx])
                        op("dve", lambda e: e.tensor_tensor(out=sm[0:8, 11:12], in0=sm[0:8, 9:10], in1=sm[0:8, 10:11], op=ALU.subtract), r=[Bsmx], w=[Bsmx])
                        op("act", lambda e: e.activation(out=sm[0:8, 11:12], in_=sm[0:8, 11:12], func=AF.Exp, scale=MLA_SCALE), r=[Bsmx], w=[Bsmx])
                        op("dve", lambda e: e.tensor_scalar(out=sm[0:8, 16:24], in0=ident_f[0:8, 0:8], scalar1=sm[0:8, 10:11], scalar2=None, op0=ALU.mult), r=[Bsmx, Bc], w=[Bsmx])
                        p3, p3b = kb.bank()
                        op("pe", lambda e: e.matmul(p3[:, 0:8], lhsT=ones_f[0:8, :], rhs=sm[0:8, 16:24], start=True, stop=True), r=[Bsmx, Bc], w=[p3b])
                        op("dve", lambda e: e.tensor_copy(out=sm[:, 24:32], in_=p3[:, 0:8]), r=[p3b, Bsmx], w=[Bsmx])
                        op("dve", lambda e: e.tensor_tensor(out=sT[:].rearrange("p b (g h) -> p (b g) h", h=8), in0=sT[:].rearrange("p b (g h) -> p (b g) h", h=8),
                                                            in1=sm[:, 24:32].unsqueeze(1).to_broadcast([128, NPG, 8]), op=ALU.subtract), r=[BsT, Bsmx], w=[BsT])
                        op("act", lambda e: e.activation(out=PT_s[:], in_=sT[:], func=AF.Exp, scale=MLA_SCALE), r=[BsT], w=[BsT])
                        p4, p4b = kb.bank()
                        for cc in range(2):
                            op("pe", lambda e: e.transpose(p4[0:1, cc * 128:(cc + 1) * 128], ckvs_f[:, cc, s_:s_ + 1], ident_f[:]), r=[Bsm, Bc], w=[p4b], inc=False)
                        op("pe", lambda e: e.transpose(p4[0:1, 264:272], sm[0:8, 11:12], ident_f[0:8, 0:8]), r=[Bsmx, Bc], w=[p4b])
                        op("dve", lambda e: e.tensor_copy(out=rowp[0:1, 0:256], in_=p4[0:1, 0:256]), r=[p4b, Bsmx], w=[Bsmx])
                        op("pool", lambda e: e.memset(rowp[0:1, 288:289], 1.0), w=[Bsmx])
                        op("dve", lambda e: e.tensor_copy(out=rowp[0:1, 296:304], in_=p4[0:1, 264:272]), r=[p4b, Bsmx], w=[Bsmx])
                        if SA < 5:
                            continue
                        for pg in range(NPG):
                            ck, ckb = ckp.get()
                            icol = idx_all[:, s_ * NPG + pg:s_ * NPG + pg + 1]
                            kb.dma_fn("pool", lambda e: e.indirect_dma_start(out=ck[:, 0:288], out_offset=None, in_=call_flat, in_offset=bass.IndirectOffsetOnAxis(ap=icol, axis=0)), r=[Bsa], w=[ckb])
                            op("pe", lambda e: e.matmul(accS0[0:8, 0:289], lhsT=PT_s[:, pg // 64, (pg % 64) * 8:(pg % 64) * 8 + 8], rhs=ck[:, 0:289], start=(pg == 0), stop=False), r=[ckb, BsT], w=[accS0b], inc=True)
                        op("pe", lambda e: e.matmul(accS0[0:8, 0:289], lhsT=rowp[:, 296:304], rhs=rowp[:, 0:289], start=False, stop=True), r=[Bsmx], w=[accS0b])
                        op("dve", lambda e: e.reciprocal(out=sm[0:8, 12:13], in_=accS0[0:8, 288:289]), r=[accS0b, Bsmx], w=[Bsmx])
                        op("dve", lambda e: e.tensor_scalar(out=olat[:, 0:256], in0=accS0[0:8, 0:256], scalar1=sm[0:8, 12:13], scalar2=None, op0=ALU.mult), r=[accS0b, Bsmx], w=[Bsmx])
                        p5, p5b = kb.bank()
                        for cc in range(2):
                            op("pe", lambda e: e.transpose(p5[:, cc * 8:(cc + 1) * 8], olat[:, cc * 128:(cc + 1) * 128], ident_f[0:8, 0:8]), r=[Bsmx, Bc], w=[p5b], inc=(cc == 1))
                        op("dve", lambda e: e.tensor_copy(out=olT[:].rearrange("p c h -> p (c h)"), in_=p5[:, 0:16]), r=[p5b, Bsmx], w=[Bsmx])
                        p6, p6b = kb.bank()
                        for h in range(H):
                            for cc in range(2):
                                op("pe", lambda e: e.matmul(p6[:, h * 8:(h + 1) * 8], lhsT=wukv_b[:, cc, h * 192 + 64:h * 192 + 192], rhs=olT[:, cc, :], start=(cc == 0), stop=(cc == 1)), r=[Bmc, Bsmx], w=[p6b], inc=(h == H - 1 and cc == 1))
                        op("dve", lambda e: e.tensor_copy(out=osT[:, 0:H, s_], in_=p6[:, 0:72:9]), r=[p6b], w=[Bos])
                kb.barrier()
            kb.es = es
            if stage < 5:
                continue
            branch_proj(l, "wmlp", 8, CI_GATE + 16, False)
            kb.barrier()

            with ExitStack() as es5:
                kb.es = es5
                wo_b = kb.sb("wo_b", [128, 8, D], BF16)
                Bwo = Buf("wo")
                dma("pool", wo_b[:], I["wo"][l].rearrange("(k p) n -> p k n", p=128), w=[Bwo])
                gtb = kb.sb("gt1bc", [128, D], F32)
                Bgt = Buf("gt1bc")
                tmpbc = kb.sb("tmpbc5", [128, 128], F32)
                make_gtbc(l, 16, gtb, Bgt, tmpbc, Buf("tmpbc5"))
                xin = Rot(kb, "xin5", [128, D], F32, 2)
                x1p = Rot(kb, "x1p5", [128, D], F32, 2)
                junkp = Rot(kb, "junk5", [128, D], F32, 1)
                xnp = Rot(kb, "xn5", [128, D], BF16, 1)
                tmps = Rot(kb, "tmps5", [128, 8, NS], F32, 3)
                ms_b = kb.sb("ms_b", [128, 8, NS], BF16)
                for tb in range(16):
                    bs = slice(tb * 128, (tb + 1) * 128)
                    xt, bxt = xin.get()
                    dma("sp", xt[:], xsrc[bs, :], w=[bxt])
                    x1t, x1b = x1p.get()
                    for half in range(2):
                        pw, pwb = kb.bank()
                        for k in range(8):
                            op("pe", lambda e: e.matmul(pw[:], lhsT=mT[:, k, bs], rhs=wo_b[:, k, half * 512:(half + 1) * 512], start=(k == 0), stop=(k == 7)), r=[mTb[k], Bwo], w=[pwb], inc=(k == 7))
                        op("dve", lambda e: e.tensor_tensor(out=x1t[:, half * 512:(half + 1) * 512], in0=pw[:], in1=gtb[:, half * 512:(half + 1) * 512], op=ALU.mult), r=[pwb, Bgt], w=[x1b])
                    op("pool", lambda e: e.tensor_tensor(out=x1t[:], in0=x1t[:], in1=xt[:], op=ALU.add), r=[x1b, bxt], w=[x1b])
                    dma("pool", xa[bs, :], x1t[:], r=[x1b])
                    norm_block(x1t, x1b, A2[l], 24, l, hT, hbuf, tb, junkp, xnp)
                op("dve", lambda e: e.tensor_copy(out=ms_b[:], in_=msT[:]), r=[Bms], w=[Bms])
                pp, ppb = kb.bank()
                for j in range(8):
                    for k in range(8):
                        op("pe", lambda e: e.matmul(pp[:, j * NS:(j + 1) * NS], lhsT=wo_b[:, k, j * 128:(j + 1) * 128], rhs=ms_b[:, k, :], start=(k == 0), stop=(k == 7)), r=[Bwo, Bms], w=[ppb], inc=(k == 7))
                tq, tqb = tmps.get()
                op("dve", lambda e: e.tensor_tensor(out=tq[:], in0=pp[:, 0:8 * NS].rearrange("p (j s) -> p j s", s=NS), in1=modT[l][:, 16:24, 1:5], op=ALU.mult), r=[ppb, Bmod[l]], w=[tqb])
                op("dve", lambda e: e.tensor_tensor(out=xsT[:], in0=xsT[:], in1=tq[:], op=ALU.add), r=[tqb, Bxs], w=[Bxs])
                sample_norm(l, As2[l], 24, tmps)
                kb.barrier()
            kb.es = es

            with ExitStack() as es6:
                kb.es = es6
                print("sbuf remaining before FFN", nc.sbuf_bytes_remaining)
                last = (l == L - 1)
                wffo_b = kb.sb("wffo_b", [128, 22, D], BF16)
                Bwf = Buf("wffo")
                for f0 in range(0, 22, 2):
                    dma("pool", wffo_b[:, f0:f0 + 2, :], I["wffo"][l, f0 * 128:(f0 + 2) * 128, :].rearrange("(f p) n -> p f n", p=128), w=[Bwf])
                gtb = kb.sb("gt2bc", [128, D], F32)
                Bgt = Buf("gt2bc")
                tmpbc = kb.sb("tmpbc6", [128, 128], F32)
                Btmpbc = Buf("tmpbc6")
                make_gtbc(l, 40, gtb, Bgt, tmpbc, Btmpbc)
                xin = Rot(kb, "xin6", [128, D], F32, 2)
                junkp = Rot(kb, "junk6", [128, D], F32, 1)
                xnp = Rot(kb, "xn6", [128, D], BF16, 1)
                sgq = Rot(kb, "sg6", [128, 512], F32, 2)
                tmps = Rot(kb, "tmps6", [128, 8, NS], F32, 3)
                acts = kb.sb("acts", [128, 22, NS], BF16)
                Bacts = Buf("acts")
                st6 = kb.sb("st6", [128, 4], F32)
                Bst6 = Buf("st6")
                if last:
                    gfb = kb.sb("gfbc", [128, D], F32)
                    Bgf = Buf("gfbc")
                    pg0, pgb0 = kb.bank()
                    pg1, pgb1 = kb.bank()
                    for c in range(8):
                        op("dve", lambda e: e.tensor_scalar(out=tmpbc[:], in0=ones_f[:], scalar1=vecT[0][:, V_GF + c:V_GF + c + 1], scalar2=None, op0=ALU.mult), r=[Bmod[0], Bc, Btmpbc], w=[Btmpbc])
                        pg, pgb = (pg0, pgb0) if c < 4 else (pg1, pgb1)
                        op("pe", lambda e: e.matmul(pg[:, (c % 4) * 128:(c % 4 + 1) * 128], lhsT=tmpbc[:], rhs=ident_f[:], start=True, stop=True), r=[Btmpbc, Bc], w=[pgb])
                    op("act", lambda e: e.activation(out=gfb[:, 0:512], in_=pg0[:], func=AF.Copy), r=[pgb0], w=[Bgf])
                    op("act", lambda e: e.activation(out=gfb[:, 512:1024], in_=pg1[:], func=AF.Copy), r=[pgb1], w=[Bgf])
                oTv = oT[:].rearrange("p n (a t) -> p (n a) t", a=2)
                mTv = mT[:].rearrange("p n (a t) -> p (n a) t", a=2)
                actv = [oTv[:, f, :] if f < 20 else mTv[:, f - 20, :] for f in range(22)]
                actb = [Buf("act%d" % f) for f in range(22)]
                for th in range(2):
                    for f in range(22):
                        gw, gwb = wpiece.get()
                        dma("pool", gw[:], I["wffi"][l, 2 * f], w=[gwb])
                        uw, uwb = wpiece.get()
                        dma("pool", uw[:], I["wffi"][l, 2 * f + 1], w=[uwb])
                        for t2 in range(2):
                            tg = th * 2 + t2
                            sl = slice(tg * 512, (tg + 1) * 512)
                            pg, pgb = kb.bank()
                            for k in range(8):
                                op("pe", lambda e: e.matmul(pg[:], lhsT=gw[:, k, :], rhs=hT[:, k, sl], start=(k == 0), stop=(k == 7)), r=[gwb] + hbuf[tg * 4:tg * 4 + 4], w=[pgb], inc=(k == 7))
                            pu, pub = kb.bank()
                            for k in range(8):
                                op("pe", lambda e: e.matmul(pu[:], lhsT=uw[:, k, :], rhs=hT[:, k, sl], start=(k == 0), stop=(k == 7)), r=[uwb] + hbuf[tg * 4:tg * 4 + 4], w=[pub], inc=(k == 7))
                            sg, sgb = sgq.get()
                            op("act", lambda e: e.activation(out=sg[:], in_=pg[:], func=AF.Silu), r=[pgb], w=[sgb])
                            op("dve", lambda e: e.tensor_tensor(out=actv[f][:, t2 * 512:(t2 + 1) * 512], in0=pu[:], in1=sg[:], op=ALU.mult), r=[pub, sgb], w=[actb[f]])
                        if th == 0:
                            pp, ppb = kb.bank()
                            for k in range(8):
                                op("pe", lambda e: e.matmul(pp[:, 0:NS], lhsT=gw[:, k, :], rhs=hsT[:, k, :], start=(k == 0), stop=(k == 7)), r=[gwb, Bhs], w=[ppb], inc=False)
                            for k in range(8):
                                op("pe", lambda e: e.matmul(pp[:, NS:2 * NS], lhsT=uw[:, k, :], rhs=hsT[:, k, :], start=(k == 0), stop=(k == 7)), r=[uwb, Bhs], w=[ppb], inc=(k == 7))
                            sg, sgb = sgq.get()
                            op("act", lambda e: e.activation(out=sg[:, 0:NS], in_=pp[:, 0:NS], func=AF.Silu), r=[ppb], w=[sgb])
                            op("dve", lambda e: e.tensor_tensor(out=acts[:, f, :], in0=pp[:, NS:2 * NS], in1=sg[:, 0:NS], op=ALU.mult), r=[ppb, sgb], w=[Bacts])
                    for t8 in range(8):
                        tb = th * 8 + t8
                        bs = slice(tb * 128, (tb + 1) * 128)
                        xt, bxt = xin.get()
                        dma("sp", xt[:], xa[bs, :], w=[bxt])
                        x2t, x2b = xin.get()
                        for half in range(2):
                            pw, pwb = kb.bank()
                            for f in range(22):
                                op("pe", lambda e: e.matmul(pw[:], lhsT=actv[f][:, t8 * 128:(t8 + 1) * 128], rhs=wffo_b[:, f, half * 512:(half + 1) * 512], start=(f == 0), stop=(f == 21)), r=[actb[f], Bwf], w=[pwb], inc=(f == 21))
                            op("dve", lambda e: e.tensor_tensor(out=x2t[:, half * 512:(half + 1) * 512], in0=pw[:], in1=gtb[:, half * 512:(half + 1) * 512], op=ALU.mult), r=[pwb, Bgt], w=[x2b])
                        op("pool", lambda e: e.tensor_tensor(out=x2t[:], in0=x2t[:], in1=xt[:], op=ALU.add), r=[x2b, bxt], w=[x2b])
                        if not last:
                            dma("pool", xb[bs, :], x2t[:], r=[x2b])
                            norm_block(x2t, x2b, A1[l + 1], 0, l + 1, hT, hbuf, tb, junkp, xnp)
                        else:
                            jt, jb = junkp.get()
                            op("act", lambda e: e.activation(out=jt[:], in_=x2t[:], func=AF.Square, accum_out=st6[:, 0:1]), r=[x2b], w=[jb, Bst6])
                            op("dve", lambda e: e.tensor_scalar(out=st6[:, 1:2], in0=st6[:, 0:1], scalar1=1.0 / D, scalar2=EPS, op0=ALU.mult, op1=ALU.add), r=[Bst6], w=[Bst6])
                            op("act", lambda e: e.activation(out=st6[:, 2:3], in_=st6[:, 1:2], func=AF.Sqrt), r=[Bst6], w=[Bst6])
                            op("dve", lambda e: e.reciprocal(out=st6[:, 3:4], in_=st6[:, 2:3]), r=[Bst6], w=[Bst6])
                            op("dve", lambda e: e.scalar_tensor_tensor(out=jt[:], in0=x2t[:], scalar=st6[:, 3:4], in1=gfb[:], op0=ALU.mult, op1=ALU.mult), r=[x2b, Bst6, Bgf, jb], w=[jb])
                            dma("sp", O["y_p"][bs, :], jt[:], r=[jb], is_out=True)
                pp, ppb = kb.bank()
                for j in range(8):
                    for f in range(22):
                        op("pe", lambda e: e.matmul(pp[:, j * NS:(j + 1) * NS], lhsT=wffo_b[:, f, j * 128:(j + 1) * 128], rhs=acts[:, f, :], start=(f == 0), stop=(f == 21)), r=[Bwf, Bacts], w=[ppb], inc=(f == 21))
                tq, tqb = tmps.get()
                op("dve", lambda e: e.tensor_tensor(out=tq[:], in0=pp[:, 0:8 * NS].rearrange("p (j s) -> p j s", s=NS), in1=modT[l][:, 40:48, 1:5], op=ALU.mult), r=[ppb, Bmod[l]], w=[tqb])
                op("dve", lambda e: e.tensor_tensor(out=xsT[:], in0=xsT[:], in1=tq[:], op=ALU.add), r=[tqb, Bxs], w=[Bxs])
                if last:
                    sq, sqb = tmps.get()
                    op("dve", lambda e: e.tensor_tensor(out=sq[:], in0=xsT[:], in1=xsT[:], op=ALU.mult), r=[Bxs], w=[sqb])
                    pt, pbuf = kb.bank()
                    for k in range(8):
                        op("pe", lambda e: e.matmul(pt[:, 0:NS], lhsT=ones_f[:], rhs=sq[:, k, :], start=(k == 0), stop=(k == 7)), r=[sqb, Bc], w=[pbuf], inc=(k == 7))
                    rs, rsb = tmps.get()
                    op("dve", lambda e: e.tensor_scalar(out=rs[:, 0, :], in0=pt[:, 0:NS], scalar1=1.0 / D, scalar2=EPS, op0=ALU.mult, op1=ALU.add), r=[pbuf], w=[rsb])
                    op("act", lambda e: e.activation(out=rs[:, 1, :], in_=rs[:, 0, :], func=AF.Sqrt), r=[rsb], w=[rsb])
                    op("dve", lambda e: e.reciprocal(out=rs[:, 2, :], in_=rs[:, 1, :]), r=[rsb], w=[rsb])
                    op("dve", lambda e: e.tensor_tensor(out=sq[:], in0=xsT[:], in1=rs[:, 2:3, :].to_broadcast([128, 8, NS]), op=ALU.mult), r=[Bxs, rsb, sqb], w=[sqb])
                    op("dve", lambda e: e.tensor_tensor(out=sq[:], in0=sq[:], in1=vecT[0][:, V_GF:V_GF + 8].unsqueeze(2).to_broadcast([128, 8, NS]), op=ALU.mult), r=[sqb, Bmod[0]], w=[sqb])
                    for k in range(8):
                        dma("sp", O["y_s"][:, k * 128:(k + 1) * 128].rearrange("s p -> p s"), sq[:, k, :], r=[sqb], is_out=True)
                else:
                    sample_norm(l + 1, As1[l + 1], 0, tmps)
                kb.barrier()
            kb.es = es
        kb.finish()
        print("ninst", kb.ninst, "dma counts", {q: v[1] for q, v in kb.dq.items()}, "eng counts", {e: kb.cnt[e] for e in kb.E}, "hw incs", dict(kb.hw))
    return nc, kb.waited


def _prep_shared(inp):
    f = np.float32
    cols = chunk_cols()
    w_in = np.asarray(inp["w_in"], f)
    win = np.zeros((L, NCI, 128, 8, 128), f)
    for ci, (c0, wd) in enumerate(cols):
        blk = w_in[:, :, c0:c0 + wd].reshape(L, 8, 128, wd)
        win[:, ci, :, :, :wd] = blk.transpose(0, 2, 1, 3)
    wada = np.ascontiguousarray(np.asarray(inp["w_ada"], f).reshape(L, 8, 128, 48, 128).transpose(0, 3, 2, 1, 4))
    vecs = np.zeros((L, 256, 128), f)
    for l in range(L):
        rows = [np.asarray(inp["b_ada"][l], f).reshape(48, 128), np.asarray(inp["g_norm1"][l], f).reshape(8, 128), np.asarray(inp["g_norm2"][l], f).reshape(8, 128),
                np.asarray(inp["rg_conv_w"][l], f).reshape(40, 128), np.asarray(inp["rg_conv_b"][l], f).reshape(10, 128), np.asarray(inp["rg_ba"][l], f).reshape(10, 128),
                np.asarray(inp["rg_bx"][l], f).reshape(10, 128), np.asarray(inp["rg_lambda"][l], f).reshape(10, 128), np.asarray(inp["gdn_conv_w"][l], f).reshape(96, 128),
                np.asarray(inp["mla_q_norm_g"][l], f).reshape(3, 128), np.asarray(inp["mla_kv_norm_g"][l], f).reshape(2, 128), np.asarray(inp["gdn_norm_g"][l], f).reshape(1, 128),
                np.asarray(inp["g_final"], f).reshape(8, 128)]
        r = np.concatenate(rows, 0)
        vecs[l, :r.shape[0]] = r
    rgw = np.concatenate([np.asarray(inp["rg_wa"], f), np.asarray(inp["rg_wx"], f)], axis=1)

    def projl(w, nk):
        return np.ascontiguousarray(np.asarray(w, f).reshape(L, nk, 128, 8, 128).transpose(0, 3, 2, 1, 4))
    wffi_src = np.asarray(inp["w_ffn_in"], f).reshape(L, 8, 128, 2, 22, 128)
    wffi = np.ascontiguousarray(wffi_src.transpose(0, 4, 3, 2, 1, 5)).reshape(L, 44, 128, 8, 128)
    wukv = np.asarray(inp["w_ukv"], f)
    wukT = np.ascontiguousarray(wukv.reshape(L, KVL, H, NOPE + VH)[:, :, :, :NOPE].transpose(0, 2, 3, 1))
    gdn_ab = np.stack([np.asarray(inp["gdn_A_log"], f), np.asarray(inp["gdn_dt_bias"], f)], axis=1)
    ident = np.eye(128, dtype=f)
    masks = np.zeros((4, 128, 512), f)
    for o in range(4):
        p = np.arange(128)[:, None]
        j = np.arange(512)[None, :]
        masks[o] = (o * 128 + p <= j)
    gmask = np.zeros((4, 128, 128), f)
    pp_ = np.arange(128)[:, None]
    jj_ = np.arange(128)[None, :]
    gmask[0] = (pp_ <= jj_)
    gmask[1] = (pp_ == 127) * np.ones((1, 128), f)
    gmask[2] = 3e4 * (jj_ <= pp_)
    gmask[3] = 3e4 * (jj_ < pp_)
    lmask = np.zeros((14, 128, 128), f)
    for s_ in range(7):
        b_ = 2 ** s_
        mk = ((pp_ // (2 * b_)) == (jj_ // (2 * b_))) & ((pp_ % (2 * b_)) < b_) & ((jj_ % (2 * b_)) >= b_)
        lmask[2 * s_] = mk
        lmask[2 * s_ + 1] = mk.T
    inv = (10000.0 ** (-np.arange(0, ROPE, 2, dtype=f) / f(ROPE))).astype(f)

    def tables(pos):
        ang = (pos.astype(f)[:, None] * inv[None, :]).astype(f)
        C = np.ones((96, len(pos)), f)
        S = np.zeros((96, len(pos)), f)
        C[64:80] = np.cos(ang).T
        C[80:96] = np.cos(ang).T
        S[64:80] = np.sin(ang).T
        S[80:96] = np.sin(ang).T
        return C, S
    ropeC, ropeS = tables(np.arange(T))
    ropeCs, ropeSs = tables(np.array([PAST]))
    R = np.zeros((96, 96), f)
    for i in range(16):
        R[64 + i, 80 + i] = -1.0
        R[80 + i, 64 + i] = 1.0
    shared = dict(
        cache_all=np.concatenate([np.asarray(inp["cache_ckv"], f), np.asarray(inp["cache_krope"], f)], axis=-1),
        win=win, wada=wada, vecs=vecs, rgw=rgw, wrgp=projl(inp["w_rg_proj"], 10), wgdp=projl(inp["w_gdn_proj"], 8), wmlp=projl(inp["w_mla_proj"], 8),
        wo=np.asarray(inp["w_o"], f), wffi=wffi, wffo=np.asarray(inp["w_ffn_out"], f), wuq=np.asarray(inp["w_uq"], f), wukv=wukv, wukT=wukT,
        gdn_ab=gdn_ab, gmask=gmask, lmask=lmask, iota=np.arange(128, dtype=f).reshape(128, 1), ident=ident, masks=masks, ropeC=ropeC, ropeS=ropeS, ropeCs=ropeCs, ropeSs=ropeSs, rfullT=np.ascontiguousarray(R.T),
    )
    return shared


def kernel(stage=99, ncores=NCORES, **inp):
    f = np.float32
    shared = _prep_shared(inp)
    in_maps = []
    for c in range(NCORES):
        s0 = c * NS
        m = dict(shared)
        m["xp"] = np.ascontiguousarray(np.asarray(inp["x_prompt"][c], f))
        m["xs"] = np.ascontiguousarray(np.asarray(inp["x_sample"][s0:s0 + NS, 0], f))
        m["st_rg_conv"] = np.ascontiguousarray(np.asarray(inp["state_rg_conv"][:, s0:s0 + NS], f))
        m["st_rg_h"] = np.ascontiguousarray(np.asarray(inp["state_rg_h"][:, s0:s0 + NS], f))
        m["st_gdn_conv"] = np.ascontiguousarray(np.asarray(inp["state_gdn_conv"][:, s0:s0 + NS], f))
        m["st_gdn_S"] = np.ascontiguousarray(np.asarray(inp["state_gdn_S"][:, s0:s0 + NS], f))
        m["ptab"] = np.ascontiguousarray(np.asarray(inp["page_table"][s0:s0 + NS], np.int32))
        m["cc"] = np.ascontiguousarray(np.concatenate([np.asarray(inp["c_prompt"][c:c + 1], f), np.asarray(inp["c_sample"][s0:s0 + NS], f)], 0))
        in_maps.append(m)
    if stage < 50:
        for m in in_maps:
            m["cache_all"] = np.zeros((L, 1, PAGE, KVL + ROPE), f)
    _, waited = build_program(stage)
    nc, _ = build_program(stage, needed=waited)
    if os.environ.get("KTRACE"):
        res = run_bass_kernel_spmd(nc, in_maps[:ncores], core_ids=list(range(ncores)), trace=True)
        print("EXEC_TIME_NS", res.exec_time_ns)
    else:
        res = run_bass_kernel_spmd(nc, in_maps[:ncores], core_ids=list(range(ncores)))
    R = list(res.results)
    while len(R) < NCORES:
        R.append({k: np.zeros_like(v) for k, v in R[0].items()})

    def cat(name, axis):
        return np.concatenate([np.expand_dims(r[name], axis) if False else r[name] for r in R], axis=axis)
    y_p = np.stack([r["y_p"] for r in R], 0)
    y_s = np.concatenate([r["y_s"] for r in R], 0)[:, None, :]
    ckv_p = np.stack([r["ckv_p"] for r in R], 1)
    krope_p = np.stack([r["krope_p"] for r in R], 1)
    rg_conv_p = np.stack([r["rg_conv_p"] for r in R], 1)
    rg_h_p = np.stack([r["rg_h_p"] for r in R], 1)
    gdn_conv_p = np.stack([r["gdn_conv_p"] for r in R], 1)
    gdn_S_p = np.stack([r["gdn_S_p"] for r in R], 1)
    ckv_s = np.concatenate([r["ckv_s"] for r in R], 1)[:, :, None, :]
    krope_s = np.concatenate([r["krope_s"] for r in R], 1)[:, :, None, :]
    rg_conv_s = np.concatenate([r["rg_conv_s"] for r in R], 1)
    rg_h_s = np.concatenate([r["rg_h_s"] for r in R], 1)
    gdn_conv_s = np.concatenate([r["gdn_conv_s"] for r in R], 1)
    gdn_S_s = np.concatenate([r["gdn_S_s"] for r in R], 1)
    outs = (y_p, y_s, ckv_p, krope_p, rg_conv_p, rg_h_p, gdn_conv_p, gdn_S_p, ckv_s, krope_s, rg_conv_s, rg_h_s, gdn_conv_s, gdn_S_s)
    return tuple(np.ascontiguousarray(o.astype(np.float32)) for o in outs)
```
